# Optimizing a Trainium2 kernel written in Bass

```python
import jax, jax.numpy as jnp
from jax import lax
import numpy as np

D_MODEL = 1024
BATCH = 32
SEQ = 256
DEPTH = 2
DEC_BATCH = 8
DEC_SEQ = 2048
PAST_LEN = 256

GRID_W = 64
N_MOD = 9
FFN_DIM = 2816
EPS = 1e-6
GLA_HEADS = 4
GLA_DK = 64
GLA_DV = 128
GLA_GATE_RANK = 16
GLA_TAU = 16.0
GLA_CHUNK = 64
FN_GROUPS = 4
FN_CH = 128
SWA_HEADS = 16
SWA_KV_HEADS = 4
SWA_GROUP = SWA_HEADS // SWA_KV_HEADS
SWA_HD = 64
WINDOW = 128
ATTN_BLOCK = 128
ROPE_BASE = 10000.0

GLA_QK = GLA_HEADS * GLA_DK
GLA_V = GLA_HEADS * GLA_DV
FN_W = FN_GROUPS * FN_CH
L0_SPLITS = (GLA_QK, 2 * GLA_QK, 2 * GLA_QK + GLA_V, 2 * GLA_QK + 2 * GLA_V,
             2 * GLA_QK + 2 * GLA_V + GLA_GATE_RANK, 2 * GLA_QK + 2 * GLA_V + 2 * GLA_GATE_RANK)
L0_IN_DIM = L0_SPLITS[-1] + FN_W
L0_MIX = GLA_V + FN_W
SWA_Q = SWA_HEADS * SWA_HD
SWA_KV = SWA_KV_HEADS * SWA_HD
L1_IN_DIM = SWA_Q + 2 * SWA_KV

kernel_name = "hybrid_diffusion_gla_fnet_swa_step"


def _rms(x, g):
    xf = x.astype(jnp.float32)
    y = xf * lax.rsqrt(jnp.mean(xf * xf, axis=-1, keepdims=True) + EPS)
    return (y * g.astype(jnp.float32)).astype(x.dtype)


def _modulation(cond, w, b):
    m = jax.nn.silu(cond) @ w + b
    return jnp.split(m[:, None, :], N_MOD, axis=-1)


def _ada_norm(x, g, shift, scale):
    return _rms(x, g) * (1 + scale) + shift


def _half_ffn(x, shift, scale, gate, g, w1, w3, w2):
    h = _ada_norm(x, g, shift, scale)
    f = (jax.nn.silu(h @ w1) * (h @ w3)) @ w2
    return x + 0.5 * gate * f


def _gla_chunked(q, k, v, log_a, s0):
    B, T, H, DK = q.shape
    DV = v.shape[-1]
    N = T // GLA_CHUNK
    f32 = jnp.float32
    q = q.astype(f32).reshape(B, N, GLA_CHUNK, H, DK)
    k = k.astype(f32).reshape(B, N, GLA_CHUNK, H, DK)
    v = v.astype(f32).reshape(B, N, GLA_CHUNK, H, DV)
    g = log_a.astype(f32).reshape(B, N, GLA_CHUNK, H, DK)
    b = jnp.cumsum(g, axis=2)
    b_last = b[:, :, -1:]
    q_dec = q * jnp.exp(b)
    k_inv = k * jnp.exp(-b)
    k_end = k * jnp.exp(b_last - b)
    mask = jnp.tril(jnp.ones((GLA_CHUNK, GLA_CHUNK), f32))
    attn = jnp.einsum('bnthk,bnshk->bnhts', q_dec, k_inv) * mask
    o_intra = jnp.einsum('bnhts,bnshv->bnthv', attn, v)
    delta = jnp.einsum('bnshk,bnshv->bnhkv', k_end, v)
    decay = jnp.exp(b_last[:, :, 0])

    def step(S, inp):
        d, dl = inp
        return d[..., None] * S + dl, S

    s_fin, s_in = lax.scan(step, s0.astype(f32), (jnp.moveaxis(decay, 1, 0), jnp.moveaxis(delta, 1, 0)))
    s_in = jnp.moveaxis(s_in, 0, 1)
    o_inter = jnp.einsum('bnthk,bnhkv->bnthv', q_dec, s_in)
    return (o_intra + o_inter).reshape(B, T, H, DV), s_fin


def _gla_fnet_mixer(h, s0_f, s0_b, w_in, w_gf, b_gf, w_gb, b_gb, g_head, w_out):
    B, T, _ = h.shape
    p = h @ w_in
    q, k, v, og, lf, lb, u = jnp.split(p, L0_SPLITS, axis=-1)
    q = q.reshape(B, T, GLA_HEADS, GLA_DK) * (GLA_DK ** -0.5)
    k = k.reshape(B, T, GLA_HEADS, GLA_DK)
    v = v.reshape(B, T, GLA_HEADS, GLA_DV)
    la_f = (jax.nn.log_sigmoid((lf @ w_gf + b_gf).astype(jnp.float32)) / GLA_TAU).reshape(B, T, GLA_HEADS, GLA_DK)
    la_b = (jax.nn.log_sigmoid((lb @ w_gb + b_gb).astype(jnp.float32)) / GLA_TAU).reshape(B, T, GLA_HEADS, GLA_DK)
    rev = lambda z: z[:, ::-1]
    o_f, s_f = _gla_chunked(q, k, v, la_f, s0_f)
    o_b, s_b = _gla_chunked(rev(q), rev(k), rev(v), rev(la_b), s0_b)
    o = _rms(o_f + rev(o_b), g_head)
    gla_out = (o * jax.nn.silu(og.reshape(B, T, GLA_HEADS, GLA_DV).astype(jnp.float32)))
    gla_out = gla_out.reshape(B, T, GLA_V).astype(h.dtype)
    uf = u.reshape(B, T, FN_GROUPS, FN_CH).astype(jnp.float32)
    fn_out = jnp.fft.fft2(uf, axes=(1, 3), norm='ortho').real.reshape(B, T, FN_W).astype(h.dtype)
    return jnp.concatenate([gla_out, fn_out], axis=-1) @ w_out, s_f, s_b


def _sink_attend(q, k, v, valid, sink):
    s = jnp.einsum('bqhgd,bkhd->bhgqk', q, k).astype(jnp.float32)
    if valid is not None:
        s = jnp.where(valid, s, -jnp.inf)
    sk = jnp.broadcast_to(sink.astype(jnp.float32)[None, :, :, None, None], s.shape[:-1] + (1,))
    p = jax.nn.softmax(jnp.concatenate([s, sk], axis=-1), axis=-1)[..., :-1]
    return jnp.einsum('bhgqk,bkhd->bqhgd', p.astype(v.dtype), v)


def _swa_project(h, w_in):
    B, T, _ = h.shape
    q, k, v = jnp.split(h @ w_in, (SWA_Q, SWA_Q + SWA_KV), axis=-1)
    q = q.reshape(B, T, SWA_KV_HEADS, SWA_GROUP, SWA_HD) * (SWA_HD ** -0.5)
    k = k.reshape(B, T, SWA_KV_HEADS, SWA_HD)
    v = v.reshape(B, T, SWA_KV_HEADS, SWA_HD)
    return q, k, v


def _axial_rope_tables(T):
    rows = T // GRID_W
    row = jnp.repeat(jnp.arange(rows), GRID_W).astype(jnp.float32)
    col = (jnp.arange(rows * GRID_W) % GRID_W).astype(jnp.float32)
    n_freq = SWA_HD // 4
    inv = ROPE_BASE ** (-jnp.arange(n_freq, dtype=jnp.float32) / n_freq)
    ar = row[:, None] * inv
    ac = col[:, None] * inv
    ang = jnp.concatenate([ar, ar, ac, ac], axis=-1)
    return jnp.cos(ang), jnp.sin(ang)


def _rope(x, cos, sin):
    T = x.shape[1]
    shp = (1, T) + (1,) * (x.ndim - 3) + (SWA_HD,)
    a, b, c2, d = jnp.split(x, 4, axis=-1)
    rot = jnp.concatenate([-b, a, -d, c2], axis=-1)
    return (x * cos.reshape(shp) + rot * sin.reshape(shp)).astype(x.dtype)


def _swa_context(h, w_in, sink, w_out):
    B, T, _ = h.shape
    q, k, v = _swa_project(h, w_in)
    nb = T // ATTN_BLOCK
    qb = jnp.moveaxis(q.reshape(B, nb, ATTN_BLOCK, SWA_KV_HEADS, SWA_GROUP, SWA_HD), 1, 0)
    o = lax.map(lambda qi: _sink_attend(qi, k, v, None, sink), qb)
    o = jnp.moveaxis(o, 0, 1).reshape(B, T, SWA_Q)
    return o @ w_out, k, v


def _swa_latent(h, ctx_k, ctx_v, w_in, sink, w_out):
    B, T, _ = h.shape
    q, k, v = _swa_project(h, w_in)
    cos, sin = _axial_rope_tables(T)
    q = _rope(q, cos, sin)
    k = _rope(k, cos, sin)
    Lc = ctx_k.shape[1]
    nb = T // ATTN_BLOCK
    pad = ((0, 0), (ATTN_BLOCK, ATTN_BLOCK), (0, 0), (0, 0))
    kp = jnp.pad(k, pad)
    vp = jnp.pad(v, pad)
    ctx_valid = jnp.ones((ATTN_BLOCK, Lc), bool)

    def block(i):
        qi = lax.dynamic_slice_in_dim(q, i * ATTN_BLOCK, ATTN_BLOCK, axis=1)
        ki = lax.dynamic_slice_in_dim(kp, i * ATTN_BLOCK, 3 * ATTN_BLOCK, axis=1)
        vi = lax.dynamic_slice_in_dim(vp, i * ATTN_BLOCK, 3 * ATTN_BLOCK, axis=1)
        qpos = i * ATTN_BLOCK + jnp.arange(ATTN_BLOCK)
        kpos = (i - 1) * ATTN_BLOCK + jnp.arange(3 * ATTN_BLOCK)
        band = (jnp.abs(qpos[:, None] - kpos[None, :]) <= WINDOW) & (kpos >= 0)[None, :] & (kpos < T)[None, :]
        valid = jnp.concatenate([ctx_valid, band], axis=1)
        return _sink_attend(qi, jnp.concatenate([ctx_k.astype(ki.dtype), ki], axis=1),
                            jnp.concatenate([ctx_v.astype(vi.dtype), vi], axis=1), valid, sink)

    o = lax.map(block, jnp.arange(nb))
    o = jnp.moveaxis(o, 0, 1).reshape(B, T, SWA_Q)
    return o @ w_out


def setup_inputs(seed: int = 0) -> dict:
    key = jax.random.key(seed)
    ks = jax.random.split(key, 32)
    D = D_MODEL
    n = lambda k, shape, s: jax.random.normal(k, shape, jnp.float32) * s
    return {
        "x_prompt": n(ks[0], (BATCH, SEQ, D), 1.0),
        "x_sample": n(ks[1], (DEC_BATCH, DEC_SEQ, D), 1.0),
        "state_l0_gla_fwd": n(ks[2], (DEC_BATCH, GLA_HEADS, GLA_DK, GLA_DV), 0.3),
        "state_l0_gla_bwd": n(ks[3], (DEC_BATCH, GLA_HEADS, GLA_DK, GLA_DV), 0.3),
        "cache_l1_k": n(ks[4], (DEC_BATCH, PAST_LEN, SWA_KV_HEADS, SWA_HD), 1.0),
        "cache_l1_v": n(ks[5], (DEC_BATCH, PAST_LEN, SWA_KV_HEADS, SWA_HD), 1.0),
        "c": n(ks[6], (DEC_BATCH, D), 1.0),
        "c_ctx": n(ks[7], (D,), 1.0),
        "mod_w": n(ks[8], (DEPTH, D, N_MOD * D), 0.5 * D ** -0.5),
        "mod_b": n(ks[9], (DEPTH, N_MOD * D), 0.02),
        "norm_g": 1.0 + n(ks[10], (DEPTH, 3, D), 0.05),
        "ffn_w1": n(ks[11], (DEPTH, 2, D, FFN_DIM), D ** -0.5),
        "ffn_w3": n(ks[12], (DEPTH, 2, D, FFN_DIM), D ** -0.5),
        "ffn_w2": n(ks[13], (DEPTH, 2, FFN_DIM, D), FFN_DIM ** -0.5),
        "l0_w_in": n(ks[14], (D, L0_IN_DIM), D ** -0.5),
        "l0_w_gf": n(ks[15], (GLA_GATE_RANK, GLA_QK), GLA_GATE_RANK ** -0.5),
        "l0_b_gf": n(ks[16], (GLA_QK,), 0.1),
        "l0_w_gb": n(ks[17], (GLA_GATE_RANK, GLA_QK), GLA_GATE_RANK ** -0.5),
        "l0_b_gb": n(ks[18], (GLA_QK,), 0.1),
        "l0_g_head": 1.0 + n(ks[19], (GLA_DV,), 0.05),
        "l0_w_out": n(ks[20], (L0_MIX, D), L0_MIX ** -0.5),
        "l1_w_in": n(ks[21], (D, L1_IN_DIM), D ** -0.5),
        "l1_sink": n(ks[22], (SWA_KV_HEADS, SWA_GROUP), 0.5),
        "l1_w_out": n(ks[23], (SWA_Q, D), SWA_Q ** -0.5),
        "final_g": 1.0 + n(ks[24], (D,), 0.05),
    }


def reference(x_prompt, x_sample, state_l0_gla_fwd, state_l0_gla_bwd, cache_l1_k, cache_l1_v, c, c_ctx,
              mod_w, mod_b, norm_g, ffn_w1, ffn_w3, ffn_w2,
              l0_w_in, l0_w_gf, l0_b_gf, l0_w_gb, l0_b_gb, l0_g_head, l0_w_out,
              l1_w_in, l1_sink, l1_w_out, final_g):
    xp, xs = x_prompt, x_sample
    new_state = []
    for layer in range(DEPTH):
        mp = _modulation(c_ctx[None, :], mod_w[layer], mod_b[layer])
        ms = _modulation(c, mod_w[layer], mod_b[layer])
        ffn_a = (norm_g[layer, 0], ffn_w1[layer, 0], ffn_w3[layer, 0], ffn_w2[layer, 0])
        ffn_b = (norm_g[layer, 2], ffn_w1[layer, 1], ffn_w3[layer, 1], ffn_w2[layer, 1])
        xp = _half_ffn(xp, mp[0], mp[1], mp[2], *ffn_a)
        xs = _half_ffn(xs, ms[0], ms[1], ms[2], *ffn_a)
        hp = _ada_norm(xp, norm_g[layer, 1], mp[3], mp[4])
        hs = _ada_norm(xs, norm_g[layer, 1], ms[3], ms[4])
        if layer % 2 == 0:
            w = (l0_w_in, l0_w_gf, l0_b_gf, l0_w_gb, l0_b_gb, l0_g_head, l0_w_out)
            zero = jnp.zeros((xp.shape[0], GLA_HEADS, GLA_DK, GLA_DV), jnp.float32)
            op, s_f, s_b = _gla_fnet_mixer(hp, zero, zero, *w)
            os_, _, _ = _gla_fnet_mixer(hs, state_l0_gla_fwd, state_l0_gla_bwd, *w)
            new_state += [s_f, s_b]
        else:
            op, k_c, v_c = _swa_context(hp, l1_w_in, l1_sink, l1_w_out)
            os_ = _swa_latent(hs, cache_l1_k, cache_l1_v, l1_w_in, l1_sink, l1_w_out)
            new_state += [k_c, v_c]
        xp = xp + mp[5] * op
        xs = xs + ms[5] * os_
        xp = _half_ffn(xp, mp[6], mp[7], mp[8], *ffn_b)
        xs = _half_ffn(xs, ms[6], ms[7], ms[8], *ffn_b)
    y_prompt = _rms(xp, final_g)
    y_sample = _rms(xs, final_g)
    return (y_prompt, y_sample, *new_state)
```

```python
import numpy as np
import ml_dtypes
from contextlib import ExitStack

import concourse.bass as bass
import concourse.mybir as mybir
from concourse.bass_utils import run_bass_kernel_spmd

F32 = mybir.dt.float32
BF16 = mybir.dt.bfloat16
AF = mybir.ActivationFunctionType
ALU = mybir.AluOpType

D = 1024
NCH = 8
TT = 512
NT = 6
FFN = 2816
NFG = 11
EPS = 1e-6
NTOK = 3072
L0_IN = 2080
L1_IN = 1536

_ESZ = {F32: 4, BF16: 2}
PAGE = 256
POOL_INFLIGHT = 3


class Op:
    __slots__ = ("eng", "fn", "deps", "idx", "sig", "sig_idx", "is_dma", "dkey", "dcount")


class Sched:
    ENGS = ("pe", "act", "dve", "pool", "sp")

    def __init__(self, tracked):
        self.q = {e: [] for e in self.ENGS}
        self.lastw = {}
        self.rd = {}
        self.dma_count = {}
        self.tracked = tracked
        self._pcache = {}
        self.final_dma = []
        self.pool_dmas = []

    def _pages(self, ap):
        t = ap.tensor
        name = t.name
        if name not in self.tracked:
            return ()
        dims = tuple((int(a), int(b)) for a, b in ap.ap)
        key = (name, int(ap.offset), dims, str(ap.dtype))
        r = self._pcache.get(key)
        if r is not None:
            return r
        esz = _ESZ[ap.dtype]
        row = int(t.shape[1])
        off = int(ap.offset)
        p0 = off // row
        f0 = off % row
        pcnt = dims[0][1]
        halves = sorted(set([(p0) // 64, (p0 + pcnt - 1) // 64]))
        starts = np.zeros(1, dtype=np.int64) + f0
        fd = dims[1:]
        if len(fd) == 0:
            run = 1
        else:
            for (st, cn) in fd[:-1]:
                if st == 0 or cn == 1:
                    continue
                starts = (starts[:, None] + (np.arange(cn, dtype=np.int64) * st)[None, :]).reshape(-1)
            st, cn = fd[-1]
            run = (cn - 1) * abs(st) + 1
        pages = set()
        for s in starts.tolist():
            b0 = (s * esz) // PAGE
            b1 = ((s + run) * esz - 1) // PAGE
            for pg in range(b0, b1 + 1):
                pages.add(pg)
        r = tuple((name, h, pg) for h in halves for pg in sorted(pages))
        self._pcache[key] = r
        return r

    def add(self, eng, fn, outs=(), ins=(), dma_key=None, final=False):
        op = Op()
        op.eng = eng
        op.fn = fn
        op.sig = False
        op.sig_idx = 0
        op.is_dma = dma_key is not None
        op.dkey = dma_key
        op.dcount = 0
        if op.is_dma:
            c = self.dma_count.get(dma_key, 0) + 1
            self.dma_count[dma_key] = c
            op.dcount = c
            if final:
                self.final_dma.append(op)
        op.idx = len(self.q[eng])
        deps = {}

        def dep(d):
            if d is op:
                return
            if d.is_dma:
                k = ("dma", d.dkey)
                if k not in deps or deps[k].dcount < d.dcount:
                    deps[k] = d
            else:
                if eng == "pe" and d.eng == "pe" and not op.is_dma:
                    return
                k = ("eng", d.eng)
                if k not in deps or deps[k].idx < d.idx:
                    deps[k] = d

        in_pages = []
        out_pages = []
        for ap in ins:
            pgs = self._pages(ap)
            if pgs and pgs[0][0].startswith("ps"):
                out_pages.extend(sorted(set((n, h, 0) for (n, h, _) in pgs)))
            else:
                in_pages.extend(pgs)
        for ap in outs:
            pgs = self._pages(ap)
            if pgs and pgs[0][0].startswith("ps"):
                out_pages.extend(sorted(set((n, h, 0) for (n, h, _) in pgs)))
            else:
                out_pages.extend(pgs)
        lastw = self.lastw
        rd = self.rd
        for pg in in_pages:
            w = lastw.get(pg)
            if w is not None:
                dep(w)
        for pg in out_pages:
            w = lastw.get(pg)
            if w is not None:
                dep(w)
            rs = rd.get(pg)
            if rs:
                for r in rs.values():
                    dep(r)
        for pg in out_pages:
            lastw[pg] = op
            rd[pg] = {}
        key_r = ("dma", op.dkey, op.dcount) if op.is_dma else op.eng
        for pg in in_pages:
            rs = rd.get(pg)
            if rs is None:
                rs = {}
                rd[pg] = rs
            rs[key_r if not op.is_dma else key_r] = op
        if op.is_dma and eng == "pool":
            pl = self.pool_dmas
            if len(pl) >= POOL_INFLIGHT:
                dep(pl[-POOL_INFLIGHT])
            pl.append(op)
        op.deps = list(deps.values())
        for d in op.deps:
            if not d.is_dma:
                d.sig = True
        self.q[eng].append(op)
        return op

    def emit(self, nc):
        for e in self.ENGS:
            n = 0
            for op in self.q[e]:
                if op.sig and not op.is_dma:
                    n += 1
                    op.sig_idx = n
        with ExitStack() as st:
            esem = {e: st.enter_context(nc.semaphore("sem_" + e)) for e in ("pe", "act", "dve", "pool")}
            dsem = {k: st.enter_context(nc.semaphore("dsem_%d" % i)) for i, k in enumerate(self.dma_count)}
            block = st.enter_context(nc.Block())

            def run(eh, e):
                waited = {}
                for op in self.q[e]:
                    for d in op.deps:
                        if d.is_dma:
                            s, v = dsem[d.dkey], 16 * d.dcount
                        else:
                            s, v = esem[d.eng], d.sig_idx
                        k = id(s)
                        if waited.get(k, 0) < v:
                            eh.wait_ge(s, v)
                            waited[k] = v
                    ins = op.fn(eh)
                    if op.is_dma:
                        ins.then_inc(dsem[op.dkey], 16)
                    elif op.sig:
                        ins.then_inc(esem[e], 1)
                if e == "sp":
                    for k, c in self.dma_count.items():
                        eh.wait_ge(dsem[k], 16 * c)

            @block.tensor
            def _(eh):
                run(eh, "pe")

            @block.scalar
            def _(eh):
                run(eh, "act")

            @block.vector
            def _(eh):
                run(eh, "dve")

            @block.gpsimd
            def _(eh):
                run(eh, "pool")

            @block.sync
            def _(eh):
                run(eh, "sp")


def _bf(a):
    return np.asarray(a, dtype=np.float32).astype(ml_dtypes.bfloat16)


CB_ID, CB_ONES_D, CB_ONES_DV, CB_ONE1, CB_MF, CB_MB, CB_ROT, CB_MNLO, CB_MNHI = (
    0, 128, 256, 384, 512, 640, 768, 896, 1408)
CB_N = 1920


def _consts():
    i = np.arange(128)
    cb = np.zeros((128, CB_N), np.float32)
    cb[:, CB_ID:CB_ID + 128] = np.eye(128)
    cb[:, CB_ONES_D:CB_ONES_D + 128] = 1.0 / 1024.0
    cb[:, CB_ONES_DV:CB_ONES_DV + 128] = 1.0 / 128.0
    cb[:, CB_ONE1:CB_ONE1 + 128] = 1.0
    s = i[:, None]
    t = i[None, :]
    same = (s // 64) == (t // 64)
    cb[:, CB_MF:CB_MF + 128] = (same & (s <= t)).astype(np.float32)
    cb[:, CB_MB:CB_MB + 128] = (same & (s >= t)).astype(np.float32)
    R = np.zeros((64, 64), np.float32)
    for m in range(16):
        R[m, 16 + m] = -1.0
        R[16 + m, m] = 1.0
        R[32 + m, 48 + m] = -1.0
        R[48 + m, 32 + m] = 1.0
    RT = np.zeros((128, 128), np.float32)
    RT[:64, :64] = R.T
    RT[64:, 64:] = R.T
    cb[:, CB_ROT:CB_ROT + 128] = RT
    lo = np.where(s >= t, 0.0, -30000.0).astype(np.float32)
    hi = np.where(s <= t, 0.0, -30000.0).astype(np.float32)
    cb[:, CB_MNLO:CB_MNLO + 512] = np.tile(lo, (1, 4))
    cb[:, CB_MNHI:CB_MNHI + 512] = np.tile(hi, (1, 4))
    cf = np.zeros((128, 128 + 512), np.float32)
    cf[:, :128] = np.eye(128)
    m = np.ones(512, np.float32)
    m[::64] = 0.0
    cf[:, 128:] = m[None, :]
    tt = np.arange(2048)
    row = (tt // 64).astype(np.float32)
    col = (tt % 64).astype(np.float32)
    inv = (np.float32(10000.0) ** (-np.arange(16, dtype=np.float32) / np.float32(16))).astype(np.float32)
    ar = row[:, None] * inv[None, :]
    ac = col[:, None] * inv[None, :]
    ang = np.concatenate([ar, ar, ac, ac], axis=-1).astype(np.float32)
    cos = np.cos(ang).astype(np.float32).T
    sin = np.sin(ang).astype(np.float32).T
    rope = np.stack([np.concatenate([cos, cos], 0), np.concatenate([sin, sin], 0)], 0)
    def dft(n):
        k = np.arange(n)
        kk = (k[:, None] * k[None, :]) % n
        a = 2.0 * np.pi * kk.astype(np.float64) / n
        return np.cos(a) / np.sqrt(n), np.sin(a) / np.sqrt(n)
    c128, s128 = dft(128)
    dftc = np.stack([c128, -s128], 0)
    c2k, s2k = dft(2048)
    c256, s256 = dft(256)
    return dict(cb=_bf(cb), cf=cf, rope=np.ascontiguousarray(rope), dftc=_bf(dftc),
                dft2k=_bf(np.stack([c2k, s2k], 0)), dft256=_bf(np.stack([c256, s256], 0)))


class Builder:
    def __init__(self, cfg=None):
        self.cfg = cfg or {}
        self.nc = bass.Bass("TRN2", target_bir_lowering=False)
        self.dram = {}
        self.in_names = []
        self.out_names = []

    def din(self, name, shape, dt=F32):
        self.dram[name] = self.nc.dram_tensor(name, list(shape), dt, kind="ExternalInput").ap()
        self.in_names.append(name)
        return self.dram[name]

    def dout(self, name, shape, dt=F32):
        self.dram[name] = self.nc.dram_tensor(name, list(shape), dt, kind="ExternalOutput").ap()
        self.out_names.append(name)
        return self.dram[name]

    def V(self, off, dt, shape):
        esz = _ESZ[dt]
        n = int(np.prod(shape))
        assert off % 4 == 0 and (n * esz) % 4 == 0
        assert off + n * esz <= self.arena_bytes, (off, n * esz, self.arena_bytes)
        a = self.arena[:, off // 4:(off + n * esz) // 4]
        if dt != F32:
            a = a.bitcast(dt)
        if len(shape) > 1:
            names = ["d%d" % i for i in range(len(shape))]
            kw = {names[i]: int(shape[i]) for i in range(len(shape) - 1)}
            a = a.rearrange("p (%s) -> p %s" % (" ".join(names), " ".join(names)), **kw)
        return a

    def build(self):
        nc = self.nc
        cfg = self.cfg
        d = self.dram
        self.din("xin", [NTOK, D])
        self.din("small1", [77, 128])
        self.din("modb", [144, 128])
        self.din("st_f", [4, 64, 128])
        self.din("st_b", [4, 64, 128])
        self.din("ck", [256, 256])
        self.din("cv", [256, 256])
        self.din("sink", [4, 4])
        self.din("mod_w", [2, D, 9 * D])
        self.din("ffn_w1", [2, 2, D, FFN])
        self.din("ffn_w3", [2, 2, D, FFN])
        self.din("ffn_w2", [2, 2, FFN, D])
        self.din("l0_w_in", [D, L0_IN])
        self.din("l0_w_gf", [16, 256])
        self.din("l0_w_gb", [16, 256])
        self.din("l0_w_out", [D, D])
        self.din("l1_w_in", [D, L1_IN])
        self.din("l1_w_out", [D, D])
        self.din("cb", [128, CB_N], BF16)
        self.din("cf", [128, 640])
        self.din("rope", [2, 128, 2048])
        self.din("dftc", [2, 128, 128], BF16)
        self.din("dft2k", [2, 2048, 2048], BF16)
        self.din("dft256", [2, 256, 256], BF16)
        self.dout("y", [NTOK, D])
        self.dout("o_sf", [4, 4, 64, 128])
        self.dout("o_sb", [4, 4, 64, 128])
        self.dout("o_kc", [1024, 256])
        self.dout("o_vc", [1024, 256])

        self.arena_bytes = 212480
        with ExitStack() as st:
            self.arena = st.enter_context(nc.sbuf_tensor("arena", [128, self.arena_bytes // 4], F32))
            self.ps = [st.enter_context(nc.psum_tensor("ps%d" % i, [128, 512], F32))[:, :] for i in range(8)]
            tracked = {"arena"} | {"ps%d" % i for i in range(8)}
            self.S = Sched(tracked)
            self._layout()
            self._program()
            self.S.emit(nc)
        return nc

    def _layout(self):
        V = self.V
        o = 0
        self.XT = V(o, F32, [NT, NCH, TT]); o += NT * NCH * TT * 4
        self.HT_off = o
        self.HT = V(o, BF16, [NT, NCH, TT]); o += NT * NCH * TT * 2
        self.WA_off = o
        self.W1 = [V(o + s * 12288, BF16, [NCH, 256]) for s in range(2)]
        self.W3 = [V(o + s * 12288 + 4096, BF16, [NCH, 256]) for s in range(2)]
        self.W2 = [V(o + s * 12288 + 8192, BF16, [2, D]) for s in range(2)]
        o += 24576
        self.CONST_off = o
        c = o
        self.CB = V(c, BF16, [CB_N]); c += CB_N * 2
        self.CF = V(c, F32, [640]); c += 2560
        self.MODT = V(c, F32, [2, 72, 2]); c += 2 * 72 * 2 * 4
        self.SM1 = V(c, F32, [77]); c += 77 * 4 + 0
        self.AM = V(c, F32, [2, 3, 2, NCH]); c += 2 * 3 * 2 * NCH * 4
        self.GH = V(c, F32, [2, 3, 2, NCH]); c += 384
        self.SC = V(c, BF16, [2, NCH]); c += 32
        self.MBT = V(c, F32, [2, 72]); c += 576
        self.SINKE = V(c, F32, [2, 4]); c += 32
        self.NBG = V(c, F32, [2, 2]); c += 16
        assert c - o <= 9472, c - o
        o += 9472
        self.P1_off = o
        self.P1_size = self.arena_bytes - o
        p = o
        self.G = V(p, BF16, [2, 2, TT]); p += 4096
        self.SA = V(p, F32, [2, TT]); p += 4096
        self.SQ = V(p, BF16, [4, TT]); p += 4096
        self.RSTD = V(p, F32, [2, TT]); p += 4096
        self.TMP = V(p, F32, [2, TT]); p += 4096
        self.MODW = [V(p + s * 4096, BF16, [NCH, 256]) for s in range(2)]; p += 8192
        self.P1_ffn_end = p
        assert p <= self.arena_bytes, p
        self.cnt = {}
        self.prenormed = set()

    def rot(self, key, n):
        v = self.cnt.get(key, 0)
        self.cnt[key] = v + 1
        return v % n

    def mm(self, out, lhsT, rhs, start=True, stop=True):
        return self.S.add("pe", lambda e: e.matmul(out, lhsT, rhs, start=start, stop=stop),
                          outs=[out], ins=[lhsT, rhs])

    def tr(self, out, in_, ident):
        return self.S.add("pe", lambda e: e.transpose(out, in_, ident), outs=[out], ins=[in_, ident])

    def act(self, out, in_, func, bias=None, scale=None, eng="act"):
        ins = [in_]
        kw = {}
        if bias is not None:
            kw["bias"] = bias
            if not isinstance(bias, (int, float)):
                ins.append(bias)
        if scale is not None:
            kw["scale"] = scale
            if not isinstance(scale, (int, float)):
                ins.append(scale)
        return self.S.add("act", lambda e: e.activation(out, in_, func, **kw), outs=[out], ins=ins)

    def sigmoid(self, out, in_):
        self.act(out, in_, AF.Exp, scale=-1.0)
        self.act(out, out, AF.Ln, bias=1.0, scale=1.0)
        self.act(out, out, AF.Exp, scale=-1.0)

    def rsqrt(self, out, in_, eps=EPS):
        self.act(out, in_, AF.Ln, bias=eps, scale=1.0)
        self.act(out, out, AF.Exp, scale=-0.5)

    def ts(self, out, in0, s1, s2, op0, op1=None, eng="dve"):
        ins = [in0] + [s for s in (s1, s2) if s is not None and not isinstance(s, (int, float))]
        if op1 is None:
            return self.S.add(eng, lambda e: e.tensor_scalar(out, in0, s1, s2, op0), outs=[out], ins=ins)
        return self.S.add(eng, lambda e: e.tensor_scalar(out, in0, s1, s2, op0, op1), outs=[out], ins=ins)

    def stt(self, out, in0, scalar, in1, op0, op1, eng="dve"):
        ins = [in0, in1] + ([] if isinstance(scalar, (int, float)) else [scalar])
        return self.S.add(eng, lambda e: e.scalar_tensor_tensor(out, in0, scalar, in1, op0, op1),
                          outs=[out], ins=ins)

    def tt(self, out, in0, in1, op, eng="dve"):
        return self.S.add(eng, lambda e: e.tensor_tensor(out, in0, in1, op), outs=[out], ins=[in0, in1])

    def cp(self, out, in_, eng="dve"):
        if eng == "act":
            return self.S.add("act", lambda e: e.copy(out, in_), outs=[out], ins=[in_])
        return self.S.add(eng, lambda e: e.tensor_copy(out, in_), outs=[out], ins=[in_])

    def memset(self, out, val, eng="dve"):
        return self.S.add(eng, lambda e: e.memset(out, val), outs=[out], ins=[])

    def dma(self, out, in_, key, eng="sp", final=False):
        return self.S.add(eng, lambda e: e.dma_start(out=out, in_=in_), outs=[out], ins=[in_],
                          dma_key=key, final=final)

    def _program(self):
        cfg = self.cfg
        stop = cfg.get("stop", "")
        self.load_consts()
        self.mod_queue = [(l, pc) for l in range(2) for pc in range(36)]
        self.mod_lq = list(self.mod_queue)
        self.mod_loaded = []
        self.preloaded = False
        self.mod_load()
        self.mod_load()
        if stop == "consts":
            self.mod_queue = []
            return self.final_out()
        for _ in range(12):
            self.mod_piece()
        if stop == "mod":
            self.mod_queue = []
            return self.final_out()
        self.load_x()
        if stop == "loadx":
            self.mod_queue = []
            return self.final_out()
        for l in cfg.get("layer_list", list(range(cfg.get("layers", 2)))):
            self.mod_ensure(l, 0)
            self.ffn(l, 0)
            self.mod_ensure(l, 2)
            self.mod_ensure(l, 1)
            self.mod_drain()
            if cfg.get("mixers", True):
                if l == 0:
                    self.mixer_l0()
                else:
                    self.mixer_l1()
            self.ffn(l, 1)
        while self.mod_queue:
            self.mod_piece()
        self.final_out()

    def load_consts(self):
        d = self.dram
        CB, CF = self.CB, self.CF
        self.dma(CB, d["cb"], "c_cb")
        self.dma(CF, d["cf"], "c_cf")
        self.IDB = CB[:, CB_ID:CB_ID + 128]
        self.ONES_D = CB[:, CB_ONES_D:CB_ONES_D + 128]
        self.ONES_DV = CB[:, CB_ONES_DV:CB_ONES_DV + 128]
        self.ONE1 = CB[:, CB_ONE1:CB_ONE1 + 128]
        self.MF = CB[:, CB_MF:CB_MF + 128]
        self.MB = CB[:, CB_MB:CB_MB + 128]
        self.ROT = CB[:, CB_ROT:CB_ROT + 128]
        self.MNLO = CB[:, CB_MNLO:CB_MNLO + 512]
        self.MNHI = CB[:, CB_MNHI:CB_MNHI + 512]
        self.IDF = CF[:, 0:128]
        self.SCANM = CF[:, 128:640]
        P1 = self.P1_off
        s1 = self.V(P1, F32, [128])
        mb = self.V(P1 + 512, F32, [2, 128])
        self.dma(s1[0:77, :], d["small1"], "c_s1")
        self.dma(mb[0:72, :, :], d["modb"].rearrange("(l j) f -> j l f", l=2), "c_mb")
        ps = self.ps[7]
        self.tr(ps[:, 0:77], s1[0:77, :], self.IDF[0:77, 0:77])
        self.cp(self.SM1, ps[:, 0:77])
        for l in range(2):
            self.tr(ps[:, 128 + l * 72:128 + (l + 1) * 72], mb[0:72, l, :], self.IDF[0:72, 0:72])
        self.cp(self.MBT, ps[:, 128:272].rearrange("p (l j) -> p l j", l=2))
        self.NG = self.SM1[:, 16:64].rearrange("p (l s c) -> p l s c", l=2, s=3)
        self.FG = self.SM1[:, 64:72]
        self.GHEAD = self.SM1[:, 72:73]
        self.BGF = self.SM1[:, 73:75]
        self.BGB = self.SM1[:, 75:77]
        t0 = self.V(P1 + 2048, F32, [16])
        self.sigmoid(t0, self.SM1[:, 0:16])
        self.tt(self.SC.rearrange("p w c -> p (w c)"), self.SM1[:, 0:16], t0, ALU.mult)
        self.mod_ps_cols = 0

    def mod_load(self):
        if not self.mod_lq:
            return
        l, pc = self.mod_lq.pop(0)
        d = self.dram
        slot = self.rot("modw_l", 2)
        src = d["mod_w"][l].rearrange("(kc p) n -> p kc n", p=128)[:, :, pc * 256:(pc + 1) * 256]
        self.dma(self.MODW[slot], src, "modw%d" % slot, eng="pool")
        self.mod_loaded.append((l, pc, slot))

    def mod_drain(self):
        while self.mod_loaded:
            self.mod_piece(prefetch=False)

    def mod_piece(self, prefetch=True):
        if not self.mod_queue:
            return
        if not self.mod_loaded:
            self.mod_load()
        l, pc, slot = self.mod_loaded.pop(0)
        assert (l, pc) == self.mod_queue.pop(0)
        W = self.MODW[slot]
        ps = self.ps[7]
        c0 = 384 + slot * 4
        for jj in range(2):
            o = ps[:, c0 + jj * 2:c0 + jj * 2 + 2]
            for kc in range(NCH):
                self.mm(o, W[:, kc, jj * 128:(jj + 1) * 128], self.SC[:, :, kc],
                        start=(kc == 0), stop=(kc == NCH - 1))
        if prefetch:
            self.mod_load()
        j0 = pc * 2
        self.tt(self.MODT[:, l, j0:j0 + 2, :], ps[:, c0:c0 + 4].rearrange("p (j w) -> p j w", j=2),
                self.MBT[:, l, j0:j0 + 2].unsqueeze(2).broadcast_to([128, 2, 2]), ALU.add)
        if pc % 12 == 11:
            s = pc // 12
            for w in range(2):
                self.stt(self.AM[:, l, s, w, :], self.MODT[:, l, (3 * s + 1) * 8:(3 * s + 2) * 8, w], 1.0,
                         self.NG[:, l, s, :], ALU.add, ALU.mult)
                gsrc = self.MODT[:, l, (3 * s + 2) * 8:(3 * s + 3) * 8, w]
                if s == 1:
                    self.cp(self.GH[:, l, s, w, :], gsrc)
                else:
                    self.ts(self.GH[:, l, s, w, :], gsrc, 0.5, None, ALU.mult)

    def mod_ensure(self, l, sgrp):
        while self.mod_queue and self.mod_queue[0] <= (l, 12 * sgrp + 11):
            self.mod_piece()

    def shift(self, l, s, w, c):
        j = (3 * s) * 8 + c
        return self.MODT[:, l, j, w:w + 1]

    def load_x(self):
        d = self.dram
        P1 = self.P1_ffn_end
        stg = [self.V(self.HT_off + s * 4096, F32, [D]) for s in range(2)]
        for tb in range(NTOK // 128):
            tile, sub = tb // 4, tb % 4
            sl = stg[tb % 2]
            self.dma(sl, d["xin"][tb * 128:(tb + 1) * 128, :], "xin%d" % (tb % 2))
            for hb in range(2):
                ps = self.ps[(tb % 2) * 2 + hb]
                for cc in range(4):
                    c = hb * 4 + cc
                    self.tr(ps[:, cc * 128:(cc + 1) * 128], sl[:, c * 128:(c + 1) * 128], self.IDF)
                dst = self.XT[:, tile, hb * 4:(hb + 1) * 4, sub * 128:(sub + 1) * 128]
                src = ps[:, :].rearrange("p (c t) -> p c t", c=4)
                if hb == 0:
                    self.cp(dst, src, eng="act")
                else:
                    self.cp(dst, src, eng="dve")

    def who(self, tile):
        return 0 if tile < 4 else 1

    def adanorm(self, tile, l, s, dst=None):
        w = self.who(tile)
        ms = self.ps[7][:, 0:TT] if False else self.ps[7]
        msv = self.ps[7][:, 0:TT]
        for half in range(2):
            self.act(self.SQ, self.XT[:, tile, half * 4:(half + 1) * 4, :], AF.Square)
            for cc in range(4):
                c = half * 4 + cc
                self.mm(msv, self.ONES_D, self.SQ[:, cc, :], start=(c == 0), stop=(c == NCH - 1))
        r = self.RSTD[:, self.rot("rstd", 2), :]
        self.rsqrt(r, msv)
        for c in range(NCH):
            t = self.TMP[:, self.rot("tmp", 2), :]
            self.stt(t, self.XT[:, tile, c, :], self.AM[:, l, s, w, c:c + 1], r, ALU.mult, ALU.mult)
            out = self.HT[:, tile, c, :] if dst is None else dst[:, c, :]
            self.act(out, t, AF.Identity, bias=self.shift(l, s, w, c), scale=1.0)

    def ffn_load(self, l, hs, fg):
        d = self.dram
        slot = fg % 2
        w1 = d["ffn_w1"][l, hs].rearrange("(kc p) f -> p kc f", p=128)[:, :, fg * 256:(fg + 1) * 256]
        w3 = d["ffn_w3"][l, hs].rearrange("(kc p) f -> p kc f", p=128)[:, :, fg * 256:(fg + 1) * 256]
        w2 = d["ffn_w2"][l, hs][fg * 256:(fg + 1) * 256, :].rearrange("(fc p) n -> p fc n", p=128)
        sk = self.cfg.get("skip_load", 0)
        if sk in (0, 2):
            self.dma(self.W1[slot], w1, "w1_%d" % slot, eng="pool")
            self.dma(self.W3[slot], w3, "w3_%d" % slot, eng="pool")
        if sk in (0, 3):
            self.dma(self.W2[slot], w2, "w2_%d" % slot, eng="pool")

    def ffn(self, l, hs):
        s = 0 if hs == 0 else 2
        ntiles = self.cfg.get("ntiles", NT)
        while len(self.mod_loaded) < 2 and self.mod_lq:
            self.mod_load()
        if self.preloaded:
            self.preloaded = False
        else:
            self.ffn_load(l, hs, 0)
            self.ffn_load(l, hs, 1)
        for tile in range(ntiles):
            if tile in self.prenormed:
                continue
            self.adanorm(tile, l, s)
        self.prenormed = set()
        if self.cfg.get("fstop") == "adanorm":
            return
        pend = None
        step = 0

        def do_o(p):
            fg, tile, par = p
            slot = fg % 2
            w = self.who(tile)
            for dc in range(NCH):
                o = self.ps[4 + self.rot("obank", 4)]
                for fc in range(2):
                    self.mm(o, self.W2[slot][:, fc, dc * 128:(dc + 1) * 128], self.G[:, par, fc, :],
                            start=(fc == 0), stop=(fc == 1))
                x = self.XT[:, tile, dc, :]
                if dc in (0, 1, 2, 4, 6):
                    self.stt(x, o, self.GH[:, l, s, w, dc:dc + 1], x, ALU.mult, ALU.add)
                else:
                    t = self.TMP[:, self.rot("tmpo", 2), :]
                    self.act(t, o, AF.Identity, scale=self.GH[:, l, s, w, dc:dc + 1])
                    self.tt(x, x, t, ALU.add)

        nfg = self.cfg.get("nfg", NFG)
        for fg in range(nfg):
            slot = fg % 2
            for tile in range(ntiles):
                par = step % 2
                step += 1
                for fc in range(2):
                    a = self.ps[fc]
                    b = self.ps[2 + fc]
                    for kc in range(NCH):
                        self.mm(a, self.W1[slot][:, kc, fc * 128:(fc + 1) * 128], self.HT[:, tile, kc, :],
                                start=(kc == 0), stop=(kc == NCH - 1))
                    for kc in range(NCH):
                        self.mm(b, self.W3[slot][:, kc, fc * 128:(fc + 1) * 128], self.HT[:, tile, kc, :],
                                start=(kc == 0), stop=(kc == NCH - 1))
                    sa = self.SA[:, fc, :]
                    self.sigmoid(sa, a)
                    if fc == 0:
                        self.tt(sa, sa, a, ALU.mult)
                        self.tt(self.G[:, par, fc, :], sa, b, ALU.mult)
                if pend is not None:
                    do_o(pend)
                    if pend[1] == ntiles - 1 and pend[0] + 2 < nfg:
                        self.ffn_load(l, hs, pend[0] + 2)
                sa = self.SA[:, 1, :]
                self.tt(sa, sa, self.ps[1], ALU.mult)
                self.tt(self.G[:, par, 1, :], sa, self.ps[3], ALU.mult)
                pend = (fg, tile, par)
            for _ in range(2):
                self.mod_piece()
        if self.cfg.get("fstop") in ("ab", "sig", "g"):
            return
        do_o(pend)

    def final_out(self):
        d = self.dram
        ntiles = self.cfg.get("ntiles", NT)
        YT = self.V(self.HT_off, F32, [NCH, TT])
        stg = [self.V(self.HT_off + 16384 + s * 4096, F32, [D]) for s in range(2)]
        for tile in range(ntiles):
            msv = self.ps[7][:, 0:TT]
            for half in range(2):
                self.act(self.SQ, self.XT[:, tile, half * 4:(half + 1) * 4, :], AF.Square)
                for cc in range(4):
                    c = half * 4 + cc
                    self.mm(msv, self.ONES_D, self.SQ[:, cc, :], start=(c == 0), stop=(c == NCH - 1))
            r = self.RSTD[:, self.rot("rstd", 2), :]
            self.rsqrt(r, msv)
            for c in range(NCH):
                self.stt(YT[:, c, :], self.XT[:, tile, c, :], self.FG[:, c:c + 1], r, ALU.mult, ALU.mult)
            for sub in range(4):
                tb = tile * 4 + sub
                sl = stg[tb % 2]
                for hb in range(2):
                    ps = self.ps[(tb % 2) * 2 + hb]
                    for cc in range(4):
                        c = hb * 4 + cc
                        self.tr(ps[:, cc * 128:(cc + 1) * 128], YT[:, c, sub * 128:(sub + 1) * 128], self.IDF)
                    if hb == 0:
                        self.cp(sl[:, 0:512], ps[:, :], eng="act")
                    else:
                        self.cp(sl[:, 512:1024], ps[:, :], eng="dve")
                self.dma(d["y"][tb * 128:(tb + 1) * 128, :], sl, "yout%d" % (tb % 2), final=True)

    def scan(self, out, d0, d1):
        return self.S.add("dve", lambda e: e.tensor_tensor_scan(out, d0, d1, 0.0, ALU.mult, ALU.add),
                          outs=[out], ins=[d0, d1])

    def mixer_l0(self):
        ntiles = self.cfg.get("ntiles", NT)
        self.ts(self.NBG[:, 0, :], self.BGF, -1.0, None, ALU.mult)
        self.ts(self.NBG[:, 1, :], self.BGB, -1.0, None, ALU.mult)
        groups = []
        if ntiles >= 4:
            groups.append(("s", [0, 1, 2, 3], self.HT_off + 4 * 8192))
        for t in range(4, ntiles):
            groups.append(("p", [t], self.HT_off))
        for kind, tiles, scr in groups:
            for tile in tiles:
                self.adanorm(tile, 0, 1)
            if self.cfg.get("fnet", True):
                self.fnet(kind, tiles, scr)
            if self.cfg.get("gla", True):
                for p in range(2):
                    self.gla_pair(kind, tiles, scr, p)

    def fnet(self, kind, tiles, scr):
        d = self.dram
        WA, P1 = self.WA_off, self.P1_off
        WU = self.V(WA, BF16, [NCH, 512])
        WOF = self.V(WA + 8192, BF16, [4, D])
        DFTC = self.V(WA + 16384, BF16, [2, 128])
        self.dma(WU, d["l0_w_in"].rearrange("(kc q) n -> q kc n", q=128)[:, :, 1568:2080], "l0wu", eng="pool")
        self.dma(WOF, d["l0_w_out"][512:1024, :].rearrange("(g q) n -> q g n", q=128), "l0wof", eng="pool")
        self.dma(DFTC, d["dftc"].rearrange("cs c k -> c cs k"), "dftc")
        UT = self.V(scr, BF16, [16, 512])
        for ti, tile in enumerate(tiles):
            for blk in range(4):
                ps = self.ps[blk % 2]
                for kc in range(NCH):
                    self.mm(ps, self.HT[:, tile, kc, blk * 128:(blk + 1) * 128], WU[:, kc, :],
                            start=(kc == 0), stop=(kc == NCH - 1))
                self.cp(UT[:, ti * 4 + blk, :], ps, eng=("act" if blk % 2 else "dve"))
        DT = [self.V(P1 + i * 8192, BF16, [4, 2, 512]) for i in range(2)]
        ABT = self.V(P1 + 16384, BF16, [2, 4, 512])
        FN = self.V(P1 + 24576, BF16, [4, 512])
        DT256 = self.V(P1, BF16, [2, 2, 256])
        if kind == "p":
            for cs_ in range(2):
                self.dma(DT256[:, :, cs_, :], d["dft256"][cs_].rearrange("(tb t) tp -> t tb tp", t=128), "dt256_%d" % cs_)
        for j, tile in enumerate(tiles):
            if kind == "s":
                for qd in range(4):
                    slot = self.rot("dt", 2)
                    src = d["dft2k"][:, qd * 512:(qd + 1) * 512, j * 512:(j + 1) * 512]
                    for cs_ in range(2):
                        self.dma(DT[slot][:, :, cs_, :], src[cs_].rearrange("(tb t) tp -> t tb tp", t=128),
                                 "dt%d_%d" % (slot, cs_))
                    for tb4 in range(4):
                        tb = qd * 4 + tb4
                        for g in range(4):
                            lhsT = UT[:, tb, g * 128:(g + 1) * 128]
                            self.mm(self.ps[g], lhsT, DT[slot][:, tb4, 0, :], start=(tb == 0), stop=(tb == 15))
                            self.mm(self.ps[4 + g], lhsT, DT[slot][:, tb4, 1, :], start=(tb == 0), stop=(tb == 15))
            else:
                for sq in range(2):
                    for tb in range(2):
                        for g in range(4):
                            lhsT = UT[:, sq * 2 + tb, g * 128:(g + 1) * 128]
                            cs = slice(sq * 256, (sq + 1) * 256)
                            self.mm(self.ps[g][:, cs], lhsT, DT256[:, tb, 0, :], start=(tb == 0), stop=(tb == 1))
                            self.mm(self.ps[4 + g][:, cs], lhsT, DT256[:, tb, 1, :], start=(tb == 0), stop=(tb == 1))
            for g in range(4):
                self.cp(ABT[:, 0, g, :], self.ps[g], eng="act")
                self.cp(ABT[:, 1, g, :], self.ps[4 + g], eng="dve")
            for g in range(4):
                y = self.ps[g]
                self.mm(y, DFTC[:, 0, :], ABT[:, 0, g, :], start=True, stop=False)
                self.mm(y, DFTC[:, 1, :], ABT[:, 1, g, :], start=False, stop=True)
                self.cp(FN[:, g, :], y, eng=("act" if g % 2 else "dve"))
            w = self.who(tile)
            for dc in range(NCH):
                o = self.ps[4 + self.rot("opb0", 4)]
                for g in range(4):
                    self.mm(o, WOF[:, g, dc * 128:(dc + 1) * 128], FN[:, g, :], start=(g == 0), stop=(g == 3))
                x = self.XT[:, tile, dc, :]
                self.stt(x, o, self.GH[:, 0, 1, w, dc:dc + 1], x, ALU.mult, ALU.add)

    def gla_pair(self, kind, tiles, scr, p):
        d = self.dram
        WA, P1 = self.WA_off, self.P1_off
        V = self.V
        WQp = V(WA, BF16, [NCH, 128])
        WKp = V(WA + 2048, BF16, [NCH, 128])
        WVp = V(WA + 4096, BF16, [NCH, 256])
        WGp = V(WA + 8192, BF16, [NCH, 256])
        WLF = V(WA + 12288, BF16, [NCH, 16])
        WLB = V(WA + 12544, BF16, [NCH, 16])
        WGF = V(WA + 12800, BF16, [128])
        WGB = V(WA + 13056, BF16, [128])
        WOp = V(WA + 13312, BF16, [2, D])
        src = d["l0_w_in"].rearrange("(kc q) n -> q kc n", q=128)
        self.dma(WLF, src[:, :, 1536:1552], "g_wlf", eng="pool")
        self.dma(WKp, src[:, :, 256 + p * 128:256 + (p + 1) * 128], "g_wk", eng="pool")
        self.dma(WGF[0:16, :], d["l0_w_gf"][:, p * 128:(p + 1) * 128], "g_wgf", eng="pool")
        self.dma(WVp, src[:, :, 512 + p * 256:512 + (p + 1) * 256], "g_wv", eng="pool")
        self.dma(WQp, src[:, :, p * 128:(p + 1) * 128], "g_wq", eng="pool")
        self.dma(WLB, src[:, :, 1552:1568], "g_wlb", eng="pool")
        self.dma(WGB[0:16, :], d["l0_w_gb"][:, p * 128:(p + 1) * 128], "g_wgb", eng="pool")
        self.dma(WGp, src[:, :, 1024 + p * 256:1024 + (p + 1) * 256], "g_wg", eng="pool")
        self.dma(WOp, d["l0_w_out"][p * 256:(p + 1) * 256, :].rearrange("(h q) n -> q h n", q=128), "g_wo", eng="pool")
        o = P1
        SP = V(o, F32, [TT]); o += 2048
        PSP = V(o, F32, [TT]); o += 2048
        T3 = V(o, F32, [TT]); o += 2048
        E1 = V(o, F32, [TT]); o += 2048
        E2 = V(o, F32, [TT]); o += 2048
        QDF = V(o, BF16, [TT]); o += 1024
        KIF = V(o, BF16, [TT]); o += 1024
        QDB = V(o, BF16, [TT]); o += 1024
        KIB = V(o, BF16, [TT]); o += 1024
        KE = V(o, BF16, [TT]); o += 1024
        KETOK = V(o, BF16, [4, 128]); o += 1024
        VTOK = V(o, BF16, [4, 256]); o += 2048
        LFT = V(o, BF16, [TT]); o += 1024
        LBT = LFT
        ATF = V(o, BF16, [2, 128]); o += 512
        ATB = V(o, BF16, [2, 128]); o += 512
        SALL = V(o, F32, [8, 128]); o += 4096
        INIT = V(o, F32, [128]); o += 512
        ZERO = V(o, F32, [128]); o += 512
        SB16 = V(o, BF16, [8, 128]); o += 2048
        GL = V(o, BF16, [2, TT]); o += 2048
        SQ16 = V(o, BF16, [TT]); o += 1024
        assert o <= self.arena_bytes, o
        SF16 = V(scr, BF16, [33, 128])
        self.memset(ZERO, 0.0)
        ps = self.ps
        c3 = lambda a: a.rearrange("q (c t) -> q c t", c=8)
        nb_f = self.NBG[:, 0, p:p + 1]
        nb_b = self.NBG[:, 1, p:p + 1]
        ktb = ps[5].bitcast(BF16)
        ktb3 = ps[3].bitcast(BF16)

        def proj_small(dst, W, tile):
            lp = ps[2][0:16, :]
            for kc in range(NCH):
                self.mm(lp, W[:, kc, :], self.HT[:, tile, kc, :], start=(kc == 0), stop=(kc == NCH - 1))
            self.cp(dst[0:16, :], lp, eng="act")

        def proj(bank, W, tile):
            for kc in range(NCH):
                self.mm(bank, W[:, kc, :], self.HT[:, tile, kc, :], start=(kc == 0), stop=(kc == NCH - 1))

        def softplus_neg(glog, nb):
            self.act(SP, glog, AF.Exp, bias=nb, scale=-1.0)
            self.act(SP, SP, AF.Ln, bias=1.0, scale=1.0)

        def ke_tok(tb):
            for blk in range(4):
                self.tr(tb[:, blk * 128:(blk + 1) * 128], KE[:, blk * 128:(blk + 1) * 128], self.IDB)
            self.cp(KETOK, tb[:, 0:512].rearrange("q (b c) -> q b c", b=4))

        def v_tok(tile):
            for blk in range(4):
                vp = ps[6 + blk // 2][:, (blk % 2) * 256:(blk % 2 + 1) * 256]
                for kc in range(NCH):
                    self.mm(vp, self.HT[:, tile, kc, blk * 128:(blk + 1) * 128], WVp[:, kc, :],
                            start=(kc == 0), stop=(kc == NCH - 1))
            self.cp(VTOK[:, 0:2, :], ps[6].rearrange("q (b c) -> q b c", b=2), eng="dve")
            self.cp(VTOK[:, 2:4, :], ps[7].rearrange("q (b c) -> q b c", b=2), eng="act")

        def delta(bank, blk, cc):
            dl = bank[:, 0:128]
            for h2 in range(2):
                self.mm(dl[h2 * 64:(h2 + 1) * 64, :], KETOK[cc * 64:(cc + 1) * 64, blk, h2 * 64:(h2 + 1) * 64],
                        VTOK[cc * 64:(cc + 1) * 64, blk, h2 * 128:(h2 + 1) * 128], start=True, stop=True)
            return dl

        def st_out(name, S32, seq):
            self.dma(d[name][seq, 2 * p:2 * p + 2].rearrange("h k v -> (h k) v"), S32, "so_%s%d" % (name, seq % 2),
                     final=True)

        def deltas(banks):
            dls = []
            for c in range(8):
                dl = banks[c % 2][:, (c // 2) * 128:(c // 2 + 1) * 128]
                blk, cc = c // 2, c % 2
                for h2 in range(2):
                    self.mm(dl[h2 * 64:(h2 + 1) * 64, :], KETOK[cc * 64:(cc + 1) * 64, blk, h2 * 64:(h2 + 1) * 64],
                            VTOK[cc * 64:(cc + 1) * 64, blk, h2 * 128:(h2 + 1) * 128], start=True, stop=True)
                dls.append(dl)
            return dls

        for ti, tile in enumerate(tiles):
            proj_small(LFT, WLF, tile)
            proj(ps[1], WKp, tile)
            self.mm(ps[3], WGF[0:16, :], LFT[0:16, :])
            v_tok(tile)
            softplus_neg(ps[3], nb_f)
            self.scan(PSP, self.SCANM, SP)
            self.tt(c3(T3), c3(PSP), c3(PSP)[:, :, 63:64].broadcast_to([128, 8, 64]), ALU.subtract)
            self.act(E2, T3, AF.Exp, scale=1.0 / 16.0)
            self.tt(KE, ps[1], E2, ALU.mult)
            self.act(E1, PSP, AF.Exp, scale=-1.0 / 16.0)
            ke_tok(ktb)
            dls = deltas([ps[0], ps[4]])
            n0 = ti * 8
            if kind == "s" and ti == 0:
                self.dma(INIT, d["st_f"][2 * p:2 * p + 2].rearrange("h k v -> (h k) v"), "st_in")
                self.cp(SF16[:, 0, :], INIT, eng="act")
            for c in range(8):
                if kind == "s":
                    s_in = INIT if (ti == 0 and c == 0) else SALL[:, (c - 1) % 8, :]
                else:
                    s_in = ZERO if c % 4 == 0 else SALL[:, c - 1, :]
                self.stt(SALL[:, c, :], s_in, E1[:, c * 64 + 63:c * 64 + 64], dls[c], ALU.mult, ALU.add)
            if kind == "s":
                self.cp(SF16[:, n0 + 1:n0 + 9, :], SALL, eng="act")
            else:
                self.cp(SF16[:, 1:8, :], SALL[:, 0:7, :], eng="act")
                self.memset(SF16[:, 0, :], 0.0)
                self.memset(SF16[:, 4, :], 0.0)
                st_out("o_sf", SALL[:, 3, :], (tile - 4) * 2)
                st_out("o_sf", SALL[:, 7, :], (tile - 4) * 2 + 1)
        for ti in range(len(tiles) - 1, -1, -1):
            tile = tiles[ti]
            w = self.who(tile)
            proj(ps[0], WQp, tile)
            proj(ps[1], WKp, tile)
            proj_small(LFT, WLF, tile)
            self.mm(ps[3], WGF[0:16, :], LFT[0:16, :])
            proj_small(LBT, WLB, tile)
            v_tok(tile)
            softplus_neg(ps[3], nb_f)
            self.scan(PSP, self.SCANM, SP)
            self.act(E1, PSP, AF.Exp, scale=-1.0 / 16.0)
            self.act(E2, PSP, AF.Exp, scale=1.0 / 16.0)
            self.stt(QDF, ps[0], 0.125, E1, ALU.mult, ALU.mult)
            self.tt(KIF, ps[1], E2, ALU.mult)
            self.mm(ps[3], WGB[0:16, :], LBT[0:16, :])
            softplus_neg(ps[3], nb_b)
            self.scan(PSP, self.SCANM, SP)
            self.tt(T3, SP, PSP, ALU.subtract)
            self.tt(c3(SP), c3(T3), c3(PSP)[:, :, 63:64].broadcast_to([128, 8, 64]), ALU.add)
            self.act(E1, SP, AF.Exp, scale=-1.0 / 16.0)
            self.act(E2, SP, AF.Exp, scale=1.0 / 16.0)
            self.stt(QDB, ps[0], 0.125, E1, ALU.mult, ALU.mult)
            self.tt(KIB, ps[1], E2, ALU.mult)
            self.act(T3, T3, AF.Exp, scale=1.0 / 16.0)
            self.tt(KE, ps[1], T3, ALU.mult)
            ke_tok(ktb3)
            dls = deltas([ps[0], ps[1]])
            last_t = (ti == len(tiles) - 1)
            if kind == "s":
                if last_t:
                    self.dma(INIT, d["st_b"][2 * p:2 * p + 2].rearrange("h k v -> (h k) v"), "st_in")
                    s7 = INIT
                else:
                    s7 = SALL[:, 0, :]
                self.cp(SB16[:, 7, :], s7, eng="act")
            for c in range(7, -1, -1):
                if kind == "s":
                    s_in = s7 if c == 7 else SALL[:, c + 1, :]
                else:
                    s_in = ZERO if c % 4 == 3 else SALL[:, c + 1, :]
                self.stt(SALL[:, c, :], s_in, E1[:, c * 64:c * 64 + 1], dls[c], ALU.mult, ALU.add)
            self.cp(SB16[:, 0:7, :], SALL[:, 1:8, :], eng="act")
            if kind == "p":
                self.memset(SB16[:, 3, :], 0.0)
                self.memset(SB16[:, 7, :], 0.0)
                st_out("o_sb", SALL[:, 0, :], (tile - 4) * 2)
                st_out("o_sb", SALL[:, 4, :], (tile - 4) * 2 + 1)
            for blk in range(3, -1, -1):
                bs = slice(blk * 128, (blk + 1) * 128)
                for h2 in range(2):
                    hs = slice(h2 * 64, (h2 + 1) * 64)
                    at = ps[2 + self.rot("atb", 2)]
                    self.mm(at[:, 0:128], KIF[hs, bs], QDF[hs, bs])
                    self.mm(at[:, 128:256], KIB[hs, bs], QDB[hs, bs])
                    sl = self.rot("afs", 2)
                    self.tt(ATF[:, sl, :], at[:, 0:128], self.MF, ALU.mult)
                    self.tt(ATB[:, sl, :], at[:, 128:256], self.MB, ALU.mult)
                    ot = ps[4 + h2]
                    vl = VTOK[:, blk, h2 * 128:(h2 + 1) * 128]
                    self.mm(ot[:, bs], vl, ATF[:, sl, :], start=True, stop=False)
                    self.mm(ot[:, bs], vl, ATB[:, sl, :], start=False, stop=False)
                    for cc in range(2):
                        c = blk * 2 + cc
                        n = ti * 8 + c
                        cs = slice(c * 64, (c + 1) * 64)
                        self.mm(ot[:, cs], SF16[hs, n, :], QDF[hs, cs], start=False, stop=False)
                        self.mm(ot[:, cs], SB16[hs, c, :], QDB[hs, cs], start=False, stop=(cc == 1))
            for h2 in range(2):
                og = ps[6 + h2]
                for kc in range(NCH):
                    self.mm(og, WGp[:, kc, h2 * 128:(h2 + 1) * 128], self.HT[:, tile, kc, :],
                            start=(kc == 0), stop=(kc == NCH - 1))
                ot = ps[4 + h2]
                self.act(SQ16, ot, AF.Square)
                self.mm(ps[2], self.ONES_DV, SQ16)
                self.rsqrt(SP, ps[2])
                self.stt(T3, ot, self.GHEAD[:, 0:1], SP, ALU.mult, ALU.mult)
                self.sigmoid(E1, og)
                self.tt(E2, og, E1, ALU.mult)
                self.tt(GL[:, h2, :], T3, E2, ALU.mult)
            for dc in range(NCH):
                ob = ps[self.rot("opb1", 2)]
                for h2 in range(2):
                    self.mm(ob, WOp[:, h2, dc * 128:(dc + 1) * 128], GL[:, h2, :], start=(h2 == 0), stop=(h2 == 1))
                x = self.XT[:, tile, dc, :]
                self.stt(x, ob, self.GH[:, 0, 1, w, dc:dc + 1], x, ALU.mult, ALU.add)
            if p == 1 and kind == "p" and self.cfg.get("early_ada", True):
                self.adanorm(tile, 0, 2)
                self.prenormed.add(tile)

    def mixer_l1(self):
        d = self.dram
        l, s = 1, 1
        ntiles = self.cfg.get("ntiles", NT)
        for tile in range(ntiles):
            self.adanorm(tile, l, s)
        WA = self.WA_off
        WQ = self.V(WA, BF16, [NCH, 512])
        WK = self.V(WA + 8192, BF16, [NCH, 128])
        WV = self.V(WA + 10240, BF16, [NCH, 128])
        WO = self.V(WA + 12288, BF16, [4, D])
        P = self.P1_off
        KT = self.V(P, BF16, [2048]); P += 4096
        VT = self.V(P, BF16, [16, 128]); P += 4096
        QT = self.V(P, BF16, [4, TT]); P += 4096
        ROPE = self.V(P, F32, [2, TT]); P += 4096
        CKT = self.V(WA + 20480, BF16, [2, 256])
        CV = self.V(WA + 21504, BF16, [2, 256])
        PT = [self.V(P + i * 1024, BF16, [TT]) for i in range(4)]; P += 4096
        OB = self.V(P, BF16, [4, TT]); P += 4096
        DENR = self.V(P, F32, [TT]); P += 2048
        KVO = self.V(P, F32, [4, 2, 128]); P += 4096
        QB = self.V(WA + 22528, BF16, [TT])
        assert P <= self.arena_bytes, P
        for p in range(2):
            for e in range(2):
                self.dma(self.SINKE[e * 64:(e + 1) * 64, p, :],
                         d["sink"][2 * p + e:2 * p + e + 1, :].partition_broadcast(64), "sink%d%d" % (p, e))
        self.act(self.SINKE, self.SINKE, AF.Exp)
        has_sample = ntiles >= 4
        groups = []
        if has_sample:
            groups.append(("s", [0, 1, 2, 3]))
        for t in range(4, ntiles):
            groups.append(("p", [t]))
        for p in range(2):
            wsrc = d["l1_w_in"].rearrange("(kc q) n -> q kc n", q=128)
            self.dma(WK, wsrc[:, :, 1024 + p * 128:1024 + (p + 1) * 128], "l1wk", eng="pool")
            self.dma(WV, wsrc[:, :, 1280 + p * 128:1280 + (p + 1) * 128], "l1wv", eng="pool")
            if has_sample:
                self.dma(CV, d["cv"].rearrange("(b t) c -> t b c", t=128), "cvf", eng="pool")
            wq_src = wsrc[:, :, p * 512:(p + 1) * 512].rearrange("q k (e g dd) -> q k e g dd", e=2, g=4)
            wq_dst = WQ.rearrange("q k (g e dd) -> q k g e dd", g=4, e=2)
            for g in range(4):
                for e in range(2):
                    self.dma(wq_dst[:, :, g, e, :], wq_src[:, :, e, g, :], "l1wq%d%d" % (g, e), eng="pool")
            wo = d["l1_w_out"].rearrange("(pp e g dd) n -> pp e dd g n", pp=2, e=2, g=4)
            for e in range(2):
                self.dma(WO[e * 64:(e + 1) * 64, :, :], wo[p, e], "l1wo%d" % e, eng="pool")
            if has_sample:
                ckf = self.V(self.P1_off + 12288, F32, [2, 128])
                self.dma(ckf, d["ck"].rearrange("(b t) c -> t b c", t=128)[:, :, p * 128:(p + 1) * 128], "ckf")
                for b in range(2):
                    ps = self.ps[6]
                    self.tr(ps[:, b * 128:(b + 1) * 128], ckf[:, b, :], self.IDF)
                self.cp(CKT[:, p, :], self.ps[6][:, 0:256])
            L1S = self.cfg.get("l1stop", 99)
            if L1S <= 1:
                return
            for kind, tiles in groups:
                w = 0 if kind == "s" else 1
                for ti, tile in enumerate(tiles):
                    if kind == "s":
                        self.dma(ROPE, d["rope"][:, :, tile * TT:(tile + 1) * TT].rearrange("c q t -> q c t"),
                                 "rope")
                    kps = self.ps[6]
                    for kc in range(NCH):
                        self.mm(kps, WK[:, kc, :], self.HT[:, tile, kc, :], start=(kc == 0), stop=(kc == NCH - 1))
                    kdst = KT[:, ti * TT:(ti + 1) * TT]
                    P1S = self.cfg.get("p1", "")
                    if P1S == "k":
                        self.cp(kdst, kps, eng="act")
                        return
                    if kind == "s":
                        self.cp(QB, kps, eng="act")
                        rps = self.ps[7]
                        self.mm(rps, self.ROT, QB)
                        self.tt(DENR, kps, ROPE[:, 0, :], ALU.mult)
                        self.tt(KVO.rearrange("q a b c -> q (a b c)")[:, 0:TT], rps, ROPE[:, 1, :], ALU.mult)
                        self.tt(kdst, DENR, KVO.rearrange("q a b c -> q (a b c)")[:, 0:TT], ALU.add)
                    else:
                        self.cp(kdst, kps, eng="act")
                    if P1S == "rope":
                        return
                    vps = self.ps[0]
                    for blk in range(4):
                        for kc in range(NCH):
                            self.mm(vps[:, blk * 128:(blk + 1) * 128], self.HT[:, tile, kc, blk * 128:(blk + 1) * 128],
                                    WV[:, kc, :], start=(kc == 0), stop=(kc == NCH - 1))
                    if P1S == "v1":
                        return
                    self.cp(VT[:, ti * 4:(ti + 1) * 4, :], vps.rearrange("q (b c) -> q b c", b=4))
                    if P1S == "v2" and ti + 1 >= self.cfg.get("p1n", 1):
                        return
                    if kind == "p":
                        self.cp(KVO[:, :, 1, :], vps.rearrange("q (b c) -> q b c", b=4), eng="act")
                        k2 = self.ps[1]
                        for blk in range(4):
                            for kc in range(NCH):
                                self.mm(k2[:, blk * 128:(blk + 1) * 128],
                                        self.HT[:, tile, kc, blk * 128:(blk + 1) * 128],
                                        WK[:, kc, :], start=(kc == 0), stop=(kc == NCH - 1))
                        self.cp(KVO[:, :, 0, :], k2.rearrange("q (b c) -> q b c", b=4))
                        t0 = (tile - 4) * TT
                        self.dma(d["o_kc"][t0:t0 + TT, p * 128:(p + 1) * 128].rearrange("(b t) c -> t b c", t=128),
                                 KVO[:, :, 0, :], "okc", final=True)
                        self.dma(d["o_vc"][t0:t0 + TT, p * 128:(p + 1) * 128].rearrange("(b t) c -> t b c", t=128),
                                 KVO[:, :, 1, :], "ovc", final=True)
                if L1S <= 2:
                    return
                for ti, tile in enumerate(tiles):
                    if kind == "s":
                        self.dma(ROPE, d["rope"][:, :, tile * TT:(tile + 1) * TT].rearrange("c q t -> q c t"),
                                 "rope")
                    if self.cfg.get("p2", "") == "r":
                        return
                    for g in range(4):
                        qps = self.ps[(g % 2) * 2]
                        for kc in range(NCH):
                            self.mm(qps, WQ[:, kc, g * 128:(g + 1) * 128], self.HT[:, tile, kc, :],
                                    start=(kc == 0), stop=(kc == NCH - 1))
                        P2S = self.cfg.get("p2", "")
                        if P2S == "q0":
                            return
                        if P2S == "q1":
                            self.act(QT[:, g, :], qps, AF.Identity, scale=0.125)
                            return
                        if kind == "s":
                            self.cp(QB, qps, eng="act")
                            rps = self.ps[(g % 2) * 2 + 1]
                            self.mm(rps, self.ROT, QB)
                            t2 = KVO.rearrange("q a b c -> q (a b c)")[:, 0:TT]
                            self.tt(DENR, qps, ROPE[:, 0, :], ALU.mult)
                            self.tt(t2, rps, ROPE[:, 1, :], ALU.mult)
                            self.tt(DENR, DENR, t2, ALU.add)
                            self.act(QT[:, g, :], DENR, AF.Identity, scale=0.125)
                            if P2S == "q2":
                                return
                        else:
                            self.act(QT[:, g, :], qps, AF.Identity, scale=0.125)
                    if L1S <= 3:
                        return
                    steps = []
                    for qb in range(4):
                        if kind == "s":
                            i = ti * 4 + qb
                            kbs = [("c", 0, None), ("c", 1, None)]
                            if i - 1 >= 0:
                                kbs.append(("l", i - 1, self.MNLO))
                            kbs.append(("l", i, None))
                            if i + 1 <= 15:
                                kbs.append(("l", i + 1, self.MNHI))
                        else:
                            sq = qb // 2
                            kbs = [("l", sq * 2, None), ("l", sq * 2 + 1, None)]
                        for ki, kb_ in enumerate(kbs):
                            steps.append((qb, ki, len(kbs), kb_))

                    def scores(stp):
                        qb, ki, nk, (src, kb, mask) = stp
                        qsl = slice(qb * 128, (qb + 1) * 128)
                        sts, pts = [], []
                        for e in range(2):
                            hs = slice(e * 64, (e + 1) * 64)
                            st = self.ps[self.rot("stbank", 4)]
                            kl = CKT[hs, p, kb * 128:(kb + 1) * 128] if src == "c" else KT[hs, kb * 128:(kb + 1) * 128]
                            self.mm(st, kl, QT[hs, :, qsl], start=True, stop=(mask is None))
                            sts.append(st)
                        if mask is not None:
                            for e in range(2):
                                self.mm(sts[e], self.IDB, mask, start=False, stop=True)
                        for e in range(2):
                            pt = PT[self.rot("pt", 4)]
                            self.act(pt, sts[e], AF.Exp)
                            pts.append(pt)
                        return pts

                    def pv(stp, pts):
                        qb, ki, nk, (src, kb, mask) = stp
                        qsl = slice(qb * 128, (qb + 1) * 128)
                        par = qb % 2
                        OT = self.ps[4 + 2 * par]
                        DEN = self.ps[5 + 2 * par]
                        first, last = (ki == 0), (ki == nk - 1)
                        for e in range(2):
                            hs = slice(e * 64, (e + 1) * 64)
                            if src == "c":
                                vl = CV[:, kb, (2 * p + e) * 64:(2 * p + e + 1) * 64]
                            else:
                                vl = VT[:, kb, e * 64:(e + 1) * 64]
                            self.mm(OT[hs, :], vl, pts[e], start=first, stop=last)
                        for e in range(2):
                            hs = slice(e * 64, (e + 1) * 64)
                            self.mm(DEN[hs, :], self.ONE1[:, 0:64], pts[e], start=first, stop=last)
                        if last:
                            self.tt(DENR.rearrange("q (g t) -> q g t", g=4), DEN.rearrange("q (g t) -> q g t", g=4),
                                    self.SINKE[:, p, :].unsqueeze(2).broadcast_to([128, 4, 128]), ALU.add)
                            self.S.add("dve", lambda e_, o=DENR: e_.reciprocal(o, o), outs=[DENR], ins=[DENR])
                            self.tt(OB[:, :, qsl], OT.rearrange("q (g t) -> q g t", g=4),
                                    DENR.rearrange("q (g t) -> q g t", g=4), ALU.mult)

                    prev = None
                    for stp in steps:
                        pts = scores(stp)
                        if prev is not None:
                            pv(*prev)
                        prev = (stp, pts)
                    pv(*prev)
                    if L1S <= 4:
                        return
                    for dc in range(NCH):
                        o = self.ps[self.rot("opbank", 4)]
                        for g in range(4):
                            self.mm(o, WO[:, g, dc * 128:(dc + 1) * 128], OB[:, g, :], start=(g == 0), stop=(g == 3))
                        x = self.XT[:, tile, dc, :]
                        self.stt(x, o, self.GH[:, l, s, w, dc:dc + 1], x, ALU.mult, ALU.add)
                    if p == 1 and self.cfg.get("early_ada", True):
                        self.adanorm(tile, 1, 2)
                        self.prenormed.add(tile)


_CACHE = {}


def _prep_inputs(inp, consts):
    f = lambda a: np.ascontiguousarray(np.asarray(a, dtype=np.float32))
    xs = f(inp["x_sample"])
    xp = f(inp["x_prompt"])
    shared = {
        "mod_w": f(inp["mod_w"]), "ffn_w1": f(inp["ffn_w1"]), "ffn_w3": f(inp["ffn_w3"]),
        "ffn_w2": f(inp["ffn_w2"]), "l0_w_in": f(inp["l0_w_in"]), "l0_w_gf": f(inp["l0_w_gf"]),
        "l0_w_gb": f(inp["l0_w_gb"]), "l0_w_out": f(inp["l0_w_out"]), "l1_w_in": f(inp["l1_w_in"]),
        "l1_w_out": f(inp["l1_w_out"]), "sink": f(inp["l1_sink"]),
        "modb": f(inp["mod_b"]).reshape(144, 128),
        "cb": consts["cb"], "cf": consts["cf"], "rope": consts["rope"], "dftc": consts["dftc"],
        "dft2k": consts["dft2k"], "dft256": consts["dft256"],
    }
    c = f(inp["c"])
    cctx = f(inp["c_ctx"])
    tail = np.concatenate([
        f(inp["norm_g"]).reshape(48, 128), f(inp["final_g"]).reshape(8, 128),
        f(inp["l0_g_head"]).reshape(1, 128), f(inp["l0_b_gf"]).reshape(2, 128),
        f(inp["l0_b_gb"]).reshape(2, 128)], axis=0)
    maps = []
    for b in range(8):
        m = dict(shared)
        m["xin"] = np.ascontiguousarray(np.concatenate([xs[b], xp[4 * b:4 * b + 4].reshape(1024, D)], axis=0))
        m["small1"] = np.ascontiguousarray(np.concatenate([c[b].reshape(8, 128), cctx.reshape(8, 128), tail], 0))
        m["st_f"] = f(inp["state_l0_gla_fwd"])[b]
        m["st_b"] = f(inp["state_l0_gla_bwd"])[b]
        m["ck"] = f(inp["cache_l1_k"])[b].reshape(256, 256)
        m["cv"] = f(inp["cache_l1_v"])[b].reshape(256, 256)
        maps.append(m)
    return maps


def run(inp, cfg=None, trace=False):
    key = repr(sorted((cfg or {}).items()))
    if "consts" not in _CACHE:
        _CACHE["consts"] = _consts()
    consts = _CACHE["consts"]
    B = Builder(cfg)
    nc = B.build()
    maps = _prep_inputs(inp, consts)
    maps = [{k: m[k] for k in B.in_names} for m in maps]
    ncores = (cfg or {}).get("cores", 8)
    c0 = (cfg or {}).get("core0", 0)
    res = run_bass_kernel_spmd(nc, maps[:ncores], core_ids=list(range(c0, c0 + ncores)), trace=trace)
    return res


def kernel(**inp):
    res = run(inp)
    r = res.results
    y = np.stack([r[b]["y"] for b in range(8)], 0)
    y_sample = np.ascontiguousarray(y[:, :2048, :])
    y_prompt = np.ascontiguousarray(y[:, 2048:, :].reshape(32, 256, D))
    sf = np.concatenate([r[b]["o_sf"] for b in range(8)], 0).astype(np.float32)
    sb = np.concatenate([r[b]["o_sb"] for b in range(8)], 0).astype(np.float32)
    kc = np.concatenate([r[b]["o_kc"] for b in range(8)], 0).reshape(32, 256, 4, 64).astype(np.float32)
    vc = np.concatenate([r[b]["o_vc"] for b in range(8)], 0).reshape(32, 256, 4, 64).astype(np.float32)
    return (y_prompt.astype(np.float32), y_sample.astype(np.float32), sf, sb, kc, vc)
```

```python
import numpy as np
import ml_dtypes
from contextlib import ExitStack

import concourse.bass as bass
import concourse.mybir as mybir
from concourse.bass_utils import run_bass_kernel_spmd

F32 = mybir.dt.float32
BF16 = mybir.dt.bfloat16
AF = mybir.ActivationFunctionType
ALU = mybir.AluOpType

D = 1024
NCH = 8
TT = 512
NT = 6
FFN = 2816
NFG = 11
EPS = 1e-6
NTOK = 3072
L0_IN = 2080
L1_IN = 1536

_ESZ = {F32: 4, BF16: 2}
PAGE = 256
POOL_INFLIGHT = 3


class Op:
    __slots__ = ("eng", "fn", "deps", "idx", "sig", "sig_idx", "is_dma", "dkey", "dcount")


class Sched:
    ENGS = ("pe", "act", "dve", "pool", "sp")

    def __init__(self, tracked):
        self.q = {e: [] for e in self.ENGS}
        self.lastw = {}
        self.rd = {}
        self.dma_count = {}
        self.tracked = tracked
        self._pcache = {}
        self.final_dma = []
        self.pool_dmas = []

    def _pages(self, ap):
        t = ap.tensor
        name = t.name
        if name not in self.tracked:
            return ()
        dims = tuple((int(a), int(b)) for a, b in ap.ap)
        key = (name, int(ap.offset), dims, str(ap.dtype))
        r = self._pcache.get(key)
        if r is not None:
            return r
        esz = _ESZ[ap.dtype]
        row = int(t.shape[1])
        off = int(ap.offset)
        p0 = off // row
        f0 = off % row
        pcnt = dims[0][1]
        halves = sorted(set([(p0) // 64, (p0 + pcnt - 1) // 64]))
        starts = np.zeros(1, dtype=np.int64) + f0
        fd = dims[1:]
        if len(fd) == 0:
            run = 1
        else:
            for (st, cn) in fd[:-1]:
                if st == 0 or cn == 1:
                    continue
                starts = (starts[:, None] + (np.arange(cn, dtype=np.int64) * st)[None, :]).reshape(-1)
            st, cn = fd[-1]
            run = (cn - 1) * abs(st) + 1
        pages = set()
        for s in starts.tolist():
            b0 = (s * esz) // PAGE
            b1 = ((s + run) * esz - 1) // PAGE
            for pg in range(b0, b1 + 1):
                pages.add(pg)
        r = tuple((name, h, pg) for h in halves for pg in sorted(pages))
        self._pcache[key] = r
        return r

    def add(self, eng, fn, outs=(), ins=(), dma_key=None, final=False):
        op = Op()
        op.eng = eng
        op.fn = fn
        op.sig = False
        op.sig_idx = 0
        op.is_dma = dma_key is not None
        op.dkey = dma_key
        op.dcount = 0
        if op.is_dma:
            c = self.dma_count.get(dma_key, 0) + 1
            self.dma_count[dma_key] = c
            op.dcount = c
            if final:
                self.final_dma.append(op)
        op.idx = len(self.q[eng])
        deps = {}

        def dep(d):
            if d is op:
                return
            if d.is_dma:
                k = ("dma", d.dkey)
                if k not in deps or deps[k].dcount < d.dcount:
                    deps[k] = d
            else:
                if eng == "pe" and d.eng == "pe" and not op.is_dma:
                    return
                k = ("eng", d.eng)
                if k not in deps or deps[k].idx < d.idx:
                    deps[k] = d

        in_pages = []
        out_pages = []
        for ap in ins:
            pgs = self._pages(ap)
            if pgs and pgs[0][0].startswith("ps"):
                out_pages.extend(sorted(set((n, h, 0) for (n, h, _) in pgs)))
            else:
                in_pages.extend(pgs)
        for ap in outs:
            pgs = self._pages(ap)
            if pgs and pgs[0][0].startswith("ps"):
                out_pages.extend(sorted(set((n, h, 0) for (n, h, _) in pgs)))
            else:
                out_pages.extend(pgs)
        lastw = self.lastw
        rd = self.rd
        for pg in in_pages:
            w = lastw.get(pg)
            if w is not None:
                dep(w)
        for pg in out_pages:
            w = lastw.get(pg)
            if w is not None:
                dep(w)
            rs = rd.get(pg)
            if rs:
                for r in rs.values():
                    dep(r)
        for pg in out_pages:
            lastw[pg] = op
            rd[pg] = {}
        key_r = ("dma", op.dkey, op.dcount) if op.is_dma else op.eng
        for pg in in_pages:
            rs = rd.get(pg)
            if rs is None:
                rs = {}
                rd[pg] = rs
            rs[key_r if not op.is_dma else key_r] = op
        if op.is_dma and eng == "pool":
            pl = self.pool_dmas
            if len(pl) >= POOL_INFLIGHT:
                dep(pl[-POOL_INFLIGHT])
            pl.append(op)
        op.deps = list(deps.values())
        for d in op.deps:
            if not d.is_dma:
                d.sig = True
        self.q[eng].append(op)
        return op

    def emit(self, nc):
        for e in self.ENGS:
            n = 0
            for op in self.q[e]:
                if op.sig and not op.is_dma:
                    n += 1
                    op.sig_idx = n
        with ExitStack() as st:
            esem = {e: st.enter_context(nc.semaphore("sem_" + e)) for e in ("pe", "act", "dve", "pool")}
            dsem = {k: st.enter_context(nc.semaphore("dsem_%d" % i)) for i, k in enumerate(self.dma_count)}
            block = st.enter_context(nc.Block())

            def run(eh, e):
                waited = {}
                for op in self.q[e]:
                    for d in op.deps:
                        if d.is_dma:
                            s, v = dsem[d.dkey], 16 * d.dcount
                        else:
                            s, v = esem[d.eng], d.sig_idx
                        k = id(s)
                        if waited.get(k, 0) < v:
                            eh.wait_ge(s, v)
                            waited[k] = v
                    ins = op.fn(eh)
                    if op.is_dma:
                        ins.then_inc(dsem[op.dkey], 16)
                    elif op.sig:
                        ins.then_inc(esem[e], 1)
                if e == "sp":
                    for k, c in self.dma_count.items():
                        eh.wait_ge(dsem[k], 16 * c)

            @block.tensor
            def _(eh):
                run(eh, "pe")

            @block.scalar
            def _(eh):
                run(eh, "act")

            @block.vector
            def _(eh):
                run(eh, "dve")

            @block.gpsimd
            def _(eh):
                run(eh, "pool")

            @block.sync
            def _(eh):
                run(eh, "sp")


def _bf(a):
    return np.asarray(a, dtype=np.float32).astype(ml_dtypes.bfloat16)


CB_ID, CB_ONES_D, CB_ONES_DV, CB_ONE1, CB_MF, CB_MB, CB_ROT, CB_MNLO, CB_MNHI = (
    0, 128, 256, 384, 512, 640, 768, 896, 1408)
CB_N = 1920


def _consts():
    i = np.arange(128)
    cb = np.zeros((128, CB_N), np.float32)
    cb[:, CB_ID:CB_ID + 128] = np.eye(128)
    cb[:, CB_ONES_D:CB_ONES_D + 128] = 1.0 / 1024.0
    cb[:, CB_ONES_DV:CB_ONES_DV + 128] = 1.0 / 128.0
    cb[:, CB_ONE1:CB_ONE1 + 128] = 1.0
    s = i[:, None]
    t = i[None, :]
    same = (s // 64) == (t // 64)
    cb[:, CB_MF:CB_MF + 128] = (same & (s <= t)).astype(np.float32)
    cb[:, CB_MB:CB_MB + 128] = (same & (s >= t)).astype(np.float32)
    R = np.zeros((64, 64), np.float32)
    for m in range(16):
        R[m, 16 + m] = -1.0
        R[16 + m, m] = 1.0
        R[32 + m, 48 + m] = -1.0
        R[48 + m, 32 + m] = 1.0
    RT = np.zeros((128, 128), np.float32)
    RT[:64, :64] = R.T
    RT[64:, 64:] = R.T
    cb[:, CB_ROT:CB_ROT + 128] = RT
    lo = np.where(s >= t, 0.0, -30000.0).astype(np.float32)
    hi = np.where(s <= t, 0.0, -30000.0).astype(np.float32)
    cb[:, CB_MNLO:CB_MNLO + 512] = np.tile(lo, (1, 4))
    cb[:, CB_MNHI:CB_MNHI + 512] = np.tile(hi, (1, 4))
    cf = np.zeros((128, 128 + 512), np.float32)
    cf[:, :128] = np.eye(128)
    m = np.ones(512, np.float32)
    m[::64] = 0.0
    cf[:, 128:] = m[None, :]
    tt = np.arange(2048)
    row = (tt // 64).astype(np.float32)
    col = (tt % 64).astype(np.float32)
    inv = (np.float32(10000.0) ** (-np.arange(16, dtype=np.float32) / np.float32(16))).astype(np.float32)
    ar = row[:, None] * inv[None, :]
    ac = col[:, None] * inv[None, :]
    ang = np.concatenate([ar, ar, ac, ac], axis=-1).astype(np.float32)
    cos = np.cos(ang).astype(np.float32).T
    sin = np.sin(ang).astype(np.float32).T
    rope = np.stack([np.concatenate([cos, cos], 0), np.concatenate([sin, sin], 0)], 0)
    def dft(n):
        k = np.arange(n)
        kk = (k[:, None] * k[None, :]) % n
        a = 2.0 * np.pi * kk.astype(np.float64) / n
        return np.cos(a) / np.sqrt(n), np.sin(a) / np.sqrt(n)
    c128, s128 = dft(128)
    dftc = np.stack([c128, -s128], 0)
    c2k, s2k = dft(2048)
    c256, s256 = dft(256)
    return dict(cb=_bf(cb), cf=cf, rope=np.ascontiguousarray(rope), dftc=_bf(dftc),
                dft2k=_bf(np.stack([c2k, s2k], 0)), dft256=_bf(np.stack([c256, s256], 0)))


class Builder:
    def __init__(self, cfg=None):
        self.cfg = cfg or {}
        self.nc = bass.Bass("TRN2", target_bir_lowering=False)
        self.dram = {}
        self.in_names = []
        self.out_names = []

    def din(self, name, shape, dt=F32):
        self.dram[name] = self.nc.dram_tensor(name, list(shape), dt, kind="ExternalInput").ap()
        self.in_names.append(name)
        return self.dram[name]

    def dout(self, name, shape, dt=F32):
        self.dram[name] = self.nc.dram_tensor(name, list(shape), dt, kind="ExternalOutput").ap()
        self.out_names.append(name)
        return self.dram[name]

    def V(self, off, dt, shape):
        esz = _ESZ[dt]
        n = int(np.prod(shape))
        assert off % 4 == 0 and (n * esz) % 4 == 0
        assert off + n * esz <= self.arena_bytes, (off, n * esz, self.arena_bytes)
        a = self.arena[:, off // 4:(off + n * esz) // 4]
        if dt != F32:
            a = a.bitcast(dt)
        if len(shape) > 1:
            names = ["d%d" % i for i in range(len(shape))]
            kw = {names[i]: int(shape[i]) for i in range(len(shape) - 1)}
            a = a.rearrange("p (%s) -> p %s" % (" ".join(names), " ".join(names)), **kw)
        return a

    def build(self):
        nc = self.nc
        cfg = self.cfg
        d = self.dram
        self.din("xin", [NTOK, D])
        self.din("small1", [77, 128])
        self.din("modb", [144, 128])
        self.din("st_f", [4, 64, 128])
        self.din("st_b", [4, 64, 128])
        self.din("ck", [256, 256])
        self.din("cv", [256, 256])
        self.din("sink", [4, 4])
        self.din("mod_w", [2, D, 9 * D])
        self.din("ffn_w1", [2, 2, D, FFN])
        self.din("ffn_w3", [2, 2, D, FFN])
        self.din("ffn_w2", [2, 2, FFN, D])
        self.din("l0_w_in", [D, L0_IN])
        self.din("l0_w_gf", [16, 256])
        self.din("l0_w_gb", [16, 256])
        self.din("l0_w_out", [D, D])
        self.din("l1_w_in", [D, L1_IN])
        self.din("l1_w_out", [D, D])
        self.din("cb", [128, CB_N], BF16)
        self.din("cf", [128, 640])
        self.din("rope", [2, 128, 2048])
        self.din("dftc", [2, 128, 128], BF16)
        self.din("dft2k", [2, 2048, 2048], BF16)
        self.din("dft256", [2, 256, 256], BF16)
        self.dout("y", [NTOK, D])
        self.dout("o_sf", [4, 4, 64, 128])
        self.dout("o_sb", [4, 4, 64, 128])
        self.dout("o_kc", [1024, 256])
        self.dout("o_vc", [1024, 256])

        self.arena_bytes = 212480
        with ExitStack() as st:
            self.arena = st.enter_context(nc.sbuf_tensor("arena", [128, self.arena_bytes // 4], F32))
            self.ps = [st.enter_context(nc.psum_tensor("ps%d" % i, [128, 512], F32))[:, :] for i in range(8)]
            tracked = {"arena"} | {"ps%d" % i for i in range(8)}
            self.S = Sched(tracked)
            self._layout()
            self._program()
            self.S.emit(nc)
        return nc

    def _layout(self):
        V = self.V
        o = 0
        self.XT = V(o, F32, [NT, NCH, TT]); o += NT * NCH * TT * 4
        self.HT_off = o
        self.HT = V(o, BF16, [NT, NCH, TT]); o += NT * NCH * TT * 2
        self.WA_off = o
        self.W1 = [V(o + s * 12288, BF16, [NCH, 256]) for s in range(2)]
        self.W3 = [V(o + s * 12288 + 4096, BF16, [NCH, 256]) for s in range(2)]
        self.W2 = [V(o + s * 12288 + 8192, BF16, [2, D]) for s in range(2)]
        o += 24576
        self.CONST_off = o
        c = o
        self.CB = V(c, BF16, [CB_N]); c += CB_N * 2
        self.CF = V(c, F32, [640]); c += 2560
        self.MODT = V(c, F32, [2, 72, 2]); c += 2 * 72 * 2 * 4
        self.SM1 = V(c, F32, [77]); c += 77 * 4 + 0
        self.AM = V(c, F32, [2, 3, 2, NCH]); c += 2 * 3 * 2 * NCH * 4
        self.GH = V(c, F32, [2, 3, 2, NCH]); c += 384
        self.SC = V(c, BF16, [2, NCH]); c += 32
        self.MBT = V(c, F32, [2, 72]); c += 576
        self.SINKE = V(c, F32, [2, 4]); c += 32
        self.NBG = V(c, F32, [2, 2]); c += 16
        assert c - o <= 9472, c - o
        o += 9472
        self.P1_off = o
        self.P1_size = self.arena_bytes - o
        p = o
        self.G = V(p, BF16, [2, 2, TT]); p += 4096
        self.SA = V(p, F32, [2, TT]); p += 4096
        self.SQ = V(p, BF16, [4, TT]); p += 4096
        self.RSTD = V(p, F32, [2, TT]); p += 4096
        self.TMP = V(p, F32, [2, TT]); p += 4096
        self.MODW = [V(p + s * 4096, BF16, [NCH, 256]) for s in range(2)]; p += 8192
        self.P1_ffn_end = p
        assert p <= self.arena_bytes, p
        self.cnt = {}
        self.prenormed = set()

    def rot(self, key, n):
        v = self.cnt.get(key, 0)
        self.cnt[key] = v + 1
        return v % n

    def mm(self, out, lhsT, rhs, start=True, stop=True):
        return self.S.add("pe", lambda e: e.matmul(out, lhsT, rhs, start=start, stop=stop),
                          outs=[out], ins=[lhsT, rhs])

    def tr(self, out, in_, ident):
        return self.S.add("pe", lambda e: e.transpose(out, in_, ident), outs=[out], ins=[in_, ident])

    def act(self, out, in_, func, bias=None, scale=None, eng="act"):
        ins = [in_]
        kw = {}
        if bias is not None:
            kw["bias"] = bias
            if not isinstance(bias, (int, float)):
                ins.append(bias)
        if scale is not None:
            kw["scale"] = scale
            if not isinstance(scale, (int, float)):
                ins.append(scale)
        return self.S.add("act", lambda e: e.activation(out, in_, func, **kw), outs=[out], ins=ins)

    def sigmoid(self, out, in_):
        self.act(out, in_, AF.Exp, scale=-1.0)
        self.act(out, out, AF.Ln, bias=1.0, scale=1.0)
        self.act(out, out, AF.Exp, scale=-1.0)

    def rsqrt(self, out, in_, eps=EPS):
        self.act(out, in_, AF.Ln, bias=eps, scale=1.0)
        self.act(out, out, AF.Exp, scale=-0.5)

    def ts(self, out, in0, s1, s2, op0, op1=None, eng="dve"):
        ins = [in0] + [s for s in (s1, s2) if s is not None and not isinstance(s, (int, float))]
        if op1 is None:
            return self.S.add(eng, lambda e: e.tensor_scalar(out, in0, s1, s2, op0), outs=[out], ins=ins)
        return self.S.add(eng, lambda e: e.tensor_scalar(out, in0, s1, s2, op0, op1), outs=[out], ins=ins)

    def stt(self, out, in0, scalar, in1, op0, op1, eng="dve"):
        ins = [in0, in1] + ([] if isinstance(scalar, (int, float)) else [scalar])
        return self.S.add(eng, lambda e: e.scalar_tensor_tensor(out, in0, scalar, in1, op0, op1),
                          outs=[out], ins=ins)

    def tt(self, out, in0, in1, op, eng="dve"):
        return self.S.add(eng, lambda e: e.tensor_tensor(out, in0, in1, op), outs=[out], ins=[in0, in1])

    def cp(self, out, in_, eng="dve"):
        if eng == "act":
            return self.S.add("act", lambda e: e.copy(out, in_), outs=[out], ins=[in_])
        return self.S.add(eng, lambda e: e.tensor_copy(out, in_), outs=[out], ins=[in_])

    def memset(self, out, val, eng="dve"):
        return self.S.add(eng, lambda e: e.memset(out, val), outs=[out], ins=[])

    def dma(self, out, in_, key, eng="sp", final=False):
        return self.S.add(eng, lambda e: e.dma_start(out=out, in_=in_), outs=[out], ins=[in_],
                          dma_key=key, final=final)

    def _program(self):
        cfg = self.cfg
        stop = cfg.get("stop", "")
        self.load_consts()
        self.mod_queue = [(l, pc) for l in range(2) for pc in range(36)]
        self.mod_lq = list(self.mod_queue)
        self.mod_loaded = []
        self.preloaded = False
        self.mod_load()
        self.mod_load()
        if stop == "consts":
            self.mod_queue = []
            return self.final_out()
        for _ in range(12):
            self.mod_piece()
        if stop == "mod":
            self.mod_queue = []
            return self.final_out()
        self.load_x()
        if stop == "loadx":
            self.mod_queue = []
            return self.final_out()
        for l in cfg.get("layer_list", list(range(cfg.get("layers", 2)))):
            self.mod_ensure(l, 0)
            self.ffn(l, 0)
            self.mod_ensure(l, 2)
            self.mod_ensure(l, 1)
            self.mod_drain()
            if cfg.get("mixers", True):
                if l == 0:
                    self.mixer_l0()
                else:
                    self.mixer_l1()
            self.ffn(l, 1)
        while self.mod_queue:
            self.mod_piece()
        self.final_out()

    def load_consts(self):
        d = self.dram
        CB, CF = self.CB, self.CF
        self.dma(CB, d["cb"], "c_cb")
        self.dma(CF, d["cf"], "c_cf")
        self.IDB = CB[:, CB_ID:CB_ID + 128]
        self.ONES_D = CB[:, CB_ONES_D:CB_ONES_D + 128]
        self.ONES_DV = CB[:, CB_ONES_DV:CB_ONES_DV + 128]
        self.ONE1 = CB[:, CB_ONE1:CB_ONE1 + 128]
        self.MF = CB[:, CB_MF:CB_MF + 128]
        self.MB = CB[:, CB_MB:CB_MB + 128]
        self.ROT = CB[:, CB_ROT:CB_ROT + 128]
        self.MNLO = CB[:, CB_MNLO:CB_MNLO + 512]
        self.MNHI = CB[:, CB_MNHI:CB_MNHI + 512]
        self.IDF = CF[:, 0:128]
        self.SCANM = CF[:, 128:640]
        P1 = self.P1_off
        s1 = self.V(P1, F32, [128])
        mb = self.V(P1 + 512, F32, [2, 128])
        self.dma(s1[0:77, :], d["small1"], "c_s1")
        self.dma(mb[0:72, :, :], d["modb"].rearrange("(l j) f -> j l f", l=2), "c_mb")
        ps = self.ps[7]
        self.tr(ps[:, 0:77], s1[0:77, :], self.IDF[0:77, 0:77])
        self.cp(self.SM1, ps[:, 0:77])
        for l in range(2):
            self.tr(ps[:, 128 + l * 72:128 + (l + 1) * 72], mb[0:72, l, :], self.IDF[0:72, 0:72])
        self.cp(self.MBT, ps[:, 128:272].rearrange("p (l j) -> p l j", l=2))
        self.NG = self.SM1[:, 16:64].rearrange("p (l s c) -> p l s c", l=2, s=3)
        self.FG = self.SM1[:, 64:72]
        self.GHEAD = self.SM1[:, 72:73]
        self.BGF = self.SM1[:, 73:75]
        self.BGB = self.SM1[:, 75:77]
        t0 = self.V(P1 + 2048, F32, [16])
        self.sigmoid(t0, self.SM1[:, 0:16])
        self.tt(self.SC.rearrange("p w c -> p (w c)"), self.SM1[:, 0:16], t0, ALU.mult)
        self.mod_ps_cols = 0

    def mod_load(self):
        if not self.mod_lq:
            return
        l, pc = self.mod_lq.pop(0)
        d = self.dram
        slot = self.rot("modw_l", 2)
        src = d["mod_w"][l].rearrange("(kc p) n -> p kc n", p=128)[:, :, pc * 256:(pc + 1) * 256]
        self.dma(self.MODW[slot], src, "modw%d" % slot, eng="pool")
        self.mod_loaded.append((l, pc, slot))

    def mod_drain(self):
        while self.mod_loaded:
            self.mod_piece(prefetch=False)

    def mod_piece(self, prefetch=True, defer=False):
        if not self.mod_queue:
            return
        if not self.mod_loaded:
            self.mod_load()
        l, pc, slot = self.mod_loaded.pop(0)
        assert (l, pc) == self.mod_queue.pop(0)
        W = self.MODW[slot]
        ps = self.ps[7]
        c0 = 384 + slot * 4
        for jj in range(2):
            o = ps[:, c0 + jj * 2:c0 + jj * 2 + 2]
            for kc in range(NCH):
                self.mm(o, W[:, kc, jj * 128:(jj + 1) * 128], self.SC[:, :, kc],
                        start=(kc == 0), stop=(kc == NCH - 1))
        if prefetch:
            self.mod_load()
        if defer:
            return lambda: self._mod_evac(l, pc, c0)
        self._mod_evac(l, pc, c0)

    def _mod_evac(self, l, pc, c0):
        ps = self.ps[7]
        j0 = pc * 2
        self.tt(self.MODT[:, l, j0:j0 + 2, :], ps[:, c0:c0 + 4].rearrange("p (j w) -> p j w", j=2),
                self.MBT[:, l, j0:j0 + 2].unsqueeze(2).broadcast_to([128, 2, 2]), ALU.add)
        if pc % 12 == 11:
            s = pc // 12
            for w in range(2):
                self.stt(self.AM[:, l, s, w, :], self.MODT[:, l, (3 * s + 1) * 8:(3 * s + 2) * 8, w], 1.0,
                         self.NG[:, l, s, :], ALU.add, ALU.mult)
                gsrc = self.MODT[:, l, (3 * s + 2) * 8:(3 * s + 3) * 8, w]
                if s == 1:
                    self.cp(self.GH[:, l, s, w, :], gsrc)
                else:
                    self.ts(self.GH[:, l, s, w, :], gsrc, 0.5, None, ALU.mult)

    def mod_ensure(self, l, sgrp):
        while self.mod_queue and self.mod_queue[0] <= (l, 12 * sgrp + 11):
            self.mod_piece()

    def shift(self, l, s, w, c):
        j = (3 * s) * 8 + c
        return self.MODT[:, l, j, w:w + 1]

    def load_x(self):
        d = self.dram
        P1 = self.P1_ffn_end
        stg = [self.V(self.HT_off + s * 4096, F32, [D]) for s in range(2)]
        for tb in range(NTOK // 128):
            tile, sub = tb // 4, tb % 4
            sl = stg[tb % 2]
            self.dma(sl, d["xin"][tb * 128:(tb + 1) * 128, :], "xin%d" % (tb % 2))
            for hb in range(2):
                ps = self.ps[(tb % 2) * 2 + hb]
                for cc in range(4):
                    c = hb * 4 + cc
                    self.tr(ps[:, cc * 128:(cc + 1) * 128], sl[:, c * 128:(c + 1) * 128], self.IDF)
                dst = self.XT[:, tile, hb * 4:(hb + 1) * 4, sub * 128:(sub + 1) * 128]
                src = ps[:, :].rearrange("p (c t) -> p c t", c=4)
                if hb == 0:
                    self.cp(dst, src, eng="act")
                else:
                    self.cp(dst, src, eng="dve")

    def who(self, tile):
        return 0 if tile < 4 else 1

    def adanorm(self, tile, l, s, dst=None):
        w = self.who(tile)
        ms = self.ps[7][:, 0:TT] if False else self.ps[7]
        msv = self.ps[7][:, 0:TT]
        for half in range(2):
            self.act(self.SQ, self.XT[:, tile, half * 4:(half + 1) * 4, :], AF.Square)
            for cc in range(4):
                c = half * 4 + cc
                self.mm(msv, self.ONES_D, self.SQ[:, cc, :], start=(c == 0), stop=(c == NCH - 1))
        r = self.RSTD[:, self.rot("rstd", 2), :]
        self.rsqrt(r, msv)
        for c in range(NCH):
            t = self.TMP[:, self.rot("tmp", 2), :]
            self.stt(t, self.XT[:, tile, c, :], self.AM[:, l, s, w, c:c + 1], r, ALU.mult, ALU.mult)
            out = self.HT[:, tile, c, :] if dst is None else dst[:, c, :]
            self.act(out, t, AF.Identity, bias=self.shift(l, s, w, c), scale=1.0)

    def ffn_load(self, l, hs, fg):
        d = self.dram
        slot = fg % 2
        w1 = d["ffn_w1"][l, hs].rearrange("(kc p) f -> p kc f", p=128)[:, :, fg * 256:(fg + 1) * 256]
        w3 = d["ffn_w3"][l, hs].rearrange("(kc p) f -> p kc f", p=128)[:, :, fg * 256:(fg + 1) * 256]
        w2 = d["ffn_w2"][l, hs][fg * 256:(fg + 1) * 256, :].rearrange("(fc p) n -> p fc n", p=128)
        sk = self.cfg.get("skip_load", 0)
        if sk in (0, 2):
            self.dma(self.W1[slot], w1, "w1_%d" % slot, eng="pool")
            self.dma(self.W3[slot], w3, "w3_%d" % slot, eng="pool")
        if sk in (0, 3):
            self.dma(self.W2[slot], w2, "w2_%d" % slot, eng="pool")

    def ffn(self, l, hs):
        s = 0 if hs == 0 else 2
        ntiles = self.cfg.get("ntiles", NT)
        while len(self.mod_loaded) < 2 and self.mod_lq:
            self.mod_load()
        if self.preloaded:
            self.preloaded = False
        else:
            self.ffn_load(l, hs, 0)
            self.ffn_load(l, hs, 1)
        for tile in range(ntiles):
            if tile in self.prenormed:
                continue
            self.adanorm(tile, l, s)
        self.prenormed = set()
        if self.cfg.get("fstop") == "adanorm":
            return
        pend = None
        step = 0

        def do_o(p):
            fg, tile, par = p
            slot = fg % 2
            w = self.who(tile)
            for dc in range(NCH):
                o = self.ps[4 + self.rot("obank", 4)]
                for fc in range(2):
                    self.mm(o, self.W2[slot][:, fc, dc * 128:(dc + 1) * 128], self.G[:, par, fc, :],
                            start=(fc == 0), stop=(fc == 1))
                x = self.XT[:, tile, dc, :]
                if dc in (0, 1, 2, 4, 6):
                    self.stt(x, o, self.GH[:, l, s, w, dc:dc + 1], x, ALU.mult, ALU.add)
                else:
                    t = self.TMP[:, self.rot("tmpo", 2), :]
                    self.act(t, o, AF.Identity, scale=self.GH[:, l, s, w, dc:dc + 1])
                    self.tt(x, x, t, ALU.add)

        nfg = self.cfg.get("nfg", NFG)
        for fg in range(nfg):
            slot = fg % 2
            for tile in range(ntiles):
                par = step % 2
                step += 1
                for fc in range(2):
                    a = self.ps[fc]
                    b = self.ps[2 + fc]
                    for kc in range(NCH):
                        self.mm(a, self.W1[slot][:, kc, fc * 128:(fc + 1) * 128], self.HT[:, tile, kc, :],
                                start=(kc == 0), stop=(kc == NCH - 1))
                    for kc in range(NCH):
                        self.mm(b, self.W3[slot][:, kc, fc * 128:(fc + 1) * 128], self.HT[:, tile, kc, :],
                                start=(kc == 0), stop=(kc == NCH - 1))
                    sa = self.SA[:, fc, :]
                    self.sigmoid(sa, a)
                    if fc == 0:
                        self.tt(sa, sa, a, ALU.mult)
                        self.tt(self.G[:, par, fc, :], sa, b, ALU.mult)
                if pend is not None:
                    do_o(pend)
                    if pend[1] == ntiles - 1 and pend[0] + 2 < nfg:
                        self.ffn_load(l, hs, pend[0] + 2)
                sa = self.SA[:, 1, :]
                self.tt(sa, sa, self.ps[1], ALU.mult)
                self.tt(self.G[:, par, 1, :], sa, self.ps[3], ALU.mult)
                pend = (fg, tile, par)
            evs = [self.mod_piece(defer=True) for _ in range(2)]
            for ev in evs:
                if ev is not None:
                    ev()
        if self.cfg.get("fstop") in ("ab", "sig", "g"):
            return
        do_o(pend)

    def final_out(self):
        d = self.dram
        ntiles = self.cfg.get("ntiles", NT)
        YT = self.V(self.HT_off, F32, [NCH, TT])
        stg = [self.V(self.HT_off + 16384 + s * 4096, F32, [D]) for s in range(2)]
        for tile in range(ntiles):
            msv = self.ps[7][:, 0:TT]
            for half in range(2):
                self.act(self.SQ, self.XT[:, tile, half * 4:(half + 1) * 4, :], AF.Square)
                for cc in range(4):
                    c = half * 4 + cc
                    self.mm(msv, self.ONES_D, self.SQ[:, cc, :], start=(c == 0), stop=(c == NCH - 1))
            r = self.RSTD[:, self.rot("rstd", 2), :]
            self.rsqrt(r, msv)
            for c in range(NCH):
                self.stt(YT[:, c, :], self.XT[:, tile, c, :], self.FG[:, c:c + 1], r, ALU.mult, ALU.mult)
            for sub in range(4):
                tb = tile * 4 + sub
                sl = stg[tb % 2]
                for hb in range(2):
                    ps = self.ps[(tb % 2) * 2 + hb]
                    for cc in range(4):
                        c = hb * 4 + cc
                        self.tr(ps[:, cc * 128:(cc + 1) * 128], YT[:, c, sub * 128:(sub + 1) * 128], self.IDF)
                    if hb == 0:
                        self.cp(sl[:, 0:512], ps[:, :], eng="act")
                    else:
                        self.cp(sl[:, 512:1024], ps[:, :], eng="dve")
                self.dma(d["y"][tb * 128:(tb + 1) * 128, :], sl, "yout%d" % (tb % 2), final=True)

    def scan(self, out, d0, d1):
        return self.S.add("dve", lambda e: e.tensor_tensor_scan(out, d0, d1, 0.0, ALU.mult, ALU.add),
                          outs=[out], ins=[d0, d1])

    def mixer_l0(self):
        ntiles = self.cfg.get("ntiles", NT)
        self.ts(self.NBG[:, 0, :], self.BGF, -1.0, None, ALU.mult)
        self.ts(self.NBG[:, 1, :], self.BGB, -1.0, None, ALU.mult)
        groups = []
        if ntiles >= 4:
            groups.append(("s", [0, 1, 2, 3], self.HT_off + 4 * 8192))
        for t in range(4, ntiles):
            groups.append(("p", [t], self.HT_off))
        for kind, tiles, scr in groups:
            for tile in tiles:
                self.adanorm(tile, 0, 1)
            if self.cfg.get("fnet", True):
                self.fnet(kind, tiles, scr)
            if self.cfg.get("gla", True):
                for p in range(2):
                    self.gla_pair(kind, tiles, scr, p)

    def fnet(self, kind, tiles, scr):
        d = self.dram
        WA, P1 = self.WA_off, self.P1_off
        WU = self.V(WA, BF16, [NCH, 512])
        WOF = self.V(WA + 8192, BF16, [4, D])
        DFTC = self.V(WA + 16384, BF16, [2, 128])
        self.dma(WU, d["l0_w_in"].rearrange("(kc q) n -> q kc n", q=128)[:, :, 1568:2080], "l0wu", eng="pool")
        self.dma(WOF, d["l0_w_out"][512:1024, :].rearrange("(g q) n -> q g n", q=128), "l0wof", eng="pool")
        self.dma(DFTC, d["dftc"].rearrange("cs c k -> c cs k"), "dftc")
        UT = self.V(scr, BF16, [16, 512])
        for ti, tile in enumerate(tiles):
            for blk in range(4):
                ps = self.ps[blk % 2]
                for kc in range(NCH):
                    self.mm(ps, self.HT[:, tile, kc, blk * 128:(blk + 1) * 128], WU[:, kc, :],
                            start=(kc == 0), stop=(kc == NCH - 1))
                self.cp(UT[:, ti * 4 + blk, :], ps, eng=("act" if blk % 2 else "dve"))
        DT = [self.V(P1 + i * 8192, BF16, [4, 2, 512]) for i in range(2)]
        ABT = self.V(P1 + 16384, BF16, [2, 4, 512])
        FN = self.V(P1 + 24576, BF16, [4, 512])
        DT256 = self.V(P1, BF16, [2, 2, 256])
        if kind == "p":
            for cs_ in range(2):
                self.dma(DT256[:, :, cs_, :], d["dft256"][cs_].rearrange("(tb t) tp -> t tb tp", t=128), "dt256_%d" % cs_)
        for j, tile in enumerate(tiles):
            if kind == "s":
                for qd in range(4):
                    slot = self.rot("dt", 2)
                    src = d["dft2k"][:, qd * 512:(qd + 1) * 512, j * 512:(j + 1) * 512]
                    for cs_ in range(2):
                        self.dma(DT[slot][:, :, cs_, :], src[cs_].rearrange("(tb t) tp -> t tb tp", t=128),
                                 "dt%d_%d" % (slot, cs_))
                    for tb4 in range(4):
                        tb = qd * 4 + tb4
                        for g in range(4):
                            lhsT = UT[:, tb, g * 128:(g + 1) * 128]
                            self.mm(self.ps[g], lhsT, DT[slot][:, tb4, 0, :], start=(tb == 0), stop=(tb == 15))
                            self.mm(self.ps[4 + g], lhsT, DT[slot][:, tb4, 1, :], start=(tb == 0), stop=(tb == 15))
            else:
                for sq in range(2):
                    for tb in range(2):
                        for g in range(4):
                            lhsT = UT[:, sq * 2 + tb, g * 128:(g + 1) * 128]
                            cs = slice(sq * 256, (sq + 1) * 256)
                            self.mm(self.ps[g][:, cs], lhsT, DT256[:, tb, 0, :], start=(tb == 0), stop=(tb == 1))
                            self.mm(self.ps[4 + g][:, cs], lhsT, DT256[:, tb, 1, :], start=(tb == 0), stop=(tb == 1))
            for g in range(4):
                self.cp(ABT[:, 0, g, :], self.ps[g], eng="act")
                self.cp(ABT[:, 1, g, :], self.ps[4 + g], eng="dve")
            for g in range(4):
                y = self.ps[g]
                self.mm(y, DFTC[:, 0, :], ABT[:, 0, g, :], start=True, stop=False)
                self.mm(y, DFTC[:, 1, :], ABT[:, 1, g, :], start=False, stop=True)
                self.cp(FN[:, g, :], y, eng=("act" if g % 2 else "dve"))
            w = self.who(tile)
            for dc in range(NCH):
                o = self.ps[4 + self.rot("opb0", 4)]
                for g in range(4):
                    self.mm(o, WOF[:, g, dc * 128:(dc + 1) * 128], FN[:, g, :], start=(g == 0), stop=(g == 3))
                x = self.XT[:, tile, dc, :]
                self.stt(x, o, self.GH[:, 0, 1, w, dc:dc + 1], x, ALU.mult, ALU.add)

    def gla_pair(self, kind, tiles, scr, p):
        d = self.dram
        WA, P1 = self.WA_off, self.P1_off
        V = self.V
        WQp = V(WA, BF16, [NCH, 128])
        WKp = V(WA + 2048, BF16, [NCH, 128])
        WVp = V(WA + 4096, BF16, [NCH, 256])
        WGp = V(WA + 8192, BF16, [NCH, 256])
        WLF = V(WA + 12288, BF16, [NCH, 16])
        WLB = V(WA + 12544, BF16, [NCH, 16])
        WGF = V(WA + 12800, BF16, [128])
        WGB = V(WA + 13056, BF16, [128])
        WOp = V(WA + 13312, BF16, [2, D])
        src = d["l0_w_in"].rearrange("(kc q) n -> q kc n", q=128)
        self.dma(WLF, src[:, :, 1536:1552], "g_wlf", eng="pool")
        self.dma(WKp, src[:, :, 256 + p * 128:256 + (p + 1) * 128], "g_wk", eng="pool")
        self.dma(WGF[0:16, :], d["l0_w_gf"][:, p * 128:(p + 1) * 128], "g_wgf", eng="pool")
        self.dma(WVp, src[:, :, 512 + p * 256:512 + (p + 1) * 256], "g_wv", eng="pool")
        self.dma(WQp, src[:, :, p * 128:(p + 1) * 128], "g_wq", eng="pool")
        self.dma(WLB, src[:, :, 1552:1568], "g_wlb", eng="pool")
        self.dma(WGB[0:16, :], d["l0_w_gb"][:, p * 128:(p + 1) * 128], "g_wgb", eng="pool")
        self.dma(WGp, src[:, :, 1024 + p * 256:1024 + (p + 1) * 256], "g_wg", eng="pool")
        self.dma(WOp, d["l0_w_out"][p * 256:(p + 1) * 256, :].rearrange("(h q) n -> q h n", q=128), "g_wo", eng="pool")
        o = P1
        SP = V(o, F32, [TT]); o += 2048
        PSP = V(o, F32, [TT]); o += 2048
        T3 = V(o, F32, [TT]); o += 2048
        E1 = V(o, F32, [TT]); o += 2048
        E2 = V(o, F32, [TT]); o += 2048
        QDF = V(o, BF16, [TT]); o += 1024
        KIF = V(o, BF16, [TT]); o += 1024
        QDB = V(o, BF16, [TT]); o += 1024
        KIB = V(o, BF16, [TT]); o += 1024
        KE = V(o, BF16, [TT]); o += 1024
        KETOK = V(o, BF16, [4, 128]); o += 1024
        VTOK = V(o, BF16, [4, 256]); o += 2048
        LFT = V(o, BF16, [TT]); o += 1024
        LBT = LFT
        ATF = V(o, BF16, [2, 128]); o += 512
        ATB = V(o, BF16, [2, 128]); o += 512
        SALL = V(o, F32, [8, 128]); o += 4096
        INIT = V(o, F32, [128]); o += 512
        ZERO = V(o, F32, [128]); o += 512
        SB16 = V(o, BF16, [8, 128]); o += 2048
        GL = V(o, BF16, [2, TT]); o += 2048
        SQ16 = V(o, BF16, [TT]); o += 1024
        assert o <= self.arena_bytes, o
        SF16 = V(scr, BF16, [33, 128])
        self.memset(ZERO, 0.0)
        ps = self.ps
        c3 = lambda a: a.rearrange("q (c t) -> q c t", c=8)
        nb_f = self.NBG[:, 0, p:p + 1]
        nb_b = self.NBG[:, 1, p:p + 1]
        ktb = ps[5].bitcast(BF16)
        ktb3 = ps[3].bitcast(BF16)

        def proj_small(dst, W, tile):
            lp = ps[2][0:16, :]
            for kc in range(NCH):
                self.mm(lp, W[:, kc, :], self.HT[:, tile, kc, :], start=(kc == 0), stop=(kc == NCH - 1))
            self.cp(dst[0:16, :], lp, eng="act")

        def proj(bank, W, tile):
            for kc in range(NCH):
                self.mm(bank, W[:, kc, :], self.HT[:, tile, kc, :], start=(kc == 0), stop=(kc == NCH - 1))

        def softplus_neg(glog, nb):
            self.act(SP, glog, AF.Exp, bias=nb, scale=-1.0)
            self.act(SP, SP, AF.Ln, bias=1.0, scale=1.0)

        def ke_tok(tb):
            for blk in range(4):
                self.tr(tb[:, blk * 128:(blk + 1) * 128], KE[:, blk * 128:(blk + 1) * 128], self.IDB)
            self.cp(KETOK, tb[:, 0:512].rearrange("q (b c) -> q b c", b=4))

        def v_tok(tile):
            for blk in range(4):
                vp = ps[6 + blk // 2][:, (blk % 2) * 256:(blk % 2 + 1) * 256]
                for kc in range(NCH):
                    self.mm(vp, self.HT[:, tile, kc, blk * 128:(blk + 1) * 128], WVp[:, kc, :],
                            start=(kc == 0), stop=(kc == NCH - 1))
            self.cp(VTOK[:, 0:2, :], ps[6].rearrange("q (b c) -> q b c", b=2), eng="dve")
            self.cp(VTOK[:, 2:4, :], ps[7].rearrange("q (b c) -> q b c", b=2), eng="act")

        def delta(bank, blk, cc):
            dl = bank[:, 0:128]
            for h2 in range(2):
                self.mm(dl[h2 * 64:(h2 + 1) * 64, :], KETOK[cc * 64:(cc + 1) * 64, blk, h2 * 64:(h2 + 1) * 64],
                        VTOK[cc * 64:(cc + 1) * 64, blk, h2 * 128:(h2 + 1) * 128], start=True, stop=True)
            return dl

        def st_out(name, S32, seq):
            self.dma(d[name][seq, 2 * p:2 * p + 2].rearrange("h k v -> (h k) v"), S32, "so_%s%d" % (name, seq % 2),
                     final=True)

        def deltas(banks):
            dls = []
            for c in range(8):
                dl = banks[c % 2][:, (c // 2) * 128:(c // 2 + 1) * 128]
                blk, cc = c // 2, c % 2
                for h2 in range(2):
                    self.mm(dl[h2 * 64:(h2 + 1) * 64, :], KETOK[cc * 64:(cc + 1) * 64, blk, h2 * 64:(h2 + 1) * 64],
                            VTOK[cc * 64:(cc + 1) * 64, blk, h2 * 128:(h2 + 1) * 128], start=True, stop=True)
                dls.append(dl)
            return dls

        for ti, tile in enumerate(tiles):
            proj_small(LFT, WLF, tile)
            proj(ps[1], WKp, tile)
            self.mm(ps[3], WGF[0:16, :], LFT[0:16, :])
            v_tok(tile)
            softplus_neg(ps[3], nb_f)
            self.scan(PSP, self.SCANM, SP)
            self.tt(c3(T3), c3(PSP), c3(PSP)[:, :, 63:64].broadcast_to([128, 8, 64]), ALU.subtract)
            self.act(E2, T3, AF.Exp, scale=1.0 / 16.0)
            self.tt(KE, ps[1], E2, ALU.mult)
            self.act(E1, PSP, AF.Exp, scale=-1.0 / 16.0)
            ke_tok(ktb)
            dls = deltas([ps[0], ps[4]])
            n0 = ti * 8
            if kind == "s" and ti == 0:
                self.dma(INIT, d["st_f"][2 * p:2 * p + 2].rearrange("h k v -> (h k) v"), "st_in")
                self.cp(SF16[:, 0, :], INIT, eng="act")
            for c in range(8):
                if kind == "s":
                    s_in = INIT if (ti == 0 and c == 0) else SALL[:, (c - 1) % 8, :]
                else:
                    s_in = ZERO if c % 4 == 0 else SALL[:, c - 1, :]
                self.stt(SALL[:, c, :], s_in, E1[:, c * 64 + 63:c * 64 + 64], dls[c], ALU.mult, ALU.add)
            if kind == "s":
                self.cp(SF16[:, n0 + 1:n0 + 9, :], SALL, eng="act")
            else:
                self.cp(SF16[:, 1:8, :], SALL[:, 0:7, :], eng="act")
                self.memset(SF16[:, 0, :], 0.0)
                self.memset(SF16[:, 4, :], 0.0)
                st_out("o_sf", SALL[:, 3, :], (tile - 4) * 2)
                st_out("o_sf", SALL[:, 7, :], (tile - 4) * 2 + 1)
        for ti in range(len(tiles) - 1, -1, -1):
            tile = tiles[ti]
            w = self.who(tile)
            proj(ps[0], WQp, tile)
            proj(ps[1], WKp, tile)
            proj_small(LFT, WLF, tile)
            self.mm(ps[3], WGF[0:16, :], LFT[0:16, :])
            proj_small(LBT, WLB, tile)
            v_tok(tile)
            softplus_neg(ps[3], nb_f)
            self.scan(PSP, self.SCANM, SP)
            self.act(E1, PSP, AF.Exp, scale=-1.0 / 16.0)
            self.act(E2, PSP, AF.Exp, scale=1.0 / 16.0)
            self.stt(QDF, ps[0], 0.125, E1, ALU.mult, ALU.mult)
            self.tt(KIF, ps[1], E2, ALU.mult)
            self.mm(ps[3], WGB[0:16, :], LBT[0:16, :])
            softplus_neg(ps[3], nb_b)
            self.scan(PSP, self.SCANM, SP)
            self.tt(T3, SP, PSP, ALU.subtract)
            self.tt(c3(SP), c3(T3), c3(PSP)[:, :, 63:64].broadcast_to([128, 8, 64]), ALU.add)
            self.act(E1, SP, AF.Exp, scale=-1.0 / 16.0)
            self.act(E2, SP, AF.Exp, scale=1.0 / 16.0)
            self.stt(QDB, ps[0], 0.125, E1, ALU.mult, ALU.mult)
            self.tt(KIB, ps[1], E2, ALU.mult)
            self.act(T3, T3, AF.Exp, scale=1.0 / 16.0)
            self.tt(KE, ps[1], T3, ALU.mult)
            ke_tok(ktb3)
            dls = deltas([ps[0], ps[1]])
            last_t = (ti == len(tiles) - 1)
            if kind == "s":
                if last_t:
                    self.dma(INIT, d["st_b"][2 * p:2 * p + 2].rearrange("h k v -> (h k) v"), "st_in")
                    s7 = INIT
                else:
                    s7 = SALL[:, 0, :]
                self.cp(SB16[:, 7, :], s7, eng="act")
            for c in range(7, -1, -1):
                if kind == "s":
                    s_in = s7 if c == 7 else SALL[:, c + 1, :]
                else:
                    s_in = ZERO if c % 4 == 3 else SALL[:, c + 1, :]
                self.stt(SALL[:, c, :], s_in, E1[:, c * 64:c * 64 + 1], dls[c], ALU.mult, ALU.add)
            self.cp(SB16[:, 0:7, :], SALL[:, 1:8, :], eng="act")
            if kind == "p":
                self.memset(SB16[:, 3, :], 0.0)
                self.memset(SB16[:, 7, :], 0.0)
                st_out("o_sb", SALL[:, 0, :], (tile - 4) * 2)
                st_out("o_sb", SALL[:, 4, :], (tile - 4) * 2 + 1)
            for blk in range(3, -1, -1):
                bs = slice(blk * 128, (blk + 1) * 128)
                for h2 in range(2):
                    hs = slice(h2 * 64, (h2 + 1) * 64)
                    at = ps[2 + self.rot("atb", 2)]
                    self.mm(at[:, 0:128], KIF[hs, bs], QDF[hs, bs])
                    self.mm(at[:, 128:256], KIB[hs, bs], QDB[hs, bs])
                    sl = self.rot("afs", 2)
                    self.tt(ATF[:, sl, :], at[:, 0:128], self.MF, ALU.mult)
                    self.tt(ATB[:, sl, :], at[:, 128:256], self.MB, ALU.mult)
                    ot = ps[4 + h2]
                    vl = VTOK[:, blk, h2 * 128:(h2 + 1) * 128]
                    self.mm(ot[:, bs], vl, ATF[:, sl, :], start=True, stop=False)
                    self.mm(ot[:, bs], vl, ATB[:, sl, :], start=False, stop=False)
                    for cc in range(2):
                        c = blk * 2 + cc
                        n = ti * 8 + c
                        cs = slice(c * 64, (c + 1) * 64)
                        self.mm(ot[:, cs], SF16[hs, n, :], QDF[hs, cs], start=False, stop=False)
                        self.mm(ot[:, cs], SB16[hs, c, :], QDB[hs, cs], start=False, stop=(cc == 1))
            for h2 in range(2):
                og = ps[6 + h2]
                for kc in range(NCH):
                    self.mm(og, WGp[:, kc, h2 * 128:(h2 + 1) * 128], self.HT[:, tile, kc, :],
                            start=(kc == 0), stop=(kc == NCH - 1))
                ot = ps[4 + h2]
                self.act(SQ16, ot, AF.Square)
                self.mm(ps[2], self.ONES_DV, SQ16)
                self.rsqrt(SP, ps[2])
                self.stt(T3, ot, self.GHEAD[:, 0:1], SP, ALU.mult, ALU.mult)
                self.sigmoid(E1, og)
                self.tt(E2, og, E1, ALU.mult)
                self.tt(GL[:, h2, :], T3, E2, ALU.mult)
            for dc in range(NCH):
                ob = ps[self.rot("opb1", 2)]
                for h2 in range(2):
                    self.mm(ob, WOp[:, h2, dc * 128:(dc + 1) * 128], GL[:, h2, :], start=(h2 == 0), stop=(h2 == 1))
                x = self.XT[:, tile, dc, :]
                self.stt(x, ob, self.GH[:, 0, 1, w, dc:dc + 1], x, ALU.mult, ALU.add)
            if p == 1 and kind == "p" and self.cfg.get("early_ada", True):
                self.adanorm(tile, 0, 2)
                self.prenormed.add(tile)

    def mixer_l1(self):
        d = self.dram
        l, s = 1, 1
        ntiles = self.cfg.get("ntiles", NT)
        for tile in range(ntiles):
            self.adanorm(tile, l, s)
        WA = self.WA_off
        WQ = self.V(WA, BF16, [NCH, 512])
        WK = self.V(WA + 8192, BF16, [NCH, 128])
        WV = self.V(WA + 10240, BF16, [NCH, 128])
        WO = self.V(WA + 12288, BF16, [4, D])
        P = self.P1_off
        KT = self.V(P, BF16, [2048]); P += 4096
        VT = self.V(P, BF16, [16, 128]); P += 4096
        QT = self.V(P, BF16, [4, TT]); P += 4096
        ROPE = self.V(P, F32, [2, TT]); P += 4096
        CKT = self.V(WA + 20480, BF16, [2, 256])
        CV = self.V(WA + 21504, BF16, [2, 256])
        PT = [self.V(P + i * 1024, BF16, [TT]) for i in range(4)]; P += 4096
        OB = self.V(P, BF16, [4, TT]); P += 4096
        DENR = self.V(P, F32, [TT]); P += 2048
        KVO = self.V(P, F32, [4, 2, 128]); P += 4096
        QB = self.V(WA + 22528, BF16, [TT])
        assert P <= self.arena_bytes, P
        for p in range(2):
            for e in range(2):
                self.dma(self.SINKE[e * 64:(e + 1) * 64, p, :],
                         d["sink"][2 * p + e:2 * p + e + 1, :].partition_broadcast(64), "sink%d%d" % (p, e))
        self.act(self.SINKE, self.SINKE, AF.Exp)
        has_sample = ntiles >= 4
        groups = []
        if has_sample:
            groups.append(("s", [0, 1, 2, 3]))
        for t in range(4, ntiles):
            groups.append(("p", [t]))
        for p in range(2):
            wsrc = d["l1_w_in"].rearrange("(kc q) n -> q kc n", q=128)
            self.dma(WK, wsrc[:, :, 1024 + p * 128:1024 + (p + 1) * 128], "l1wk", eng="pool")
            self.dma(WV, wsrc[:, :, 1280 + p * 128:1280 + (p + 1) * 128], "l1wv", eng="pool")
            if has_sample:
                self.dma(CV, d["cv"].rearrange("(b t) c -> t b c", t=128), "cvf", eng="pool")
            wq_src = wsrc[:, :, p * 512:(p + 1) * 512].rearrange("q k (e g dd) -> q k e g dd", e=2, g=4)
            wq_dst = WQ.rearrange("q k (g e dd) -> q k g e dd", g=4, e=2)
            for g in range(4):
                for e in range(2):
                    self.dma(wq_dst[:, :, g, e, :], wq_src[:, :, e, g, :], "l1wq%d%d" % (g, e), eng="pool")
            wo = d["l1_w_out"].rearrange("(pp e g dd) n -> pp e dd g n", pp=2, e=2, g=4)
            for e in range(2):
                self.dma(WO[e * 64:(e + 1) * 64, :, :], wo[p, e], "l1wo%d" % e, eng="pool")
            if has_sample:
                ckf = self.V(self.P1_off + 12288, F32, [2, 128])
                self.dma(ckf, d["ck"].rearrange("(b t) c -> t b c", t=128)[:, :, p * 128:(p + 1) * 128], "ckf")
                for b in range(2):
                    ps = self.ps[6]
                    self.tr(ps[:, b * 128:(b + 1) * 128], ckf[:, b, :], self.IDF)
                self.cp(CKT[:, p, :], self.ps[6][:, 0:256])
            L1S = self.cfg.get("l1stop", 99)
            if L1S <= 1:
                return
            for kind, tiles in groups:
                w = 0 if kind == "s" else 1
                for ti, tile in enumerate(tiles):
                    if kind == "s":
                        self.dma(ROPE, d["rope"][:, :, tile * TT:(tile + 1) * TT].rearrange("c q t -> q c t"),
                                 "rope")
                    kps = self.ps[6]
                    for kc in range(NCH):
                        self.mm(kps, WK[:, kc, :], self.HT[:, tile, kc, :], start=(kc == 0), stop=(kc == NCH - 1))
                    kdst = KT[:, ti * TT:(ti + 1) * TT]
                    P1S = self.cfg.get("p1", "")
                    if P1S == "k":
                        self.cp(kdst, kps, eng="act")
                        return
                    if kind == "s":
                        self.cp(QB, kps, eng="act")
                        rps = self.ps[7]
                        self.mm(rps, self.ROT, QB)
                        self.tt(DENR, kps, ROPE[:, 0, :], ALU.mult)
                        self.tt(KVO.rearrange("q a b c -> q (a b c)")[:, 0:TT], rps, ROPE[:, 1, :], ALU.mult)
                        self.tt(kdst, DENR, KVO.rearrange("q a b c -> q (a b c)")[:, 0:TT], ALU.add)
                    else:
                        self.cp(kdst, kps, eng="act")
                    if P1S == "rope":
                        return
                    vps = self.ps[0]
                    for blk in range(4):
                        for kc in range(NCH):
                            self.mm(vps[:, blk * 128:(blk + 1) * 128], self.HT[:, tile, kc, blk * 128:(blk + 1) * 128],
                                    WV[:, kc, :], start=(kc == 0), stop=(kc == NCH - 1))
                    if P1S == "v1":
                        return
                    self.cp(VT[:, ti * 4:(ti + 1) * 4, :], vps.rearrange("q (b c) -> q b c", b=4))
                    if P1S == "v2" and ti + 1 >= self.cfg.get("p1n", 1):
                        return
                    if kind == "p":
                        self.cp(KVO[:, :, 1, :], vps.rearrange("q (b c) -> q b c", b=4), eng="act")
                        k2 = self.ps[1]
                        for blk in range(4):
                            for kc in range(NCH):
                                self.mm(k2[:, blk * 128:(blk + 1) * 128],
                                        self.HT[:, tile, kc, blk * 128:(blk + 1) * 128],
                                        WK[:, kc, :], start=(kc == 0), stop=(kc == NCH - 1))
                        self.cp(KVO[:, :, 0, :], k2.rearrange("q (b c) -> q b c", b=4))
                        t0 = (tile - 4) * TT
                        self.dma(d["o_kc"][t0:t0 + TT, p * 128:(p + 1) * 128].rearrange("(b t) c -> t b c", t=128),
                                 KVO[:, :, 0, :], "okc", final=True)
                        self.dma(d["o_vc"][t0:t0 + TT, p * 128:(p + 1) * 128].rearrange("(b t) c -> t b c", t=128),
                                 KVO[:, :, 1, :], "ovc", final=True)
                if L1S <= 2:
                    return
                for ti, tile in enumerate(tiles):
                    if kind == "s":
                        self.dma(ROPE, d["rope"][:, :, tile * TT:(tile + 1) * TT].rearrange("c q t -> q c t"),
                                 "rope")
                    if self.cfg.get("p2", "") == "r":
                        return
                    for g in range(4):
                        qps = self.ps[(g % 2) * 2]
                        for kc in range(NCH):
                            self.mm(qps, WQ[:, kc, g * 128:(g + 1) * 128], self.HT[:, tile, kc, :],
                                    start=(kc == 0), stop=(kc == NCH - 1))
                        P2S = self.cfg.get("p2", "")
                        if P2S == "q0":
                            return
                        if P2S == "q1":
                            self.act(QT[:, g, :], qps, AF.Identity, scale=0.125)
                            return
                        if kind == "s":
                            self.cp(QB, qps, eng="act")
                            rps = self.ps[(g % 2) * 2 + 1]
                            self.mm(rps, self.ROT, QB)
                            t2 = KVO.rearrange("q a b c -> q (a b c)")[:, 0:TT]
                            self.tt(DENR, qps, ROPE[:, 0, :], ALU.mult)
                            self.tt(t2, rps, ROPE[:, 1, :], ALU.mult)
                            self.tt(DENR, DENR, t2, ALU.add)
                            self.act(QT[:, g, :], DENR, AF.Identity, scale=0.125)
                            if P2S == "q2":
                                return
                        else:
                            self.act(QT[:, g, :], qps, AF.Identity, scale=0.125)
                    if L1S <= 3:
                        return
                    steps = []
                    for qb in range(4):
                        if kind == "s":
                            i = ti * 4 + qb
                            kbs = [("c", 0, None), ("c", 1, None)]
                            if i - 1 >= 0:
                                kbs.append(("l", i - 1, self.MNLO))
                            kbs.append(("l", i, None))
                            if i + 1 <= 15:
                                kbs.append(("l", i + 1, self.MNHI))
                        else:
                            sq = qb // 2
                            kbs = [("l", sq * 2, None), ("l", sq * 2 + 1, None)]
                        for ki, kb_ in enumerate(kbs):
                            steps.append((qb, ki, len(kbs), kb_))

                    def scores(stp):
                        qb, ki, nk, (src, kb, mask) = stp
                        qsl = slice(qb * 128, (qb + 1) * 128)
                        sts, pts = [], []
                        for e in range(2):
                            hs = slice(e * 64, (e + 1) * 64)
                            st = self.ps[self.rot("stbank", 4)]
                            kl = CKT[hs, p, kb * 128:(kb + 1) * 128] if src == "c" else KT[hs, kb * 128:(kb + 1) * 128]
                            self.mm(st, kl, QT[hs, :, qsl], start=True, stop=(mask is None))
                            sts.append(st)
                        if mask is not None:
                            for e in range(2):
                                self.mm(sts[e], self.IDB, mask, start=False, stop=True)
                        for e in range(2):
                            pt = PT[self.rot("pt", 4)]
                            self.act(pt, sts[e], AF.Exp)
                            pts.append(pt)
                        return pts

                    def pv(stp, pts):
                        qb, ki, nk, (src, kb, mask) = stp
                        qsl = slice(qb * 128, (qb + 1) * 128)
                        par = qb % 2
                        OT = self.ps[4 + 2 * par]
                        DEN = self.ps[5 + 2 * par]
                        first, last = (ki == 0), (ki == nk - 1)
                        for e in range(2):
                            hs = slice(e * 64, (e + 1) * 64)
                            if src == "c":
                                vl = CV[:, kb, (2 * p + e) * 64:(2 * p + e + 1) * 64]
                            else:
                                vl = VT[:, kb, e * 64:(e + 1) * 64]
                            self.mm(OT[hs, :], vl, pts[e], start=first, stop=last)
                        for e in range(2):
                            hs = slice(e * 64, (e + 1) * 64)
                            self.mm(DEN[hs, :], self.ONE1[:, 0:64], pts[e], start=first, stop=last)
                        if last:
                            self.tt(DENR.rearrange("q (g t) -> q g t", g=4), DEN.rearrange("q (g t) -> q g t", g=4),
                                    self.SINKE[:, p, :].unsqueeze(2).broadcast_to([128, 4, 128]), ALU.add)
                            self.S.add("dve", lambda e_, o=DENR: e_.reciprocal(o, o), outs=[DENR], ins=[DENR])
                            self.tt(OB[:, :, qsl], OT.rearrange("q (g t) -> q g t", g=4),
                                    DENR.rearrange("q (g t) -> q g t", g=4), ALU.mult)

                    prev = None
                    for stp in steps:
                        pts = scores(stp)
                        if prev is not None:
                            pv(*prev)
                        prev = (stp, pts)
                    pv(*prev)
                    if L1S <= 4:
                        return
                    for dc in range(NCH):
                        o = self.ps[self.rot("opbank", 4)]
                        for g in range(4):
                            self.mm(o, WO[:, g, dc * 128:(dc + 1) * 128], OB[:, g, :], start=(g == 0), stop=(g == 3))
                        x = self.XT[:, tile, dc, :]
                        self.stt(x, o, self.GH[:, l, s, w, dc:dc + 1], x, ALU.mult, ALU.add)
                    if p == 1 and self.cfg.get("early_ada", True):
                        self.adanorm(tile, 1, 2)
                        self.prenormed.add(tile)


_CACHE = {}


def _prep_inputs(inp, consts):
    f = lambda a: np.ascontiguousarray(np.asarray(a, dtype=np.float32))
    xs = f(inp["x_sample"])
    xp = f(inp["x_prompt"])
    shared = {
        "mod_w": f(inp["mod_w"]), "ffn_w1": f(inp["ffn_w1"]), "ffn_w3": f(inp["ffn_w3"]),
        "ffn_w2": f(inp["ffn_w2"]), "l0_w_in": f(inp["l0_w_in"]), "l0_w_gf": f(inp["l0_w_gf"]),
        "l0_w_gb": f(inp["l0_w_gb"]), "l0_w_out": f(inp["l0_w_out"]), "l1_w_in": f(inp["l1_w_in"]),
        "l1_w_out": f(inp["l1_w_out"]), "sink": f(inp["l1_sink"]),
        "modb": f(inp["mod_b"]).reshape(144, 128),
        "cb": consts["cb"], "cf": consts["cf"], "rope": consts["rope"], "dftc": consts["dftc"],
        "dft2k": consts["dft2k"], "dft256": consts["dft256"],
    }
    c = f(inp["c"])
    cctx = f(inp["c_ctx"])
    tail = np.concatenate([
        f(inp["norm_g"]).reshape(48, 128), f(inp["final_g"]).reshape(8, 128),
        f(inp["l0_g_head"]).reshape(1, 128), f(inp["l0_b_gf"]).reshape(2, 128),
        f(inp["l0_b_gb"]).reshape(2, 128)], axis=0)
    maps = []
    for b in range(8):
        m = dict(shared)
        m["xin"] = np.ascontiguousarray(np.concatenate([xs[b], xp[4 * b:4 * b + 4].reshape(1024, D)], axis=0))
        m["small1"] = np.ascontiguousarray(np.concatenate([c[b].reshape(8, 128), cctx.reshape(8, 128), tail], 0))
        m["st_f"] = f(inp["state_l0_gla_fwd"])[b]
        m["st_b"] = f(inp["state_l0_gla_bwd"])[b]
        m["ck"] = f(inp["cache_l1_k"])[b].reshape(256, 256)
        m["cv"] = f(inp["cache_l1_v"])[b].reshape(256, 256)
        maps.append(m)
    return maps


def run(inp, cfg=None, trace=False):
    key = repr(sorted((cfg or {}).items()))
    if "consts" not in _CACHE:
        _CACHE["consts"] = _consts()
    consts = _CACHE["consts"]
    B = Builder(cfg)
    nc = B.build()
    maps = _prep_inputs(inp, consts)
    maps = [{k: m[k] for k in B.in_names} for m in maps]
    ncores = (cfg or {}).get("cores", 8)
    c0 = (cfg or {}).get("core0", 0)
    res = run_bass_kernel_spmd(nc, maps[:ncores], core_ids=list(range(c0, c0 + ncores)), trace=trace)
    return res


def kernel(**inp):
    res = run(inp)
    r = res.results
    y = np.stack([r[b]["y"] for b in range(8)], 0)
    y_sample = np.ascontiguousarray(y[:, :2048, :])
    y_prompt = np.ascontiguousarray(y[:, 2048:, :].reshape(32, 256, D))
    sf = np.concatenate([r[b]["o_sf"] for b in range(8)], 0).astype(np.float32)
    sb = np.concatenate([r[b]["o_sb"] for b in range(8)], 0).astype(np.float32)
    kc = np.concatenate([r[b]["o_kc"] for b in range(8)], 0).reshape(32, 256, 4, 64).astype(np.float32)
    vc = np.concatenate([r[b]["o_vc"] for b in range(8)], 0).reshape(32, 256, 4, 64).astype(np.float32)
    return (y_prompt.astype(np.float32), y_sample.astype(np.float32), sf, sb, kc, vc)
```

```python
import numpy as np
import ml_dtypes
from contextlib import ExitStack

import concourse.bass as bass
import concourse.mybir as mybir
from concourse.bass_utils import run_bass_kernel_spmd

F32 = mybir.dt.float32
BF16 = mybir.dt.bfloat16
AF = mybir.ActivationFunctionType
ALU = mybir.AluOpType

D = 1024
NCH = 8
TT = 512
NT = 6
FFN = 2816
NFG = 11
EPS = 1e-6
NTOK = 3072
L0_IN = 2080
L1_IN = 1536

_ESZ = {F32: 4, BF16: 2}
PAGE = 256
POOL_INFLIGHT = 3


class Op:
    __slots__ = ("eng", "fn", "deps", "idx", "sig", "sig_idx", "is_dma", "dkey", "dcount")


class Sched:
    ENGS = ("pe", "act", "dve", "pool", "sp")

    def __init__(self, tracked):
        self.q = {e: [] for e in self.ENGS}
        self.lastw = {}
        self.rd = {}
        self.dma_count = {}
        self.tracked = tracked
        self._pcache = {}
        self.final_dma = []
        self.pool_dmas = []

    def _pages(self, ap):
        t = ap.tensor
        name = t.name
        if name not in self.tracked:
            return ()
        dims = tuple((int(a), int(b)) for a, b in ap.ap)
        key = (name, int(ap.offset), dims, str(ap.dtype))
        r = self._pcache.get(key)
        if r is not None:
            return r
        esz = _ESZ[ap.dtype]
        row = int(t.shape[1])
        off = int(ap.offset)
        p0 = off // row
        f0 = off % row
        pcnt = dims[0][1]
        halves = sorted(set([(p0) // 64, (p0 + pcnt - 1) // 64]))
        starts = np.zeros(1, dtype=np.int64) + f0
        fd = dims[1:]
        if len(fd) == 0:
            run = 1
        else:
            for (st, cn) in fd[:-1]:
                if st == 0 or cn == 1:
                    continue
                starts = (starts[:, None] + (np.arange(cn, dtype=np.int64) * st)[None, :]).reshape(-1)
            st, cn = fd[-1]
            run = (cn - 1) * abs(st) + 1
        pages = set()
        for s in starts.tolist():
            b0 = (s * esz) // PAGE
            b1 = ((s + run) * esz - 1) // PAGE
            for pg in range(b0, b1 + 1):
                pages.add(pg)
        r = tuple((name, h, pg) for h in halves for pg in sorted(pages))
        self._pcache[key] = r
        return r

    def add(self, eng, fn, outs=(), ins=(), dma_key=None, final=False):
        op = Op()
        op.eng = eng
        op.fn = fn
        op.sig = False
        op.sig_idx = 0
        op.is_dma = dma_key is not None
        op.dkey = dma_key
        op.dcount = 0
        if op.is_dma:
            c = self.dma_count.get(dma_key, 0) + 1
            self.dma_count[dma_key] = c
            op.dcount = c
            if final:
                self.final_dma.append(op)
        op.idx = len(self.q[eng])
        deps = {}

        def dep(d):
            if d is op:
                return
            if d.is_dma:
                k = ("dma", d.dkey)
                if k not in deps or deps[k].dcount < d.dcount:
                    deps[k] = d
            else:
                if eng == "pe" and d.eng == "pe" and not op.is_dma:
                    return
                k = ("eng", d.eng)
                if k not in deps or deps[k].idx < d.idx:
                    deps[k] = d

        in_pages = []
        out_pages = []
        for ap in ins:
            pgs = self._pages(ap)
            if pgs and pgs[0][0].startswith("ps"):
                out_pages.extend(sorted(set((n, h, 0) for (n, h, _) in pgs)))
            else:
                in_pages.extend(pgs)
        for ap in outs:
            pgs = self._pages(ap)
            if pgs and pgs[0][0].startswith("ps"):
                out_pages.extend(sorted(set((n, h, 0) for (n, h, _) in pgs)))
            else:
                out_pages.extend(pgs)
        lastw = self.lastw
        rd = self.rd
        for pg in in_pages:
            w = lastw.get(pg)
            if w is not None:
                dep(w)
        for pg in out_pages:
            w = lastw.get(pg)
            if w is not None:
                dep(w)
            rs = rd.get(pg)
            if rs:
                for r in rs.values():
                    dep(r)
        for pg in out_pages:
            lastw[pg] = op
            rd[pg] = {}
        key_r = ("dma", op.dkey, op.dcount) if op.is_dma else op.eng
        for pg in in_pages:
            rs = rd.get(pg)
            if rs is None:
                rs = {}
                rd[pg] = rs
            rs[key_r if not op.is_dma else key_r] = op
        if op.is_dma and eng == "pool":
            pl = self.pool_dmas
            if len(pl) >= POOL_INFLIGHT:
                dep(pl[-POOL_INFLIGHT])
            pl.append(op)
        op.deps = list(deps.values())
        for d in op.deps:
            if not d.is_dma:
                d.sig = True
        self.q[eng].append(op)
        return op

    def emit(self, nc):
        for e in self.ENGS:
            n = 0
            for op in self.q[e]:
                if op.sig and not op.is_dma:
                    n += 1
                    op.sig_idx = n
        with ExitStack() as st:
            esem = {e: st.enter_context(nc.semaphore("sem_" + e)) for e in ("pe", "act", "dve", "pool")}
            dsem = {k: st.enter_context(nc.semaphore("dsem_%d" % i)) for i, k in enumerate(self.dma_count)}
            block = st.enter_context(nc.Block())

            def run(eh, e):
                waited = {}
                for op in self.q[e]:
                    for d in op.deps:
                        if d.is_dma:
                            s, v = dsem[d.dkey], 16 * d.dcount
                        else:
                            s, v = esem[d.eng], d.sig_idx
                        k = id(s)
                        if waited.get(k, 0) < v:
                            eh.wait_ge(s, v)
                            waited[k] = v
                    ins = op.fn(eh)
                    if op.is_dma:
                        ins.then_inc(dsem[op.dkey], 16)
                    elif op.sig:
                        ins.then_inc(esem[e], 1)
                if e == "sp":
                    for k, c in self.dma_count.items():
                        eh.wait_ge(dsem[k], 16 * c)

            @block.tensor
            def _(eh):
                run(eh, "pe")

            @block.scalar
            def _(eh):
                run(eh, "act")

            @block.vector
            def _(eh):
                run(eh, "dve")

            @block.gpsimd
            def _(eh):
                run(eh, "pool")

            @block.sync
            def _(eh):
                run(eh, "sp")


def _bf(a):
    return np.asarray(a, dtype=np.float32).astype(ml_dtypes.bfloat16)


CB_ID, CB_ONES_D, CB_ONES_DV, CB_ONE1, CB_MF, CB_MB, CB_ROT, CB_MNLO, CB_MNHI = (
    0, 128, 256, 384, 512, 640, 768, 896, 1408)
CB_N = 1920


def _consts():
    i = np.arange(128)
    cb = np.zeros((128, CB_N), np.float32)
    cb[:, CB_ID:CB_ID + 128] = np.eye(128)
    cb[:, CB_ONES_D:CB_ONES_D + 128] = 1.0 / 1024.0
    cb[:, CB_ONES_DV:CB_ONES_DV + 128] = 1.0 / 128.0
    cb[:, CB_ONE1:CB_ONE1 + 128] = 1.0
    s = i[:, None]
    t = i[None, :]
    same = (s // 64) == (t // 64)
    cb[:, CB_MF:CB_MF + 128] = (same & (s <= t)).astype(np.float32)
    cb[:, CB_MB:CB_MB + 128] = (same & (s >= t)).astype(np.float32)
    R = np.zeros((64, 64), np.float32)
    for m in range(16):
        R[m, 16 + m] = -1.0
        R[16 + m, m] = 1.0
        R[32 + m, 48 + m] = -1.0
        R[48 + m, 32 + m] = 1.0
    RT = np.zeros((128, 128), np.float32)
    RT[:64, :64] = R.T
    RT[64:, 64:] = R.T
    cb[:, CB_ROT:CB_ROT + 128] = RT
    lo = np.where(s >= t, 0.0, -30000.0).astype(np.float32)
    hi = np.where(s <= t, 0.0, -30000.0).astype(np.float32)
    cb[:, CB_MNLO:CB_MNLO + 512] = np.tile(lo, (1, 4))
    cb[:, CB_MNHI:CB_MNHI + 512] = np.tile(hi, (1, 4))
    cf = np.zeros((128, 128 + 512), np.float32)
    cf[:, :128] = np.eye(128)
    m = np.ones(512, np.float32)
    m[::64] = 0.0
    cf[:, 128:] = m[None, :]
    tt = np.arange(2048)
    row = (tt // 64).astype(np.float32)
    col = (tt % 64).astype(np.float32)
    inv = (np.float32(10000.0) ** (-np.arange(16, dtype=np.float32) / np.float32(16))).astype(np.float32)
    ar = row[:, None] * inv[None, :]
    ac = col[:, None] * inv[None, :]
    ang = np.concatenate([ar, ar, ac, ac], axis=-1).astype(np.float32)
    cos = np.cos(ang).astype(np.float32).T
    sin = np.sin(ang).astype(np.float32).T
    rope = np.stack([np.concatenate([cos, cos], 0), np.concatenate([sin, sin], 0)], 0)
    def dft(n):
        k = np.arange(n)
        kk = (k[:, None] * k[None, :]) % n
        a = 2.0 * np.pi * kk.astype(np.float64) / n
        return np.cos(a) / np.sqrt(n), np.sin(a) / np.sqrt(n)
    c128, s128 = dft(128)
    dftc = np.stack([c128, -s128], 0)
    c2k, s2k = dft(2048)
    c256, s256 = dft(256)
    return dict(cb=_bf(cb), cf=cf, rope=np.ascontiguousarray(rope), dftc=_bf(dftc),
                dft2k=_bf(np.stack([c2k, s2k], 0)), dft256=_bf(np.stack([c256, s256], 0)))


class Builder:
    def __init__(self, cfg=None):
        self.cfg = cfg or {}
        self.nc = bass.Bass("TRN2", target_bir_lowering=False)
        self.dram = {}
        self.in_names = []
        self.out_names = []

    def din(self, name, shape, dt=F32):
        self.dram[name] = self.nc.dram_tensor(name, list(shape), dt, kind="ExternalInput").ap()
        self.in_names.append(name)
        return self.dram[name]

    def dout(self, name, shape, dt=F32):
        self.dram[name] = self.nc.dram_tensor(name, list(shape), dt, kind="ExternalOutput").ap()
        self.out_names.append(name)
        return self.dram[name]

    def V(self, off, dt, shape):
        esz = _ESZ[dt]
        n = int(np.prod(shape))
        assert off % 4 == 0 and (n * esz) % 4 == 0
        assert off + n * esz <= self.arena_bytes, (off, n * esz, self.arena_bytes)
        a = self.arena[:, off // 4:(off + n * esz) // 4]
        if dt != F32:
            a = a.bitcast(dt)
        if len(shape) > 1:
            names = ["d%d" % i for i in range(len(shape))]
            kw = {names[i]: int(shape[i]) for i in range(len(shape) - 1)}
            a = a.rearrange("p (%s) -> p %s" % (" ".join(names), " ".join(names)), **kw)
        return a

    def build(self):
        nc = self.nc
        cfg = self.cfg
        d = self.dram
        self.din("xin", [NTOK, D])
        self.din("small1", [77, 128])
        self.din("modb", [144, 128])
        self.din("st_f", [4, 64, 128])
        self.din("st_b", [4, 64, 128])
        self.din("ck", [256, 256])
        self.din("cv", [256, 256])
        self.din("sink", [4, 4])
        self.din("mod_w", [2, D, 9 * D])
        self.din("ffn_w1", [2, 2, D, FFN])
        self.din("ffn_w3", [2, 2, D, FFN])
        self.din("ffn_w2", [2, 2, FFN, D])
        self.din("l0_w_in", [D, L0_IN])
        self.din("l0_w_gf", [16, 256])
        self.din("l0_w_gb", [16, 256])
        self.din("l0_w_out", [D, D])
        self.din("l1_w_in", [D, L1_IN])
        self.din("l1_w_out", [D, D])
        self.din("cb", [128, CB_N], BF16)
        self.din("cf", [128, 640])
        self.din("rope", [2, 128, 2048])
        self.din("dftc", [2, 128, 128], BF16)
        self.din("dft2k", [2, 2048, 2048], BF16)
        self.din("dft256", [2, 256, 256], BF16)
        self.dout("y", [NTOK, D])
        self.dout("o_sf", [4, 4, 64, 128])
        self.dout("o_sb", [4, 4, 64, 128])
        self.dout("o_kc", [1024, 256])
        self.dout("o_vc", [1024, 256])

        self.arena_bytes = 212480
        with ExitStack() as st:
            self.arena = st.enter_context(nc.sbuf_tensor("arena", [128, self.arena_bytes // 4], F32))
            self.ps = [st.enter_context(nc.psum_tensor("ps%d" % i, [128, 512], F32))[:, :] for i in range(8)]
            tracked = {"arena"} | {"ps%d" % i for i in range(8)}
            self.S = Sched(tracked)
            self._layout()
            self._program()
            self.S.emit(nc)
        return nc

    def _layout(self):
        V = self.V
        o = 0
        self.XT = V(o, F32, [NT, NCH, TT]); o += NT * NCH * TT * 4
        self.HT_off = o
        self.HT = V(o, BF16, [NT, NCH, TT]); o += NT * NCH * TT * 2
        self.WA_off = o
        self.W1 = [V(o + s * 12288, BF16, [NCH, 256]) for s in range(2)]
        self.W3 = [V(o + s * 12288 + 4096, BF16, [NCH, 256]) for s in range(2)]
        self.W2 = [V(o + s * 12288 + 8192, BF16, [2, D]) for s in range(2)]
        o += 24576
        self.CONST_off = o
        c = o
        self.CB = V(c, BF16, [CB_N]); c += CB_N * 2
        self.CF = V(c, F32, [640]); c += 2560
        self.MODT = V(c, F32, [2, 72, 2]); c += 2 * 72 * 2 * 4
        self.SM1 = V(c, F32, [77]); c += 77 * 4 + 0
        self.AM = V(c, F32, [2, 3, 2, NCH]); c += 2 * 3 * 2 * NCH * 4
        self.GH = V(c, F32, [2, 3, 2, NCH]); c += 384
        self.SC = V(c, BF16, [2, NCH]); c += 32
        self.MBT = V(c, F32, [2, 72]); c += 576
        self.SINKE = V(c, F32, [2, 4]); c += 32
        self.NBG = V(c, F32, [2, 2]); c += 16
        assert c - o <= 9472, c - o
        o += 9472
        self.P1_off = o
        self.P1_size = self.arena_bytes - o
        p = o
        self.G = V(p, BF16, [2, 2, TT]); p += 4096
        self.SA = V(p, F32, [2, TT]); p += 4096
        self.SQ = V(p, BF16, [4, TT]); p += 4096
        self.RSTD = V(p, F32, [2, TT]); p += 4096
        self.TMP = V(p, F32, [2, TT]); p += 4096
        self.MODW = [V(p + s * 4096, BF16, [NCH, 256]) for s in range(2)]; p += 8192
        self.P1_ffn_end = p
        assert p <= self.arena_bytes, p
        self.cnt = {}
        self.prenormed = set()

    def rot(self, key, n):
        v = self.cnt.get(key, 0)
        self.cnt[key] = v + 1
        return v % n

    def mm(self, out, lhsT, rhs, start=True, stop=True):
        return self.S.add("pe", lambda e: e.matmul(out, lhsT, rhs, start=start, stop=stop),
                          outs=[out], ins=[lhsT, rhs])

    def tr(self, out, in_, ident):
        return self.S.add("pe", lambda e: e.transpose(out, in_, ident), outs=[out], ins=[in_, ident])

    def act(self, out, in_, func, bias=None, scale=None, eng="act"):
        ins = [in_]
        kw = {}
        if bias is not None:
            kw["bias"] = bias
            if not isinstance(bias, (int, float)):
                ins.append(bias)
        if scale is not None:
            kw["scale"] = scale
            if not isinstance(scale, (int, float)):
                ins.append(scale)
        return self.S.add("act", lambda e: e.activation(out, in_, func, **kw), outs=[out], ins=ins)

    def sigmoid(self, out, in_):
        self.act(out, in_, AF.Exp, scale=-1.0)
        self.act(out, out, AF.Ln, bias=1.0, scale=1.0)
        self.act(out, out, AF.Exp, scale=-1.0)

    def rsqrt(self, out, in_, eps=EPS):
        self.act(out, in_, AF.Ln, bias=eps, scale=1.0)
        self.act(out, out, AF.Exp, scale=-0.5)

    def ts(self, out, in0, s1, s2, op0, op1=None, eng="dve"):
        ins = [in0] + [s for s in (s1, s2) if s is not None and not isinstance(s, (int, float))]
        if op1 is None:
            return self.S.add(eng, lambda e: e.tensor_scalar(out, in0, s1, s2, op0), outs=[out], ins=ins)
        return self.S.add(eng, lambda e: e.tensor_scalar(out, in0, s1, s2, op0, op1), outs=[out], ins=ins)

    def stt(self, out, in0, scalar, in1, op0, op1, eng="dve"):
        ins = [in0, in1] + ([] if isinstance(scalar, (int, float)) else [scalar])
        return self.S.add(eng, lambda e: e.scalar_tensor_tensor(out, in0, scalar, in1, op0, op1),
                          outs=[out], ins=ins)

    def tt(self, out, in0, in1, op, eng="dve"):
        return self.S.add(eng, lambda e: e.tensor_tensor(out, in0, in1, op), outs=[out], ins=[in0, in1])

    def cp(self, out, in_, eng="dve"):
        if eng == "act":
            return self.S.add("act", lambda e: e.copy(out, in_), outs=[out], ins=[in_])
        return self.S.add(eng, lambda e: e.tensor_copy(out, in_), outs=[out], ins=[in_])

    def memset(self, out, val, eng="dve"):
        return self.S.add(eng, lambda e: e.memset(out, val), outs=[out], ins=[])

    def dma(self, out, in_, key, eng="sp", final=False):
        return self.S.add(eng, lambda e: e.dma_start(out=out, in_=in_), outs=[out], ins=[in_],
                          dma_key=key, final=final)

    def _program(self):
        cfg = self.cfg
        stop = cfg.get("stop", "")
        self.load_consts()
        self.mod_queue = [(l, pc) for l in range(2) for pc in range(36)]
        self.mod_lq = list(self.mod_queue)
        self.mod_loaded = []
        self.preloaded = False
        self.mod_load()
        self.mod_load()
        if stop == "consts":
            self.mod_queue = []
            return self.final_out()
        for _ in range(12):
            self.mod_piece()
        if stop == "mod":
            self.mod_queue = []
            return self.final_out()
        self.load_x()
        if stop == "loadx":
            self.mod_queue = []
            return self.final_out()
        for l in cfg.get("layer_list", list(range(cfg.get("layers", 2)))):
            self.mod_ensure(l, 0)
            self.ffn(l, 0)
            self.mod_ensure(l, 2)
            self.mod_ensure(l, 1)
            self.mod_drain()
            if cfg.get("mixers", True):
                if l == 0:
                    self.mixer_l0()
                else:
                    self.mixer_l1()
            self.ffn(l, 1)
        while self.mod_queue:
            self.mod_piece()
        self.final_out()

    def load_consts(self):
        d = self.dram
        CB, CF = self.CB, self.CF
        self.dma(CB, d["cb"], "c_cb")
        self.dma(CF, d["cf"], "c_cf")
        self.IDB = CB[:, CB_ID:CB_ID + 128]
        self.ONES_D = CB[:, CB_ONES_D:CB_ONES_D + 128]
        self.ONES_DV = CB[:, CB_ONES_DV:CB_ONES_DV + 128]
        self.ONE1 = CB[:, CB_ONE1:CB_ONE1 + 128]
        self.MF = CB[:, CB_MF:CB_MF + 128]
        self.MB = CB[:, CB_MB:CB_MB + 128]
        self.ROT = CB[:, CB_ROT:CB_ROT + 128]
        self.MNLO = CB[:, CB_MNLO:CB_MNLO + 512]
        self.MNHI = CB[:, CB_MNHI:CB_MNHI + 512]
        self.IDF = CF[:, 0:128]
        self.SCANM = CF[:, 128:640]
        P1 = self.P1_off
        s1 = self.V(P1, F32, [128])
        mb = self.V(P1 + 512, F32, [2, 128])
        self.dma(s1[0:77, :], d["small1"], "c_s1")
        self.dma(mb[0:72, :, :], d["modb"].rearrange("(l j) f -> j l f", l=2), "c_mb")
        ps = self.ps[7]
        self.tr(ps[:, 0:77], s1[0:77, :], self.IDF[0:77, 0:77])
        self.cp(self.SM1, ps[:, 0:77])
        for l in range(2):
            self.tr(ps[:, 128 + l * 72:128 + (l + 1) * 72], mb[0:72, l, :], self.IDF[0:72, 0:72])
        self.cp(self.MBT, ps[:, 128:272].rearrange("p (l j) -> p l j", l=2))
        self.NG = self.SM1[:, 16:64].rearrange("p (l s c) -> p l s c", l=2, s=3)
        self.FG = self.SM1[:, 64:72]
        self.GHEAD = self.SM1[:, 72:73]
        self.BGF = self.SM1[:, 73:75]
        self.BGB = self.SM1[:, 75:77]
        t0 = self.V(P1 + 2048, F32, [16])
        self.sigmoid(t0, self.SM1[:, 0:16])
        self.tt(self.SC.rearrange("p w c -> p (w c)"), self.SM1[:, 0:16], t0, ALU.mult)
        self.mod_ps_cols = 0

    def mod_load(self):
        if not self.mod_lq:
            return
        l, pc = self.mod_lq.pop(0)
        d = self.dram
        slot = self.rot("modw_l", 2)
        src = d["mod_w"][l].rearrange("(kc p) n -> p kc n", p=128)[:, :, pc * 256:(pc + 1) * 256]
        self.dma(self.MODW[slot], src, "modw%d" % slot, eng="pool")
        self.mod_loaded.append((l, pc, slot))

    def mod_drain(self):
        while self.mod_loaded:
            self.mod_piece(prefetch=False)

    def mod_piece(self, prefetch=True, defer=False):
        if not self.mod_queue:
            return
        if not self.mod_loaded:
            self.mod_load()
        l, pc, slot = self.mod_loaded.pop(0)
        assert (l, pc) == self.mod_queue.pop(0)
        W = self.MODW[slot]
        ps = self.ps[7]
        c0 = 384 + slot * 4
        for jj in range(2):
            o = ps[:, c0 + jj * 2:c0 + jj * 2 + 2]
            for kc in range(NCH):
                self.mm(o, W[:, kc, jj * 128:(jj + 1) * 128], self.SC[:, :, kc],
                        start=(kc == 0), stop=(kc == NCH - 1))
        if prefetch:
            self.mod_load()
        if defer:
            return lambda: self._mod_evac(l, pc, c0)
        self._mod_evac(l, pc, c0)

    def _mod_evac(self, l, pc, c0):
        ps = self.ps[7]
        j0 = pc * 2
        self.tt(self.MODT[:, l, j0:j0 + 2, :], ps[:, c0:c0 + 4].rearrange("p (j w) -> p j w", j=2),
                self.MBT[:, l, j0:j0 + 2].unsqueeze(2).broadcast_to([128, 2, 2]), ALU.add)
        if pc % 12 == 11:
            s = pc // 12
            for w in range(2):
                self.stt(self.AM[:, l, s, w, :], self.MODT[:, l, (3 * s + 1) * 8:(3 * s + 2) * 8, w], 1.0,
                         self.NG[:, l, s, :], ALU.add, ALU.mult)
                gsrc = self.MODT[:, l, (3 * s + 2) * 8:(3 * s + 3) * 8, w]
                if s == 1:
                    self.cp(self.GH[:, l, s, w, :], gsrc)
                else:
                    self.ts(self.GH[:, l, s, w, :], gsrc, 0.5, None, ALU.mult)

    def mod_ensure(self, l, sgrp):
        while self.mod_queue and self.mod_queue[0] <= (l, 12 * sgrp + 11):
            self.mod_piece()

    def shift(self, l, s, w, c):
        j = (3 * s) * 8 + c
        return self.MODT[:, l, j, w:w + 1]

    def load_x(self):
        d = self.dram
        P1 = self.P1_ffn_end
        stg = [self.V(self.HT_off + s * 4096, F32, [D]) for s in range(2)]
        for tb in range(NTOK // 128):
            tile, sub = tb // 4, tb % 4
            sl = stg[tb % 2]
            self.dma(sl, d["xin"][tb * 128:(tb + 1) * 128, :], "xin%d" % (tb % 2))
            for hb in range(2):
                ps = self.ps[(tb % 2) * 2 + hb]
                for cc in range(4):
                    c = hb * 4 + cc
                    self.tr(ps[:, cc * 128:(cc + 1) * 128], sl[:, c * 128:(c + 1) * 128], self.IDF)
                dst = self.XT[:, tile, hb * 4:(hb + 1) * 4, sub * 128:(sub + 1) * 128]
                src = ps[:, :].rearrange("p (c t) -> p c t", c=4)
                if hb == 0:
                    self.cp(dst, src, eng="act")
                else:
                    self.cp(dst, src, eng="dve")

    def who(self, tile):
        return 0 if tile < 4 else 1

    def adanorm(self, tile, l, s, dst=None):
        w = self.who(tile)
        ms = self.ps[7][:, 0:TT] if False else self.ps[7]
        msv = self.ps[7][:, 0:TT]
        for half in range(2):
            self.act(self.SQ, self.XT[:, tile, half * 4:(half + 1) * 4, :], AF.Square)
            for cc in range(4):
                c = half * 4 + cc
                self.mm(msv, self.ONES_D, self.SQ[:, cc, :], start=(c == 0), stop=(c == NCH - 1))
        r = self.RSTD[:, self.rot("rstd", 2), :]
        self.rsqrt(r, msv)
        for c in range(NCH):
            t = self.TMP[:, self.rot("tmp", 2), :]
            self.stt(t, self.XT[:, tile, c, :], self.AM[:, l, s, w, c:c + 1], r, ALU.mult, ALU.mult)
            out = self.HT[:, tile, c, :] if dst is None else dst[:, c, :]
            self.act(out, t, AF.Identity, bias=self.shift(l, s, w, c), scale=1.0)

    def ffn_load(self, l, hs, fg):
        d = self.dram
        slot = fg % 2
        w1 = d["ffn_w1"][l, hs].rearrange("(kc p) f -> p kc f", p=128)[:, :, fg * 256:(fg + 1) * 256]
        w3 = d["ffn_w3"][l, hs].rearrange("(kc p) f -> p kc f", p=128)[:, :, fg * 256:(fg + 1) * 256]
        w2 = d["ffn_w2"][l, hs][fg * 256:(fg + 1) * 256, :].rearrange("(fc p) n -> p fc n", p=128)
        sk = self.cfg.get("skip_load", 0)
        if sk in (0, 2):
            self.dma(self.W1[slot], w1, "w1_%d" % slot, eng="pool")
            self.dma(self.W3[slot], w3, "w3_%d" % slot, eng="pool")
        if sk in (0, 3):
            self.dma(self.W2[slot], w2, "w2_%d" % slot, eng="pool")

    def ffn(self, l, hs):
        s = 0 if hs == 0 else 2
        ntiles = self.cfg.get("ntiles", NT)
        while len(self.mod_loaded) < 2 and self.mod_lq:
            self.mod_load()
        if self.preloaded:
            self.preloaded = False
        else:
            self.ffn_load(l, hs, 0)
            self.ffn_load(l, hs, 1)
        for tile in range(ntiles):
            if tile in self.prenormed:
                continue
            self.adanorm(tile, l, s)
        self.prenormed = set()
        if self.cfg.get("fstop") == "adanorm":
            return
        pend = None
        step = 0

        def do_o(p):
            fg, tile, par = p
            slot = fg % 2
            w = self.who(tile)
            for dc in range(NCH):
                o = self.ps[4 + self.rot("obank", 4)]
                for fc in range(2):
                    self.mm(o, self.W2[slot][:, fc, dc * 128:(dc + 1) * 128], self.G[:, par, fc, :],
                            start=(fc == 0), stop=(fc == 1))
                x = self.XT[:, tile, dc, :]
                if dc in (0, 1, 2, 4, 6):
                    self.stt(x, o, self.GH[:, l, s, w, dc:dc + 1], x, ALU.mult, ALU.add)
                else:
                    t = self.TMP[:, self.rot("tmpo", 2), :]
                    self.act(t, o, AF.Identity, scale=self.GH[:, l, s, w, dc:dc + 1])
                    self.tt(x, x, t, ALU.add)

        nfg = self.cfg.get("nfg", NFG)
        for fg in range(nfg):
            slot = fg % 2
            for tile in range(ntiles):
                par = step % 2
                step += 1
                for fc in range(2):
                    a = self.ps[fc]
                    b = self.ps[2 + fc]
                    for kc in range(NCH):
                        self.mm(a, self.W1[slot][:, kc, fc * 128:(fc + 1) * 128], self.HT[:, tile, kc, :],
                                start=(kc == 0), stop=(kc == NCH - 1))
                    for kc in range(NCH):
                        self.mm(b, self.W3[slot][:, kc, fc * 128:(fc + 1) * 128], self.HT[:, tile, kc, :],
                                start=(kc == 0), stop=(kc == NCH - 1))
                    sa = self.SA[:, fc, :]
                    self.sigmoid(sa, a)
                    if fc == 0:
                        self.tt(sa, sa, a, ALU.mult)
                        self.tt(self.G[:, par, fc, :], sa, b, ALU.mult)
                if pend is not None:
                    do_o(pend)
                    if pend[1] == ntiles - 1 and pend[0] + 2 < nfg:
                        self.ffn_load(l, hs, pend[0] + 2)
                sa = self.SA[:, 1, :]
                self.tt(sa, sa, self.ps[1], ALU.mult)
                self.tt(self.G[:, par, 1, :], sa, self.ps[3], ALU.mult)
                pend = (fg, tile, par)
            evs = [self.mod_piece(defer=True) for _ in range(2)]
            for ev in evs:
                if ev is not None:
                    ev()
        if self.cfg.get("fstop") in ("ab", "sig", "g"):
            return
        do_o(pend)

    def final_out(self):
        d = self.dram
        ntiles = self.cfg.get("ntiles", NT)
        YT = self.V(self.HT_off, F32, [NCH, TT])
        stg = [self.V(self.HT_off + 16384 + s * 4096, F32, [D]) for s in range(2)]
        for tile in range(ntiles):
            msv = self.ps[7][:, 0:TT]
            for half in range(2):
                self.act(self.SQ, self.XT[:, tile, half * 4:(half + 1) * 4, :], AF.Square)
                for cc in range(4):
                    c = half * 4 + cc
                    self.mm(msv, self.ONES_D, self.SQ[:, cc, :], start=(c == 0), stop=(c == NCH - 1))
            r = self.RSTD[:, self.rot("rstd", 2), :]
            self.rsqrt(r, msv)
            for c in range(NCH):
                self.stt(YT[:, c, :], self.XT[:, tile, c, :], self.FG[:, c:c + 1], r, ALU.mult, ALU.mult)
            for sub in range(4):
                tb = tile * 4 + sub
                sl = stg[tb % 2]
                for hb in range(2):
                    ps = self.ps[(tb % 2) * 2 + hb]
                    for cc in range(4):
                        c = hb * 4 + cc
                        self.tr(ps[:, cc * 128:(cc + 1) * 128], YT[:, c, sub * 128:(sub + 1) * 128], self.IDF)
                    if hb == 0:
                        self.cp(sl[:, 0:512], ps[:, :], eng="act")
                    else:
                        self.cp(sl[:, 512:1024], ps[:, :], eng="dve")
                self.dma(d["y"][tb * 128:(tb + 1) * 128, :], sl, "yout%d" % (tb % 2), final=True)

    def scan(self, out, d0, d1):
        return self.S.add("dve", lambda e: e.tensor_tensor_scan(out, d0, d1, 0.0, ALU.mult, ALU.add),
                          outs=[out], ins=[d0, d1])

    def mixer_l0(self):
        ntiles = self.cfg.get("ntiles", NT)
        self.ts(self.NBG[:, 0, :], self.BGF, -1.0, None, ALU.mult)
        self.ts(self.NBG[:, 1, :], self.BGB, -1.0, None, ALU.mult)
        groups = []
        if ntiles >= 4:
            groups.append(("s", [0, 1, 2, 3], self.HT_off + 4 * 8192))
        for t in range(4, ntiles):
            groups.append(("p", [t], self.HT_off))
        for kind, tiles, scr in groups:
            for tile in tiles:
                self.adanorm(tile, 0, 1)
            if self.cfg.get("fnet", True):
                self.fnet(kind, tiles, scr)
            if self.cfg.get("gla", True):
                for p in range(2):
                    self.gla_pair(kind, tiles, scr, p)

    def fnet(self, kind, tiles, scr):
        d = self.dram
        WA, P1 = self.WA_off, self.P1_off
        WU = self.V(WA, BF16, [NCH, 512])
        WOF = self.V(WA + 8192, BF16, [4, D])
        DFTC = self.V(WA + 16384, BF16, [2, 128])
        self.dma(WU, d["l0_w_in"].rearrange("(kc q) n -> q kc n", q=128)[:, :, 1568:2080], "l0wu", eng="pool")
        self.dma(WOF, d["l0_w_out"][512:1024, :].rearrange("(g q) n -> q g n", q=128), "l0wof", eng="pool")
        self.dma(DFTC, d["dftc"].rearrange("cs c k -> c cs k"), "dftc")
        UT = self.V(scr, BF16, [16, 512])
        for ti, tile in enumerate(tiles):
            for blk in range(4):
                ps = self.ps[blk % 2]
                for kc in range(NCH):
                    self.mm(ps, self.HT[:, tile, kc, blk * 128:(blk + 1) * 128], WU[:, kc, :],
                            start=(kc == 0), stop=(kc == NCH - 1))
                self.cp(UT[:, ti * 4 + blk, :], ps, eng=("act" if blk % 2 else "dve"))
        DT = [self.V(P1 + i * 8192, BF16, [4, 2, 512]) for i in range(2)]
        ABT = self.V(P1 + 16384, BF16, [2, 4, 512])
        FN = self.V(P1 + 24576, BF16, [4, 512])
        DT256 = self.V(P1, BF16, [2, 2, 256])
        if kind == "p":
            for cs_ in range(2):
                self.dma(DT256[:, :, cs_, :], d["dft256"][cs_].rearrange("(tb t) tp -> t tb tp", t=128), "dt256_%d" % cs_)
        for j, tile in enumerate(tiles):
            if kind == "s":
                for qd in range(4):
                    slot = self.rot("dt", 2)
                    src = d["dft2k"][:, qd * 512:(qd + 1) * 512, j * 512:(j + 1) * 512]
                    for cs_ in range(2):
                        self.dma(DT[slot][:, :, cs_, :], src[cs_].rearrange("(tb t) tp -> t tb tp", t=128),
                                 "dt%d_%d" % (slot, cs_))
                    for tb4 in range(4):
                        tb = qd * 4 + tb4
                        for g in range(4):
                            lhsT = UT[:, tb, g * 128:(g + 1) * 128]
                            self.mm(self.ps[g], lhsT, DT[slot][:, tb4, 0, :], start=(tb == 0), stop=(tb == 15))
                            self.mm(self.ps[4 + g], lhsT, DT[slot][:, tb4, 1, :], start=(tb == 0), stop=(tb == 15))
            else:
                for sq in range(2):
                    for tb in range(2):
                        for g in range(4):
                            lhsT = UT[:, sq * 2 + tb, g * 128:(g + 1) * 128]
                            cs = slice(sq * 256, (sq + 1) * 256)
                            self.mm(self.ps[g][:, cs], lhsT, DT256[:, tb, 0, :], start=(tb == 0), stop=(tb == 1))
                            self.mm(self.ps[4 + g][:, cs], lhsT, DT256[:, tb, 1, :], start=(tb == 0), stop=(tb == 1))
            for g in range(4):
                self.cp(ABT[:, 0, g, :], self.ps[g], eng="act")
                self.cp(ABT[:, 1, g, :], self.ps[4 + g], eng="dve")
            for g in range(4):
                y = self.ps[g]
                self.mm(y, DFTC[:, 0, :], ABT[:, 0, g, :], start=True, stop=False)
                self.mm(y, DFTC[:, 1, :], ABT[:, 1, g, :], start=False, stop=True)
                self.cp(FN[:, g, :], y, eng=("act" if g % 2 else "dve"))
            w = self.who(tile)
            for dc in range(NCH):
                o = self.ps[4 + self.rot("opb0", 4)]
                for g in range(4):
                    self.mm(o, WOF[:, g, dc * 128:(dc + 1) * 128], FN[:, g, :], start=(g == 0), stop=(g == 3))
                x = self.XT[:, tile, dc, :]
                self.stt(x, o, self.GH[:, 0, 1, w, dc:dc + 1], x, ALU.mult, ALU.add)

    def gla_pair(self, kind, tiles, scr, p):
        d = self.dram
        WA, P1 = self.WA_off, self.P1_off
        V = self.V
        WQp = V(WA, BF16, [NCH, 128])
        WKp = V(WA + 2048, BF16, [NCH, 128])
        WVp = V(WA + 4096, BF16, [NCH, 256])
        WGp = V(WA + 8192, BF16, [NCH, 256])
        WLF = V(WA + 12288, BF16, [NCH, 16])
        WLB = V(WA + 12544, BF16, [NCH, 16])
        WGF = V(WA + 12800, BF16, [128])
        WGB = V(WA + 13056, BF16, [128])
        WOp = V(WA + 13312, BF16, [2, D])
        src = d["l0_w_in"].rearrange("(kc q) n -> q kc n", q=128)
        self.dma(WLF, src[:, :, 1536:1552], "g_wlf", eng="pool")
        self.dma(WKp, src[:, :, 256 + p * 128:256 + (p + 1) * 128], "g_wk", eng="pool")
        self.dma(WGF[0:16, :], d["l0_w_gf"][:, p * 128:(p + 1) * 128], "g_wgf", eng="pool")
        self.dma(WVp, src[:, :, 512 + p * 256:512 + (p + 1) * 256], "g_wv", eng="pool")
        self.dma(WQp, src[:, :, p * 128:(p + 1) * 128], "g_wq", eng="pool")
        self.dma(WLB, src[:, :, 1552:1568], "g_wlb", eng="pool")
        self.dma(WGB[0:16, :], d["l0_w_gb"][:, p * 128:(p + 1) * 128], "g_wgb", eng="pool")
        self.dma(WGp, src[:, :, 1024 + p * 256:1024 + (p + 1) * 256], "g_wg", eng="pool")
        self.dma(WOp, d["l0_w_out"][p * 256:(p + 1) * 256, :].rearrange("(h q) n -> q h n", q=128), "g_wo", eng="pool")
        o = P1
        SP = V(o, F32, [TT]); o += 2048
        PSP = V(o, F32, [TT]); o += 2048
        T3 = V(o, F32, [TT]); o += 2048
        E1 = V(o, F32, [TT]); o += 2048
        E2 = V(o, F32, [TT]); o += 2048
        QDF = V(o, BF16, [TT]); o += 1024
        KIF = V(o, BF16, [TT]); o += 1024
        QDB = V(o, BF16, [TT]); o += 1024
        KIB = V(o, BF16, [TT]); o += 1024
        KE = V(o, BF16, [TT]); o += 1024
        KETOK = V(o, BF16, [4, 128]); o += 1024
        VTOK = V(o, BF16, [4, 256]); o += 2048
        LFT = V(o, BF16, [TT]); o += 1024
        LBT = LFT
        ATF = V(o, BF16, [2, 128]); o += 512
        ATB = V(o, BF16, [2, 128]); o += 512
        SALL = V(o, F32, [8, 128]); o += 4096
        INIT = V(o, F32, [128]); o += 512
        ZERO = V(o, F32, [128]); o += 512
        SB16 = V(o, BF16, [8, 128]); o += 2048
        GL = V(o, BF16, [2, TT]); o += 2048
        SQ16 = V(o, BF16, [TT]); o += 1024
        assert o <= self.arena_bytes, o
        SF16 = V(scr, BF16, [33, 128])
        self.memset(ZERO, 0.0)
        ps = self.ps
        c3 = lambda a: a.rearrange("q (c t) -> q c t", c=8)
        nb_f = self.NBG[:, 0, p:p + 1]
        nb_b = self.NBG[:, 1, p:p + 1]
        ktb = ps[5].bitcast(BF16)
        ktb3 = ps[3].bitcast(BF16)

        def proj_small(dst, W, tile):
            lp = ps[2][0:16, :]
            for kc in range(NCH):
                self.mm(lp, W[:, kc, :], self.HT[:, tile, kc, :], start=(kc == 0), stop=(kc == NCH - 1))
            self.cp(dst[0:16, :], lp, eng="act")

        def proj(bank, W, tile):
            for kc in range(NCH):
                self.mm(bank, W[:, kc, :], self.HT[:, tile, kc, :], start=(kc == 0), stop=(kc == NCH - 1))

        def softplus_neg(glog, nb):
            self.act(SP, glog, AF.Exp, bias=nb, scale=-1.0)
            self.act(SP, SP, AF.Ln, bias=1.0, scale=1.0)

        def ke_tok(tb):
            for blk in range(4):
                self.tr(tb[:, blk * 128:(blk + 1) * 128], KE[:, blk * 128:(blk + 1) * 128], self.IDB)
            self.cp(KETOK, tb[:, 0:512].rearrange("q (b c) -> q b c", b=4))

        def v_tok(tile):
            for blk in range(4):
                vp = ps[6 + blk // 2][:, (blk % 2) * 256:(blk % 2 + 1) * 256]
                for kc in range(NCH):
                    self.mm(vp, self.HT[:, tile, kc, blk * 128:(blk + 1) * 128], WVp[:, kc, :],
                            start=(kc == 0), stop=(kc == NCH - 1))
            self.cp(VTOK[:, 0:2, :], ps[6].rearrange("q (b c) -> q b c", b=2), eng="dve")
            self.cp(VTOK[:, 2:4, :], ps[7].rearrange("q (b c) -> q b c", b=2), eng="act")

        def delta(bank, blk, cc):
            dl = bank[:, 0:128]
            for h2 in range(2):
                self.mm(dl[h2 * 64:(h2 + 1) * 64, :], KETOK[cc * 64:(cc + 1) * 64, blk, h2 * 64:(h2 + 1) * 64],
                        VTOK[cc * 64:(cc + 1) * 64, blk, h2 * 128:(h2 + 1) * 128], start=True, stop=True)
            return dl

        def st_out(name, S32, seq):
            self.dma(d[name][seq, 2 * p:2 * p + 2].rearrange("h k v -> (h k) v"), S32, "so_%s%d" % (name, seq % 2),
                     final=True)

        def deltas(banks):
            dls = []
            for c in range(8):
                dl = banks[c % 2][:, (c // 2) * 128:(c // 2 + 1) * 128]
                blk, cc = c // 2, c % 2
                for h2 in range(2):
                    self.mm(dl[h2 * 64:(h2 + 1) * 64, :], KETOK[cc * 64:(cc + 1) * 64, blk, h2 * 64:(h2 + 1) * 64],
                            VTOK[cc * 64:(cc + 1) * 64, blk, h2 * 128:(h2 + 1) * 128], start=True, stop=True)
                dls.append(dl)
            return dls

        for ti, tile in enumerate(tiles):
            proj_small(LFT, WLF, tile)
            proj(ps[1], WKp, tile)
            self.mm(ps[3], WGF[0:16, :], LFT[0:16, :])
            v_tok(tile)
            softplus_neg(ps[3], nb_f)
            self.scan(PSP, self.SCANM, SP)
            self.tt(c3(T3), c3(PSP), c3(PSP)[:, :, 63:64].broadcast_to([128, 8, 64]), ALU.subtract)
            self.act(E2, T3, AF.Exp, scale=1.0 / 16.0)
            self.tt(KE, ps[1], E2, ALU.mult)
            self.act(E1, PSP, AF.Exp, scale=-1.0 / 16.0)
            ke_tok(ktb)
            dls = deltas([ps[0], ps[4]])
            n0 = ti * 8
            if kind == "s" and ti == 0:
                self.dma(INIT, d["st_f"][2 * p:2 * p + 2].rearrange("h k v -> (h k) v"), "st_in")
                self.cp(SF16[:, 0, :], INIT, eng="act")
            for c in range(8):
                if kind == "s":
                    s_in = INIT if (ti == 0 and c == 0) else SALL[:, (c - 1) % 8, :]
                else:
                    s_in = ZERO if c % 4 == 0 else SALL[:, c - 1, :]
                self.stt(SALL[:, c, :], s_in, E1[:, c * 64 + 63:c * 64 + 64], dls[c], ALU.mult, ALU.add)
            if kind == "s":
                self.cp(SF16[:, n0 + 1:n0 + 9, :], SALL, eng="act")
            else:
                self.cp(SF16[:, 1:8, :], SALL[:, 0:7, :], eng="act")
                self.memset(SF16[:, 0, :], 0.0)
                self.memset(SF16[:, 4, :], 0.0)
                st_out("o_sf", SALL[:, 3, :], (tile - 4) * 2)
                st_out("o_sf", SALL[:, 7, :], (tile - 4) * 2 + 1)
        for ti in range(len(tiles) - 1, -1, -1):
            tile = tiles[ti]
            w = self.who(tile)
            proj(ps[0], WQp, tile)
            proj(ps[1], WKp, tile)
            proj_small(LFT, WLF, tile)
            self.mm(ps[3], WGF[0:16, :], LFT[0:16, :])
            proj_small(LBT, WLB, tile)
            v_tok(tile)
            for h2 in range(2):
                for kc in range(NCH):
                    self.mm(ps[6 + h2], WGp[:, kc, h2 * 128:(h2 + 1) * 128], self.HT[:, tile, kc, :],
                            start=(kc == 0), stop=(kc == NCH - 1))
            softplus_neg(ps[3], nb_f)
            self.scan(PSP, self.SCANM, SP)
            self.act(E1, PSP, AF.Exp, scale=-1.0 / 16.0)
            self.act(E2, PSP, AF.Exp, scale=1.0 / 16.0)
            self.stt(QDF, ps[0], 0.125, E1, ALU.mult, ALU.mult)
            self.tt(KIF, ps[1], E2, ALU.mult)
            self.mm(ps[3], WGB[0:16, :], LBT[0:16, :])
            softplus_neg(ps[3], nb_b)
            self.scan(PSP, self.SCANM, SP)
            self.tt(T3, SP, PSP, ALU.subtract)
            self.tt(c3(SP), c3(T3), c3(PSP)[:, :, 63:64].broadcast_to([128, 8, 64]), ALU.add)
            self.act(E1, SP, AF.Exp, scale=-1.0 / 16.0)
            self.act(E2, SP, AF.Exp, scale=1.0 / 16.0)
            self.stt(QDB, ps[0], 0.125, E1, ALU.mult, ALU.mult)
            self.tt(KIB, ps[1], E2, ALU.mult)
            self.act(T3, T3, AF.Exp, scale=1.0 / 16.0)
            self.tt(KE, ps[1], T3, ALU.mult)
            ke_tok(ktb3)
            dls = deltas([ps[0], ps[1]])
            last_t = (ti == len(tiles) - 1)
            if kind == "s":
                if last_t:
                    self.dma(INIT, d["st_b"][2 * p:2 * p + 2].rearrange("h k v -> (h k) v"), "st_in")
                    s7 = INIT
                else:
                    s7 = SALL[:, 0, :]
                self.cp(SB16[:, 7, :], s7, eng="act")
            for c in range(7, -1, -1):
                if kind == "s":
                    s_in = s7 if c == 7 else SALL[:, c + 1, :]
                else:
                    s_in = ZERO if c % 4 == 3 else SALL[:, c + 1, :]
                self.stt(SALL[:, c, :], s_in, E1[:, c * 64:c * 64 + 1], dls[c], ALU.mult, ALU.add)
            self.cp(SB16[:, 0:7, :], SALL[:, 1:8, :], eng="act")
            if kind == "p":
                self.memset(SB16[:, 3, :], 0.0)
                self.memset(SB16[:, 7, :], 0.0)
                st_out("o_sb", SALL[:, 0, :], (tile - 4) * 2)
                st_out("o_sb", SALL[:, 4, :], (tile - 4) * 2 + 1)
            for blk in range(3, -1, -1):
                bs = slice(blk * 128, (blk + 1) * 128)
                for h2 in range(2):
                    hs = slice(h2 * 64, (h2 + 1) * 64)
                    at = ps[2 + self.rot("atb", 2)]
                    self.mm(at[:, 0:128], KIF[hs, bs], QDF[hs, bs])
                    self.mm(at[:, 128:256], KIB[hs, bs], QDB[hs, bs])
                    sl = self.rot("afs", 2)
                    self.tt(ATF[:, sl, :], at[:, 0:128], self.MF, ALU.mult)
                    self.tt(ATB[:, sl, :], at[:, 128:256], self.MB, ALU.mult)
                    ot = ps[4 + h2]
                    vl = VTOK[:, blk, h2 * 128:(h2 + 1) * 128]
                    self.mm(ot[:, bs], vl, ATF[:, sl, :], start=True, stop=False)
                    self.mm(ot[:, bs], vl, ATB[:, sl, :], start=False, stop=False)
                    for cc in range(2):
                        c = blk * 2 + cc
                        n = ti * 8 + c
                        cs = slice(c * 64, (c + 1) * 64)
                        self.mm(ot[:, cs], SF16[hs, n, :], QDF[hs, cs], start=False, stop=False)
                        self.mm(ot[:, cs], SB16[hs, c, :], QDB[hs, cs], start=False, stop=(cc == 1))
            for h2 in range(2):
                og = ps[6 + h2]
                ot = ps[4 + h2]
                self.act(SQ16, ot, AF.Square)
                self.mm(ps[2], self.ONES_DV, SQ16)
                self.rsqrt(SP, ps[2])
                self.stt(T3, ot, self.GHEAD[:, 0:1], SP, ALU.mult, ALU.mult)
                self.sigmoid(E1, og)
                self.tt(E2, og, E1, ALU.mult)
                self.tt(GL[:, h2, :], T3, E2, ALU.mult)
            for dc in range(NCH):
                ob = ps[self.rot("opb1", 2)]
                for h2 in range(2):
                    self.mm(ob, WOp[:, h2, dc * 128:(dc + 1) * 128], GL[:, h2, :], start=(h2 == 0), stop=(h2 == 1))
                x = self.XT[:, tile, dc, :]
                self.stt(x, ob, self.GH[:, 0, 1, w, dc:dc + 1], x, ALU.mult, ALU.add)
            if p == 1 and kind == "p" and self.cfg.get("early_ada", True):
                self.adanorm(tile, 0, 2)
                self.prenormed.add(tile)

    def mixer_l1(self):
        d = self.dram
        l, s = 1, 1
        ntiles = self.cfg.get("ntiles", NT)
        for tile in range(ntiles):
            self.adanorm(tile, l, s)
        WA = self.WA_off
        WQ = self.V(WA, BF16, [NCH, 512])
        WK = self.V(WA + 8192, BF16, [NCH, 128])
        WV = self.V(WA + 10240, BF16, [NCH, 128])
        WO = self.V(WA + 12288, BF16, [4, D])
        P = self.P1_off
        KT = self.V(P, BF16, [2048]); P += 4096
        VT = self.V(P, BF16, [16, 128]); P += 4096
        QT = self.V(P, BF16, [4, TT]); P += 4096
        ROPE = self.V(P, F32, [2, TT]); P += 4096
        CKT = self.V(WA + 20480, BF16, [2, 256])
        CV = self.V(WA + 21504, BF16, [2, 256])
        PT = [self.V(P + i * 1024, BF16, [TT]) for i in range(4)]; P += 4096
        OB = self.V(P, BF16, [4, TT]); P += 4096
        DENR = self.V(P, F32, [TT]); P += 2048
        KVO = self.V(P, F32, [4, 2, 128]); P += 4096
        QB = self.V(WA + 22528, BF16, [TT])
        assert P <= self.arena_bytes, P
        for p in range(2):
            for e in range(2):
                self.dma(self.SINKE[e * 64:(e + 1) * 64, p, :],
                         d["sink"][2 * p + e:2 * p + e + 1, :].partition_broadcast(64), "sink%d%d" % (p, e))
        self.act(self.SINKE, self.SINKE, AF.Exp)
        has_sample = ntiles >= 4
        groups = []
        if has_sample:
            groups.append(("s", [0, 1, 2, 3]))
        for t in range(4, ntiles):
            groups.append(("p", [t]))
        for p in range(2):
            wsrc = d["l1_w_in"].rearrange("(kc q) n -> q kc n", q=128)
            self.dma(WK, wsrc[:, :, 1024 + p * 128:1024 + (p + 1) * 128], "l1wk", eng="pool")
            self.dma(WV, wsrc[:, :, 1280 + p * 128:1280 + (p + 1) * 128], "l1wv", eng="pool")
            if has_sample:
                self.dma(CV, d["cv"].rearrange("(b t) c -> t b c", t=128), "cvf", eng="pool")
            wq_src = wsrc[:, :, p * 512:(p + 1) * 512].rearrange("q k (e g dd) -> q k e g dd", e=2, g=4)
            wq_dst = WQ.rearrange("q k (g e dd) -> q k g e dd", g=4, e=2)
            for g in range(4):
                for e in range(2):
                    self.dma(wq_dst[:, :, g, e, :], wq_src[:, :, e, g, :], "l1wq%d%d" % (g, e), eng="pool")
            wo = d["l1_w_out"].rearrange("(pp e g dd) n -> pp e dd g n", pp=2, e=2, g=4)
            for e in range(2):
                self.dma(WO[e * 64:(e + 1) * 64, :, :], wo[p, e], "l1wo%d" % e, eng="pool")
            if has_sample:
                ckf = self.V(self.P1_off + 12288, F32, [2, 128])
                self.dma(ckf, d["ck"].rearrange("(b t) c -> t b c", t=128)[:, :, p * 128:(p + 1) * 128], "ckf")
                for b in range(2):
                    ps = self.ps[6]
                    self.tr(ps[:, b * 128:(b + 1) * 128], ckf[:, b, :], self.IDF)
                self.cp(CKT[:, p, :], self.ps[6][:, 0:256])
            L1S = self.cfg.get("l1stop", 99)
            if L1S <= 1:
                return
            for kind, tiles in groups:
                w = 0 if kind == "s" else 1
                for ti, tile in enumerate(tiles):
                    if kind == "s":
                        self.dma(ROPE, d["rope"][:, :, tile * TT:(tile + 1) * TT].rearrange("c q t -> q c t"),
                                 "rope")
                    kps = self.ps[6]
                    for kc in range(NCH):
                        self.mm(kps, WK[:, kc, :], self.HT[:, tile, kc, :], start=(kc == 0), stop=(kc == NCH - 1))
                    kdst = KT[:, ti * TT:(ti + 1) * TT]
                    P1S = self.cfg.get("p1", "")
                    if P1S == "k":
                        self.cp(kdst, kps, eng="act")
                        return
                    if kind == "s":
                        self.cp(QB, kps, eng="act")
                        rps = self.ps[7]
                        self.mm(rps, self.ROT, QB)
                        self.tt(DENR, kps, ROPE[:, 0, :], ALU.mult)
                        self.tt(KVO.rearrange("q a b c -> q (a b c)")[:, 0:TT], rps, ROPE[:, 1, :], ALU.mult)
                        self.tt(kdst, DENR, KVO.rearrange("q a b c -> q (a b c)")[:, 0:TT], ALU.add)
                    else:
                        self.cp(kdst, kps, eng="act")
                    if P1S == "rope":
                        return
                    vps = self.ps[0]
                    for blk in range(4):
                        for kc in range(NCH):
                            self.mm(vps[:, blk * 128:(blk + 1) * 128], self.HT[:, tile, kc, blk * 128:(blk + 1) * 128],
                                    WV[:, kc, :], start=(kc == 0), stop=(kc == NCH - 1))
                    if P1S == "v1":
                        return
                    self.cp(VT[:, ti * 4:(ti + 1) * 4, :], vps.rearrange("q (b c) -> q b c", b=4))
                    if P1S == "v2" and ti + 1 >= self.cfg.get("p1n", 1):
                        return
                    if kind == "p":
                        self.cp(KVO[:, :, 1, :], vps.rearrange("q (b c) -> q b c", b=4), eng="act")
                        k2 = self.ps[1]
                        for blk in range(4):
                            for kc in range(NCH):
                                self.mm(k2[:, blk * 128:(blk + 1) * 128],
                                        self.HT[:, tile, kc, blk * 128:(blk + 1) * 128],
                                        WK[:, kc, :], start=(kc == 0), stop=(kc == NCH - 1))
                        self.cp(KVO[:, :, 0, :], k2.rearrange("q (b c) -> q b c", b=4))
                        t0 = (tile - 4) * TT
                        self.dma(d["o_kc"][t0:t0 + TT, p * 128:(p + 1) * 128].rearrange("(b t) c -> t b c", t=128),
                                 KVO[:, :, 0, :], "okc", final=True)
                        self.dma(d["o_vc"][t0:t0 + TT, p * 128:(p + 1) * 128].rearrange("(b t) c -> t b c", t=128),
                                 KVO[:, :, 1, :], "ovc", final=True)
                if L1S <= 2:
                    return
                for ti, tile in enumerate(tiles):
                    if kind == "s":
                        self.dma(ROPE, d["rope"][:, :, tile * TT:(tile + 1) * TT].rearrange("c q t -> q c t"),
                                 "rope")
                    if self.cfg.get("p2", "") == "r":
                        return
                    for g in range(4):
                        qps = self.ps[(g % 2) * 2]
                        for kc in range(NCH):
                            self.mm(qps, WQ[:, kc, g * 128:(g + 1) * 128], self.HT[:, tile, kc, :],
                                    start=(kc == 0), stop=(kc == NCH - 1))
                        P2S = self.cfg.get("p2", "")
                        if P2S == "q0":
                            return
                        if P2S == "q1":
                            self.act(QT[:, g, :], qps, AF.Identity, scale=0.125)
                            return
                        if kind == "s":
                            self.cp(QB, qps, eng="act")
                            rps = self.ps[(g % 2) * 2 + 1]
                            self.mm(rps, self.ROT, QB)
                            t2 = KVO.rearrange("q a b c -> q (a b c)")[:, 0:TT]
                            self.tt(DENR, qps, ROPE[:, 0, :], ALU.mult)
                            self.tt(t2, rps, ROPE[:, 1, :], ALU.mult)
                            self.tt(DENR, DENR, t2, ALU.add)
                            self.act(QT[:, g, :], DENR, AF.Identity, scale=0.125)
                            if P2S == "q2":
                                return
                        else:
                            self.act(QT[:, g, :], qps, AF.Identity, scale=0.125)
                    if L1S <= 3:
                        return
                    steps = []
                    for qb in range(4):
                        if kind == "s":
                            i = ti * 4 + qb
                            kbs = [("c", 0, None), ("c", 1, None)]
                            if i - 1 >= 0:
                                kbs.append(("l", i - 1, self.MNLO))
                            kbs.append(("l", i, None))
                            if i + 1 <= 15:
                                kbs.append(("l", i + 1, self.MNHI))
                        else:
                            sq = qb // 2
                            kbs = [("l", sq * 2, None), ("l", sq * 2 + 1, None)]
                        for ki, kb_ in enumerate(kbs):
                            steps.append((qb, ki, len(kbs), kb_))

                    def scores(stp):
                        qb, ki, nk, (src, kb, mask) = stp
                        qsl = slice(qb * 128, (qb + 1) * 128)
                        sts, pts = [], []
                        for e in range(2):
                            hs = slice(e * 64, (e + 1) * 64)
                            st = self.ps[self.rot("stbank", 4)]
                            kl = CKT[hs, p, kb * 128:(kb + 1) * 128] if src == "c" else KT[hs, kb * 128:(kb + 1) * 128]
                            self.mm(st, kl, QT[hs, :, qsl], start=True, stop=(mask is None))
                            sts.append(st)
                        if mask is not None:
                            for e in range(2):
                                self.mm(sts[e], self.IDB, mask, start=False, stop=True)
                        for e in range(2):
                            pt = PT[self.rot("pt", 4)]
                            self.act(pt, sts[e], AF.Exp)
                            pts.append(pt)
                        return pts

                    def pv(stp, pts):
                        qb, ki, nk, (src, kb, mask) = stp
                        qsl = slice(qb * 128, (qb + 1) * 128)
                        par = qb % 2
                        OT = self.ps[4 + 2 * par]
                        DEN = self.ps[5 + 2 * par]
                        first, last = (ki == 0), (ki == nk - 1)
                        for e in range(2):
                            hs = slice(e * 64, (e + 1) * 64)
                            if src == "c":
                                vl = CV[:, kb, (2 * p + e) * 64:(2 * p + e + 1) * 64]
                            else:
                                vl = VT[:, kb, e * 64:(e + 1) * 64]
                            self.mm(OT[hs, :], vl, pts[e], start=first, stop=last)
                        for e in range(2):
                            hs = slice(e * 64, (e + 1) * 64)
                            self.mm(DEN[hs, :], self.ONE1[:, 0:64], pts[e], start=first, stop=last)
                        if last:
                            self.tt(DENR.rearrange("q (g t) -> q g t", g=4), DEN.rearrange("q (g t) -> q g t", g=4),
                                    self.SINKE[:, p, :].unsqueeze(2).broadcast_to([128, 4, 128]), ALU.add)
                            self.S.add("dve", lambda e_, o=DENR: e_.reciprocal(o, o), outs=[DENR], ins=[DENR])
                            self.tt(OB[:, :, qsl], OT.rearrange("q (g t) -> q g t", g=4),
                                    DENR.rearrange("q (g t) -> q g t", g=4), ALU.mult)

                    prev = None
                    for stp in steps:
                        pts = scores(stp)
                        if prev is not None:
                            pv(*prev)
                        prev = (stp, pts)
                    pv(*prev)
                    if L1S <= 4:
                        return
                    for dc in range(NCH):
                        o = self.ps[self.rot("opbank", 4)]
                        for g in range(4):
                            self.mm(o, WO[:, g, dc * 128:(dc + 1) * 128], OB[:, g, :], start=(g == 0), stop=(g == 3))
                        x = self.XT[:, tile, dc, :]
                        self.stt(x, o, self.GH[:, l, s, w, dc:dc + 1], x, ALU.mult, ALU.add)
                    if p == 1 and self.cfg.get("early_ada", True):
                        self.adanorm(tile, 1, 2)
                        self.prenormed.add(tile)


_CACHE = {}


def _prep_inputs(inp, consts):
    f = lambda a: np.ascontiguousarray(np.asarray(a, dtype=np.float32))
    xs = f(inp["x_sample"])
    xp = f(inp["x_prompt"])
    shared = {
        "mod_w": f(inp["mod_w"]), "ffn_w1": f(inp["ffn_w1"]), "ffn_w3": f(inp["ffn_w3"]),
        "ffn_w2": f(inp["ffn_w2"]), "l0_w_in": f(inp["l0_w_in"]), "l0_w_gf": f(inp["l0_w_gf"]),
        "l0_w_gb": f(inp["l0_w_gb"]), "l0_w_out": f(inp["l0_w_out"]), "l1_w_in": f(inp["l1_w_in"]),
        "l1_w_out": f(inp["l1_w_out"]), "sink": f(inp["l1_sink"]),
        "modb": f(inp["mod_b"]).reshape(144, 128),
        "cb": consts["cb"], "cf": consts["cf"], "rope": consts["rope"], "dftc": consts["dftc"],
        "dft2k": consts["dft2k"], "dft256": consts["dft256"],
    }
    c = f(inp["c"])
    cctx = f(inp["c_ctx"])
    tail = np.concatenate([
        f(inp["norm_g"]).reshape(48, 128), f(inp["final_g"]).reshape(8, 128),
        f(inp["l0_g_head"]).reshape(1, 128), f(inp["l0_b_gf"]).reshape(2, 128),
        f(inp["l0_b_gb"]).reshape(2, 128)], axis=0)
    maps = []
    for b in range(8):
        m = dict(shared)
        m["xin"] = np.ascontiguousarray(np.concatenate([xs[b], xp[4 * b:4 * b + 4].reshape(1024, D)], axis=0))
        m["small1"] = np.ascontiguousarray(np.concatenate([c[b].reshape(8, 128), cctx.reshape(8, 128), tail], 0))
        m["st_f"] = f(inp["state_l0_gla_fwd"])[b]
        m["st_b"] = f(inp["state_l0_gla_bwd"])[b]
        m["ck"] = f(inp["cache_l1_k"])[b].reshape(256, 256)
        m["cv"] = f(inp["cache_l1_v"])[b].reshape(256, 256)
        maps.append(m)
    return maps


def run(inp, cfg=None, trace=False):
    key = repr(sorted((cfg or {}).items()))
    if "consts" not in _CACHE:
        _CACHE["consts"] = _consts()
    consts = _CACHE["consts"]
    B = Builder(cfg)
    nc = B.build()
    maps = _prep_inputs(inp, consts)
    maps = [{k: m[k] for k in B.in_names} for m in maps]
    ncores = (cfg or {}).get("cores", 8)
    c0 = (cfg or {}).get("core0", 0)
    res = run_bass_kernel_spmd(nc, maps[:ncores], core_ids=list(range(c0, c0 + ncores)), trace=trace)
    return res


def kernel(**inp):
    res = run(inp)
    r = res.results
    y = np.stack([r[b]["y"] for b in range(8)], 0)
    y_sample = np.ascontiguousarray(y[:, :2048, :])
    y_prompt = np.ascontiguousarray(y[:, 2048:, :].reshape(32, 256, D))
    sf = np.concatenate([r[b]["o_sf"] for b in range(8)], 0).astype(np.float32)
    sb = np.concatenate([r[b]["o_sb"] for b in range(8)], 0).astype(np.float32)
    kc = np.concatenate([r[b]["o_kc"] for b in range(8)], 0).reshape(32, 256, 4, 64).astype(np.float32)
    vc = np.concatenate([r[b]["o_vc"] for b in range(8)], 0).reshape(32, 256, 4, 64).astype(np.float32)
    return (y_prompt.astype(np.float32), y_sample.astype(np.float32), sf, sb, kc, vc)
```

```python
import numpy as np
import ml_dtypes
from contextlib import ExitStack

import concourse.bass as bass
import concourse.mybir as mybir
from concourse.bass_utils import run_bass_kernel_spmd

F32 = mybir.dt.float32
BF16 = mybir.dt.bfloat16
AF = mybir.ActivationFunctionType
ALU = mybir.AluOpType

D = 1024
NCH = 8
TT = 512
NT = 6
FFN = 2816
NFG = 11
EPS = 1e-6
NTOK = 3072
L0_IN = 2080
L1_IN = 1536

_ESZ = {F32: 4, BF16: 2}
PAGE = 256
POOL_INFLIGHT = 3


class Op:
    __slots__ = ("eng", "fn", "deps", "idx", "sig", "sig_idx", "is_dma", "dkey", "dcount")


class Sched:
    ENGS = ("pe", "act", "dve", "pool", "sp")

    def __init__(self, tracked):
        self.q = {e: [] for e in self.ENGS}
        self.lastw = {}
        self.rd = {}
        self.dma_count = {}
        self.tracked = tracked
        self._pcache = {}
        self.final_dma = []
        self.pool_dmas = []

    def _pages(self, ap):
        t = ap.tensor
        name = t.name
        if name not in self.tracked:
            return ()
        dims = tuple((int(a), int(b)) for a, b in ap.ap)
        key = (name, int(ap.offset), dims, str(ap.dtype))
        r = self._pcache.get(key)
        if r is not None:
            return r
        esz = _ESZ[ap.dtype]
        row = int(t.shape[1])
        off = int(ap.offset)
        p0 = off // row
        f0 = off % row
        pcnt = dims[0][1]
        halves = sorted(set([(p0) // 64, (p0 + pcnt - 1) // 64]))
        starts = np.zeros(1, dtype=np.int64) + f0
        fd = dims[1:]
        if len(fd) == 0:
            run = 1
        else:
            for (st, cn) in fd[:-1]:
                if st == 0 or cn == 1:
                    continue
                starts = (starts[:, None] + (np.arange(cn, dtype=np.int64) * st)[None, :]).reshape(-1)
            st, cn = fd[-1]
            run = (cn - 1) * abs(st) + 1
        pages = set()
        for s in starts.tolist():
            b0 = (s * esz) // PAGE
            b1 = ((s + run) * esz - 1) // PAGE
            for pg in range(b0, b1 + 1):
                pages.add(pg)
        r = tuple((name, h, pg) for h in halves for pg in sorted(pages))
        self._pcache[key] = r
        return r

    def add(self, eng, fn, outs=(), ins=(), dma_key=None, final=False):
        op = Op()
        op.eng = eng
        op.fn = fn
        op.sig = False
        op.sig_idx = 0
        op.is_dma = dma_key is not None
        op.dkey = dma_key
        op.dcount = 0
        if op.is_dma:
            c = self.dma_count.get(dma_key, 0) + 1
            self.dma_count[dma_key] = c
            op.dcount = c
            if final:
                self.final_dma.append(op)
        op.idx = len(self.q[eng])
        deps = {}

        def dep(d):
            if d is op:
                return
            if d.is_dma:
                k = ("dma", d.dkey)
                if k not in deps or deps[k].dcount < d.dcount:
                    deps[k] = d
            else:
                if eng == "pe" and d.eng == "pe" and not op.is_dma:
                    return
                k = ("eng", d.eng)
                if k not in deps or deps[k].idx < d.idx:
                    deps[k] = d

        in_pages = []
        out_pages = []
        for ap in ins:
            pgs = self._pages(ap)
            if pgs and pgs[0][0].startswith("ps"):
                out_pages.extend(sorted(set((n, h, 0) for (n, h, _) in pgs)))
            else:
                in_pages.extend(pgs)
        for ap in outs:
            pgs = self._pages(ap)
            if pgs and pgs[0][0].startswith("ps"):
                out_pages.extend(sorted(set((n, h, 0) for (n, h, _) in pgs)))
            else:
                out_pages.extend(pgs)
        lastw = self.lastw
        rd = self.rd
        for pg in in_pages:
            w = lastw.get(pg)
            if w is not None:
                dep(w)
        for pg in out_pages:
            w = lastw.get(pg)
            if w is not None:
                dep(w)
            rs = rd.get(pg)
            if rs:
                for r in rs.values():
                    dep(r)
        for pg in out_pages:
            lastw[pg] = op
            rd[pg] = {}
        key_r = ("dma", op.dkey, op.dcount) if op.is_dma else op.eng
        for pg in in_pages:
            rs = rd.get(pg)
            if rs is None:
                rs = {}
                rd[pg] = rs
            rs[key_r if not op.is_dma else key_r] = op
        if op.is_dma and eng == "pool":
            pl = self.pool_dmas
            if len(pl) >= POOL_INFLIGHT:
                dep(pl[-POOL_INFLIGHT])
            pl.append(op)
        op.deps = list(deps.values())
        for d in op.deps:
            if not d.is_dma:
                d.sig = True
        self.q[eng].append(op)
        return op

    def emit(self, nc):
        for e in self.ENGS:
            n = 0
            for op in self.q[e]:
                if op.sig and not op.is_dma:
                    n += 1
                    op.sig_idx = n
        with ExitStack() as st:
            esem = {e: st.enter_context(nc.semaphore("sem_" + e)) for e in ("pe", "act", "dve", "pool")}
            dsem = {k: st.enter_context(nc.semaphore("dsem_%d" % i)) for i, k in enumerate(self.dma_count)}
            block = st.enter_context(nc.Block())

            def run(eh, e):
                waited = {}
                for op in self.q[e]:
                    for d in op.deps:
                        if d.is_dma:
                            s, v = dsem[d.dkey], 16 * d.dcount
                        else:
                            s, v = esem[d.eng], d.sig_idx
                        k = id(s)
                        if waited.get(k, 0) < v:
                            eh.wait_ge(s, v)
                            waited[k] = v
                    ins = op.fn(eh)
                    if op.is_dma:
                        ins.then_inc(dsem[op.dkey], 16)
                    elif op.sig:
                        ins.then_inc(esem[e], 1)
                if e == "sp":
                    for k, c in self.dma_count.items():
                        eh.wait_ge(dsem[k], 16 * c)

            @block.tensor
            def _(eh):
                run(eh, "pe")

            @block.scalar
            def _(eh):
                run(eh, "act")

            @block.vector
            def _(eh):
                run(eh, "dve")

            @block.gpsimd
            def _(eh):
                run(eh, "pool")

            @block.sync
            def _(eh):
                run(eh, "sp")


def _bf(a):
    return np.asarray(a, dtype=np.float32).astype(ml_dtypes.bfloat16)


CB_ID, CB_ONES_D, CB_ONES_DV, CB_ONE1, CB_MF, CB_MB, CB_ROT, CB_MNLO, CB_MNHI = (
    0, 128, 256, 384, 512, 640, 768, 896, 1408)
CB_N = 1920


def _consts():
    i = np.arange(128)
    cb = np.zeros((128, CB_N), np.float32)
    cb[:, CB_ID:CB_ID + 128] = np.eye(128)
    cb[:, CB_ONES_D:CB_ONES_D + 128] = 1.0 / 1024.0
    cb[:, CB_ONES_DV:CB_ONES_DV + 128] = 1.0 / 128.0
    cb[:, CB_ONE1:CB_ONE1 + 128] = 1.0
    s = i[:, None]
    t = i[None, :]
    same = (s // 64) == (t // 64)
    cb[:, CB_MF:CB_MF + 128] = (same & (s <= t)).astype(np.float32)
    cb[:, CB_MB:CB_MB + 128] = (same & (s >= t)).astype(np.float32)
    R = np.zeros((64, 64), np.float32)
    for m in range(16):
        R[m, 16 + m] = -1.0
        R[16 + m, m] = 1.0
        R[32 + m, 48 + m] = -1.0
        R[48 + m, 32 + m] = 1.0
    RT = np.zeros((128, 128), np.float32)
    RT[:64, :64] = R.T
    RT[64:, 64:] = R.T
    cb[:, CB_ROT:CB_ROT + 128] = RT
    lo = np.where(s >= t, 0.0, -30000.0).astype(np.float32)
    hi = np.where(s <= t, 0.0, -30000.0).astype(np.float32)
    cb[:, CB_MNLO:CB_MNLO + 512] = np.tile(lo, (1, 4))
    cb[:, CB_MNHI:CB_MNHI + 512] = np.tile(hi, (1, 4))
    cf = np.zeros((128, 128 + 512), np.float32)
    cf[:, :128] = np.eye(128)
    m = np.ones(512, np.float32)
    m[::64] = 0.0
    cf[:, 128:] = m[None, :]
    tt = np.arange(2048)
    row = (tt // 64).astype(np.float32)
    col = (tt % 64).astype(np.float32)
    inv = (np.float32(10000.0) ** (-np.arange(16, dtype=np.float32) / np.float32(16))).astype(np.float32)
    ar = row[:, None] * inv[None, :]
    ac = col[:, None] * inv[None, :]
    ang = np.concatenate([ar, ar, ac, ac], axis=-1).astype(np.float32)
    cos = np.cos(ang).astype(np.float32).T
    sin = np.sin(ang).astype(np.float32).T
    rope = np.stack([np.concatenate([cos, cos], 0), np.concatenate([sin, sin], 0)], 0)
    def dft(n):
        k = np.arange(n)
        kk = (k[:, None] * k[None, :]) % n
        a = 2.0 * np.pi * kk.astype(np.float64) / n
        return np.cos(a) / np.sqrt(n), np.sin(a) / np.sqrt(n)
    c128, s128 = dft(128)
    dftc = np.stack([c128, -s128], 0)
    c2k, s2k = dft(2048)
    c256, s256 = dft(256)
    return dict(cb=_bf(cb), cf=cf, rope=np.ascontiguousarray(rope), dftc=_bf(dftc),
                dft2k=_bf(np.stack([c2k, s2k], 0)), dft256=_bf(np.stack([c256, s256], 0)))


class Builder:
    def __init__(self, cfg=None):
        self.cfg = cfg or {}
        self.nc = bass.Bass("TRN2", target_bir_lowering=False)
        self.dram = {}
        self.in_names = []
        self.out_names = []

    def din(self, name, shape, dt=F32):
        self.dram[name] = self.nc.dram_tensor(name, list(shape), dt, kind="ExternalInput").ap()
        self.in_names.append(name)
        return self.dram[name]

    def dout(self, name, shape, dt=F32):
        self.dram[name] = self.nc.dram_tensor(name, list(shape), dt, kind="ExternalOutput").ap()
        self.out_names.append(name)
        return self.dram[name]

    def V(self, off, dt, shape):
        esz = _ESZ[dt]
        n = int(np.prod(shape))
        assert off % 4 == 0 and (n * esz) % 4 == 0
        assert off + n * esz <= self.arena_bytes, (off, n * esz, self.arena_bytes)
        a = self.arena[:, off // 4:(off + n * esz) // 4]
        if dt != F32:
            a = a.bitcast(dt)
        if len(shape) > 1:
            names = ["d%d" % i for i in range(len(shape))]
            kw = {names[i]: int(shape[i]) for i in range(len(shape) - 1)}
            a = a.rearrange("p (%s) -> p %s" % (" ".join(names), " ".join(names)), **kw)
        return a

    def build(self):
        nc = self.nc
        cfg = self.cfg
        d = self.dram
        self.din("xin", [NTOK, D])
        self.din("small1", [77, 128])
        self.din("modb", [144, 128])
        self.din("st_f", [4, 64, 128])
        self.din("st_b", [4, 64, 128])
        self.din("ck", [256, 256])
        self.din("cv", [256, 256])
        self.din("sink", [4, 4])
        self.din("mod_w", [2, D, 9 * D])
        self.din("ffn_w1", [2, 2, D, FFN])
        self.din("ffn_w3", [2, 2, D, FFN])
        self.din("ffn_w2", [2, 2, FFN, D])
        self.din("l0_w_in", [D, L0_IN])
        self.din("l0_w_gf", [16, 256])
        self.din("l0_w_gb", [16, 256])
        self.din("l0_w_out", [D, D])
        self.din("l1_w_in", [D, L1_IN])
        self.din("l1_w_out", [D, D])
        self.din("cb", [128, CB_N], BF16)
        self.din("cf", [128, 640])
        self.din("rope", [2, 128, 2048])
        self.din("dftc", [2, 128, 128], BF16)
        self.din("dft2k", [2, 2048, 2048], BF16)
        self.din("dft256", [2, 256, 256], BF16)
        self.dout("y", [NTOK, D])
        self.dout("o_sf", [4, 4, 64, 128])
        self.dout("o_sb", [4, 4, 64, 128])
        self.dout("o_kc", [1024, 256])
        self.dout("o_vc", [1024, 256])

        self.arena_bytes = 212480
        with ExitStack() as st:
            self.arena = st.enter_context(nc.sbuf_tensor("arena", [128, self.arena_bytes // 4], F32))
            self.ps = [st.enter_context(nc.psum_tensor("ps%d" % i, [128, 512], F32))[:, :] for i in range(8)]
            tracked = {"arena"} | {"ps%d" % i for i in range(8)}
            self.S = Sched(tracked)
            self._layout()
            self._program()
            self.S.emit(nc)
        return nc

    def _layout(self):
        V = self.V
        o = 0
        self.XT = V(o, F32, [NT, NCH, TT]); o += NT * NCH * TT * 4
        self.HT_off = o
        self.HT = V(o, BF16, [NT, NCH, TT]); o += NT * NCH * TT * 2
        self.WA_off = o
        self.W1 = [V(o + s * 12288, BF16, [NCH, 256]) for s in range(2)]
        self.W3 = [V(o + s * 12288 + 4096, BF16, [NCH, 256]) for s in range(2)]
        self.W2 = [V(o + s * 12288 + 8192, BF16, [2, D]) for s in range(2)]
        o += 24576
        self.CONST_off = o
        c = o
        self.CB = V(c, BF16, [CB_N]); c += CB_N * 2
        self.CF = V(c, F32, [640]); c += 2560
        self.MODT = V(c, F32, [2, 72, 2]); c += 2 * 72 * 2 * 4
        self.SM1 = V(c, F32, [77]); c += 77 * 4 + 0
        self.AM = V(c, F32, [2, 3, 2, NCH]); c += 2 * 3 * 2 * NCH * 4
        self.GH = V(c, F32, [2, 3, 2, NCH]); c += 384
        self.SC = V(c, BF16, [2, NCH]); c += 32
        self.MBT = V(c, F32, [2, 72]); c += 576
        self.SINKE = V(c, F32, [2, 4]); c += 32
        self.NBG = V(c, F32, [2, 2]); c += 16
        assert c - o <= 9472, c - o
        o += 9472
        self.P1_off = o
        self.P1_size = self.arena_bytes - o
        p = o
        self.G = V(p, BF16, [2, 2, TT]); p += 4096
        self.SA = V(p, F32, [2, TT]); p += 4096
        self.SQ = V(p, BF16, [4, TT]); p += 4096
        self.RSTD = V(p, F32, [2, TT]); p += 4096
        self.TMP = V(p, F32, [2, TT]); p += 4096
        self.MODW = [V(p + s * 4096, BF16, [NCH, 256]) for s in range(2)]; p += 8192
        self.P1_ffn_end = p
        assert p <= self.arena_bytes, p
        self.cnt = {}
        self.prenormed = set()

    def rot(self, key, n):
        v = self.cnt.get(key, 0)
        self.cnt[key] = v + 1
        return v % n

    def mm(self, out, lhsT, rhs, start=True, stop=True):
        return self.S.add("pe", lambda e: e.matmul(out, lhsT, rhs, start=start, stop=stop),
                          outs=[out], ins=[lhsT, rhs])

    def tr(self, out, in_, ident):
        return self.S.add("pe", lambda e: e.transpose(out, in_, ident), outs=[out], ins=[in_, ident])

    def act(self, out, in_, func, bias=None, scale=None, eng="act"):
        ins = [in_]
        kw = {}
        if bias is not None:
            kw["bias"] = bias
            if not isinstance(bias, (int, float)):
                ins.append(bias)
        if scale is not None:
            kw["scale"] = scale
            if not isinstance(scale, (int, float)):
                ins.append(scale)
        return self.S.add("act", lambda e: e.activation(out, in_, func, **kw), outs=[out], ins=ins)

    def sigmoid(self, out, in_):
        self.act(out, in_, AF.Exp, scale=-1.0)
        self.act(out, out, AF.Ln, bias=1.0, scale=1.0)
        self.act(out, out, AF.Exp, scale=-1.0)

    def rsqrt(self, out, in_, eps=EPS):
        self.act(out, in_, AF.Ln, bias=eps, scale=1.0)
        self.act(out, out, AF.Exp, scale=-0.5)

    def ts(self, out, in0, s1, s2, op0, op1=None, eng="dve"):
        ins = [in0] + [s for s in (s1, s2) if s is not None and not isinstance(s, (int, float))]
        if op1 is None:
            return self.S.add(eng, lambda e: e.tensor_scalar(out, in0, s1, s2, op0), outs=[out], ins=ins)
        return self.S.add(eng, lambda e: e.tensor_scalar(out, in0, s1, s2, op0, op1), outs=[out], ins=ins)

    def stt(self, out, in0, scalar, in1, op0, op1, eng="dve"):
        ins = [in0, in1] + ([] if isinstance(scalar, (int, float)) else [scalar])
        return self.S.add(eng, lambda e: e.scalar_tensor_tensor(out, in0, scalar, in1, op0, op1),
                          outs=[out], ins=ins)

    def tt(self, out, in0, in1, op, eng="dve"):
        return self.S.add(eng, lambda e: e.tensor_tensor(out, in0, in1, op), outs=[out], ins=[in0, in1])

    def cp(self, out, in_, eng="dve"):
        if eng == "act":
            return self.S.add("act", lambda e: e.copy(out, in_), outs=[out], ins=[in_])
        return self.S.add(eng, lambda e: e.tensor_copy(out, in_), outs=[out], ins=[in_])

    def memset(self, out, val, eng="dve"):
        return self.S.add(eng, lambda e: e.memset(out, val), outs=[out], ins=[])

    def dma(self, out, in_, key, eng="sp", final=False):
        return self.S.add(eng, lambda e: e.dma_start(out=out, in_=in_), outs=[out], ins=[in_],
                          dma_key=key, final=final)

    def _program(self):
        cfg = self.cfg
        stop = cfg.get("stop", "")
        self.load_consts()
        self.mod_queue = [(l, pc) for l in range(2) for pc in range(36)]
        self.mod_lq = list(self.mod_queue)
        self.mod_loaded = []
        self.preloaded = False
        self.mod_load()
        self.mod_load()
        if stop == "consts":
            self.mod_queue = []
            return self.final_out()
        for _ in range(12):
            self.mod_piece()
        if stop == "mod":
            self.mod_queue = []
            return self.final_out()
        self.load_x()
        if stop == "loadx":
            self.mod_queue = []
            return self.final_out()
        for l in cfg.get("layer_list", list(range(cfg.get("layers", 2)))):
            self.mod_ensure(l, 0)
            self.ffn(l, 0)
            self.mod_ensure(l, 2)
            self.mod_ensure(l, 1)
            self.mod_drain()
            if cfg.get("mixers", True):
                if l == 0:
                    self.mixer_l0()
                else:
                    self.mixer_l1()
            self.ffn(l, 1)
        while self.mod_queue:
            self.mod_piece()
        self.final_out()

    def load_consts(self):
        d = self.dram
        CB, CF = self.CB, self.CF
        self.dma(CB, d["cb"], "c_cb")
        self.dma(CF, d["cf"], "c_cf")
        self.IDB = CB[:, CB_ID:CB_ID + 128]
        self.ONES_D = CB[:, CB_ONES_D:CB_ONES_D + 128]
        self.ONES_DV = CB[:, CB_ONES_DV:CB_ONES_DV + 128]
        self.ONE1 = CB[:, CB_ONE1:CB_ONE1 + 128]
        self.MF = CB[:, CB_MF:CB_MF + 128]
        self.MB = CB[:, CB_MB:CB_MB + 128]
        self.ROT = CB[:, CB_ROT:CB_ROT + 128]
        self.MNLO = CB[:, CB_MNLO:CB_MNLO + 512]
        self.MNHI = CB[:, CB_MNHI:CB_MNHI + 512]
        self.IDF = CF[:, 0:128]
        self.SCANM = CF[:, 128:640]
        P1 = self.P1_off
        s1 = self.V(P1, F32, [128])
        mb = self.V(P1 + 512, F32, [2, 128])
        self.dma(s1[0:77, :], d["small1"], "c_s1")
        self.dma(mb[0:72, :, :], d["modb"].rearrange("(l j) f -> j l f", l=2), "c_mb")
        ps = self.ps[7]
        self.tr(ps[:, 0:77], s1[0:77, :], self.IDF[0:77, 0:77])
        self.cp(self.SM1, ps[:, 0:77])
        for l in range(2):
            self.tr(ps[:, 128 + l * 72:128 + (l + 1) * 72], mb[0:72, l, :], self.IDF[0:72, 0:72])
        self.cp(self.MBT, ps[:, 128:272].rearrange("p (l j) -> p l j", l=2))
        self.NG = self.SM1[:, 16:64].rearrange("p (l s c) -> p l s c", l=2, s=3)
        self.FG = self.SM1[:, 64:72]
        self.GHEAD = self.SM1[:, 72:73]
        self.BGF = self.SM1[:, 73:75]
        self.BGB = self.SM1[:, 75:77]
        t0 = self.V(P1 + 2048, F32, [16])
        self.sigmoid(t0, self.SM1[:, 0:16])
        self.tt(self.SC.rearrange("p w c -> p (w c)"), self.SM1[:, 0:16], t0, ALU.mult)
        self.mod_ps_cols = 0

    def mod_load(self):
        if not self.mod_lq:
            return
        l, pc = self.mod_lq.pop(0)
        d = self.dram
        slot = self.rot("modw_l", 2)
        src = d["mod_w"][l].rearrange("(kc p) n -> p kc n", p=128)[:, :, pc * 256:(pc + 1) * 256]
        self.dma(self.MODW[slot], src, "modw%d" % slot, eng="pool")
        self.mod_loaded.append((l, pc, slot))

    def mod_drain(self):
        while self.mod_loaded:
            self.mod_piece(prefetch=False)

    def mod_piece(self, prefetch=True, defer=False):
        if not self.mod_queue:
            return
        if not self.mod_loaded:
            self.mod_load()
        l, pc, slot = self.mod_loaded.pop(0)
        assert (l, pc) == self.mod_queue.pop(0)
        W = self.MODW[slot]
        ps = self.ps[7]
        c0 = 384 + slot * 4
        for jj in range(2):
            o = ps[:, c0 + jj * 2:c0 + jj * 2 + 2]
            for kc in range(NCH):
                self.mm(o, W[:, kc, jj * 128:(jj + 1) * 128], self.SC[:, :, kc],
                        start=(kc == 0), stop=(kc == NCH - 1))
        if prefetch:
            self.mod_load()
        if defer:
            return lambda: self._mod_evac(l, pc, c0)
        self._mod_evac(l, pc, c0)

    def _mod_evac(self, l, pc, c0):
        ps = self.ps[7]
        j0 = pc * 2
        self.tt(self.MODT[:, l, j0:j0 + 2, :], ps[:, c0:c0 + 4].rearrange("p (j w) -> p j w", j=2),
                self.MBT[:, l, j0:j0 + 2].unsqueeze(2).broadcast_to([128, 2, 2]), ALU.add)
        if pc % 12 == 11:
            s = pc // 12
            for w in range(2):
                self.stt(self.AM[:, l, s, w, :], self.MODT[:, l, (3 * s + 1) * 8:(3 * s + 2) * 8, w], 1.0,
                         self.NG[:, l, s, :], ALU.add, ALU.mult)
                gsrc = self.MODT[:, l, (3 * s + 2) * 8:(3 * s + 3) * 8, w]
                if s == 1:
                    self.cp(self.GH[:, l, s, w, :], gsrc)
                else:
                    self.ts(self.GH[:, l, s, w, :], gsrc, 0.5, None, ALU.mult)

    def mod_ensure(self, l, sgrp):
        while self.mod_queue and self.mod_queue[0] <= (l, 12 * sgrp + 11):
            self.mod_piece()

    def shift(self, l, s, w, c):
        j = (3 * s) * 8 + c
        return self.MODT[:, l, j, w:w + 1]

    def load_x(self):
        d = self.dram
        P1 = self.P1_ffn_end
        stg = [self.V(self.HT_off + s * 4096, F32, [D]) for s in range(2)]
        for tb in range(NTOK // 128):
            tile, sub = tb // 4, tb % 4
            sl = stg[tb % 2]
            self.dma(sl, d["xin"][tb * 128:(tb + 1) * 128, :], "xin%d" % (tb % 2))
            for hb in range(2):
                ps = self.ps[(tb % 2) * 2 + hb]
                for cc in range(4):
                    c = hb * 4 + cc
                    self.tr(ps[:, cc * 128:(cc + 1) * 128], sl[:, c * 128:(c + 1) * 128], self.IDF)
                dst = self.XT[:, tile, hb * 4:(hb + 1) * 4, sub * 128:(sub + 1) * 128]
                src = ps[:, :].rearrange("p (c t) -> p c t", c=4)
                if hb == 0:
                    self.cp(dst, src, eng="act")
                else:
                    self.cp(dst, src, eng="dve")

    def who(self, tile):
        return 0 if tile < 4 else 1

    def adanorm(self, tile, l, s, dst=None):
        w = self.who(tile)
        ms = self.ps[7][:, 0:TT] if False else self.ps[7]
        msv = self.ps[7][:, 0:TT]
        for half in range(2):
            self.act(self.SQ, self.XT[:, tile, half * 4:(half + 1) * 4, :], AF.Square)
            for cc in range(4):
                c = half * 4 + cc
                self.mm(msv, self.ONES_D, self.SQ[:, cc, :], start=(c == 0), stop=(c == NCH - 1))
        r = self.RSTD[:, self.rot("rstd", 2), :]
        self.rsqrt(r, msv)
        for c in range(NCH):
            t = self.TMP[:, self.rot("tmp", 2), :]
            self.stt(t, self.XT[:, tile, c, :], self.AM[:, l, s, w, c:c + 1], r, ALU.mult, ALU.mult)
            out = self.HT[:, tile, c, :] if dst is None else dst[:, c, :]
            self.act(out, t, AF.Identity, bias=self.shift(l, s, w, c), scale=1.0)

    def ffn_load(self, l, hs, fg):
        d = self.dram
        slot = fg % 2
        w1 = d["ffn_w1"][l, hs].rearrange("(kc p) f -> p kc f", p=128)[:, :, fg * 256:(fg + 1) * 256]
        w3 = d["ffn_w3"][l, hs].rearrange("(kc p) f -> p kc f", p=128)[:, :, fg * 256:(fg + 1) * 256]
        w2 = d["ffn_w2"][l, hs][fg * 256:(fg + 1) * 256, :].rearrange("(fc p) n -> p fc n", p=128)
        sk = self.cfg.get("skip_load", 0)
        if sk in (0, 2):
            self.dma(self.W1[slot], w1, "w1_%d" % slot, eng="pool")
            self.dma(self.W3[slot], w3, "w3_%d" % slot, eng="pool")
        if sk in (0, 3):
            self.dma(self.W2[slot], w2, "w2_%d" % slot, eng="pool")

    def ffn(self, l, hs):
        s = 0 if hs == 0 else 2
        ntiles = self.cfg.get("ntiles", NT)
        while len(self.mod_loaded) < 2 and self.mod_lq:
            self.mod_load()
        if self.preloaded:
            self.preloaded = False
        else:
            self.ffn_load(l, hs, 0)
            self.ffn_load(l, hs, 1)
        for tile in range(ntiles):
            if tile in self.prenormed:
                continue
            self.adanorm(tile, l, s)
        self.prenormed = set()
        if self.cfg.get("fstop") == "adanorm":
            return
        pend = None
        step = 0

        def do_o(p):
            fg, tile, par = p
            slot = fg % 2
            w = self.who(tile)
            for dc in range(NCH):
                o = self.ps[4 + self.rot("obank", 4)]
                for fc in range(2):
                    self.mm(o, self.W2[slot][:, fc, dc * 128:(dc + 1) * 128], self.G[:, par, fc, :],
                            start=(fc == 0), stop=(fc == 1))
                x = self.XT[:, tile, dc, :]
                if dc in (0, 1, 2, 4, 6):
                    self.stt(x, o, self.GH[:, l, s, w, dc:dc + 1], x, ALU.mult, ALU.add)
                else:
                    t = self.TMP[:, self.rot("tmpo", 2), :]
                    self.act(t, o, AF.Identity, scale=self.GH[:, l, s, w, dc:dc + 1])
                    self.tt(x, x, t, ALU.add)

        nfg = self.cfg.get("nfg", NFG)
        for fg in range(nfg):
            slot = fg % 2
            for tile in range(ntiles):
                par = step % 2
                step += 1
                for fc in range(2):
                    a = self.ps[fc]
                    b = self.ps[2 + fc]
                    for kc in range(NCH):
                        self.mm(a, self.W1[slot][:, kc, fc * 128:(fc + 1) * 128], self.HT[:, tile, kc, :],
                                start=(kc == 0), stop=(kc == NCH - 1))
                    for kc in range(NCH):
                        self.mm(b, self.W3[slot][:, kc, fc * 128:(fc + 1) * 128], self.HT[:, tile, kc, :],
                                start=(kc == 0), stop=(kc == NCH - 1))
                    sa = self.SA[:, fc, :]
                    self.sigmoid(sa, a)
                    if fc == 0:
                        self.tt(sa, sa, a, ALU.mult)
                        self.tt(self.G[:, par, fc, :], sa, b, ALU.mult)
                if pend is not None:
                    do_o(pend)
                    if pend[1] == ntiles - 1 and pend[0] + 2 < nfg:
                        self.ffn_load(l, hs, pend[0] + 2)
                sa = self.SA[:, 1, :]
                self.tt(sa, sa, self.ps[1], ALU.mult)
                self.tt(self.G[:, par, 1, :], sa, self.ps[3], ALU.mult)
                pend = (fg, tile, par)
            evs = [self.mod_piece(defer=True) for _ in range(2)]
            for ev in evs:
                if ev is not None:
                    ev()
        if self.cfg.get("fstop") in ("ab", "sig", "g"):
            return
        do_o(pend)

    def final_out(self):
        d = self.dram
        ntiles = self.cfg.get("ntiles", NT)
        YT = self.V(self.HT_off, F32, [NCH, TT])
        stg = [self.V(self.HT_off + 16384 + s * 4096, F32, [D]) for s in range(2)]
        for tile in range(ntiles):
            msv = self.ps[7][:, 0:TT]
            for half in range(2):
                self.act(self.SQ, self.XT[:, tile, half * 4:(half + 1) * 4, :], AF.Square)
                for cc in range(4):
                    c = half * 4 + cc
                    self.mm(msv, self.ONES_D, self.SQ[:, cc, :], start=(c == 0), stop=(c == NCH - 1))
            r = self.RSTD[:, self.rot("rstd", 2), :]
            self.rsqrt(r, msv)
            for c in range(NCH):
                self.stt(YT[:, c, :], self.XT[:, tile, c, :], self.FG[:, c:c + 1], r, ALU.mult, ALU.mult)
            for sub in range(4):
                tb = tile * 4 + sub
                sl = stg[tb % 2]
                for hb in range(2):
                    ps = self.ps[(tb % 2) * 2 + hb]
                    for cc in range(4):
                        c = hb * 4 + cc
                        self.tr(ps[:, cc * 128:(cc + 1) * 128], YT[:, c, sub * 128:(sub + 1) * 128], self.IDF)
                    if hb == 0:
                        self.cp(sl[:, 0:512], ps[:, :], eng="act")
                    else:
                        self.cp(sl[:, 512:1024], ps[:, :], eng="dve")
                self.dma(d["y"][tb * 128:(tb + 1) * 128, :], sl, "yout%d" % (tb % 2), final=True)

    def scan(self, out, d0, d1):
        return self.S.add("dve", lambda e: e.tensor_tensor_scan(out, d0, d1, 0.0, ALU.mult, ALU.add),
                          outs=[out], ins=[d0, d1])

    def mixer_l0(self):
        ntiles = self.cfg.get("ntiles", NT)
        self.ts(self.NBG[:, 0, :], self.BGF, -1.0, None, ALU.mult)
        self.ts(self.NBG[:, 1, :], self.BGB, -1.0, None, ALU.mult)
        groups = []
        if ntiles >= 4:
            groups.append(("s", [0, 1, 2, 3], self.HT_off + 4 * 8192))
        for t in range(4, ntiles):
            groups.append(("p", [t], self.HT_off))
        for kind, tiles, scr in groups:
            for tile in tiles:
                self.adanorm(tile, 0, 1)
            if self.cfg.get("fnet", True):
                self.fnet(kind, tiles, scr)
            if self.cfg.get("gla", True):
                for p in range(2):
                    self.gla_pair(kind, tiles, scr, p)

    def fnet(self, kind, tiles, scr):
        d = self.dram
        WA, P1 = self.WA_off, self.P1_off
        WU = self.V(WA, BF16, [NCH, 512])
        WOF = self.V(WA + 8192, BF16, [4, D])
        DFTC = self.V(WA + 16384, BF16, [2, 128])
        self.dma(WU, d["l0_w_in"].rearrange("(kc q) n -> q kc n", q=128)[:, :, 1568:2080], "l0wu", eng="pool")
        self.dma(WOF, d["l0_w_out"][512:1024, :].rearrange("(g q) n -> q g n", q=128), "l0wof", eng="pool")
        self.dma(DFTC, d["dftc"].rearrange("cs c k -> c cs k"), "dftc")
        UT = self.V(scr, BF16, [16, 512])
        for ti, tile in enumerate(tiles):
            for blk in range(4):
                ps = self.ps[blk % 2]
                for kc in range(NCH):
                    self.mm(ps, self.HT[:, tile, kc, blk * 128:(blk + 1) * 128], WU[:, kc, :],
                            start=(kc == 0), stop=(kc == NCH - 1))
                self.cp(UT[:, ti * 4 + blk, :], ps, eng=("act" if blk % 2 else "dve"))
        DT = [self.V(P1 + i * 8192, BF16, [4, 2, 512]) for i in range(2)]
        ABT = self.V(P1 + 16384, BF16, [2, 4, 512])
        FN = self.V(P1 + 24576, BF16, [4, 512])
        DT256 = self.V(P1, BF16, [2, 2, 256])
        if kind == "p":
            for cs_ in range(2):
                self.dma(DT256[:, :, cs_, :], d["dft256"][cs_].rearrange("(tb t) tp -> t tb tp", t=128), "dt256_%d" % cs_)
        for j, tile in enumerate(tiles):
            if kind == "s":
                for qd in range(4):
                    slot = self.rot("dt", 2)
                    src = d["dft2k"][:, qd * 512:(qd + 1) * 512, j * 512:(j + 1) * 512]
                    for cs_ in range(2):
                        self.dma(DT[slot][:, :, cs_, :], src[cs_].rearrange("(tb t) tp -> t tb tp", t=128),
                                 "dt%d_%d" % (slot, cs_))
                    for tb4 in range(4):
                        tb = qd * 4 + tb4
                        for g in range(4):
                            lhsT = UT[:, tb, g * 128:(g + 1) * 128]
                            self.mm(self.ps[g], lhsT, DT[slot][:, tb4, 0, :], start=(tb == 0), stop=(tb == 15))
                            self.mm(self.ps[4 + g], lhsT, DT[slot][:, tb4, 1, :], start=(tb == 0), stop=(tb == 15))
            else:
                for sq in range(2):
                    for tb in range(2):
                        for g in range(4):
                            lhsT = UT[:, sq * 2 + tb, g * 128:(g + 1) * 128]
                            cs = slice(sq * 256, (sq + 1) * 256)
                            self.mm(self.ps[g][:, cs], lhsT, DT256[:, tb, 0, :], start=(tb == 0), stop=(tb == 1))
                            self.mm(self.ps[4 + g][:, cs], lhsT, DT256[:, tb, 1, :], start=(tb == 0), stop=(tb == 1))
            for g in range(4):
                self.cp(ABT[:, 0, g, :], self.ps[g], eng="act")
                self.cp(ABT[:, 1, g, :], self.ps[4 + g], eng="dve")
            for g in range(4):
                y = self.ps[g]
                self.mm(y, DFTC[:, 0, :], ABT[:, 0, g, :], start=True, stop=False)
                self.mm(y, DFTC[:, 1, :], ABT[:, 1, g, :], start=False, stop=True)
                self.cp(FN[:, g, :], y, eng=("act" if g % 2 else "dve"))
            w = self.who(tile)
            for dc in range(NCH):
                o = self.ps[4 + self.rot("opb0", 4)]
                for g in range(4):
                    self.mm(o, WOF[:, g, dc * 128:(dc + 1) * 128], FN[:, g, :], start=(g == 0), stop=(g == 3))
                x = self.XT[:, tile, dc, :]
                self.stt(x, o, self.GH[:, 0, 1, w, dc:dc + 1], x, ALU.mult, ALU.add)

    def gla_pair(self, kind, tiles, scr, p):
        d = self.dram
        WA, P1 = self.WA_off, self.P1_off
        V = self.V
        WQp = V(WA, BF16, [NCH, 128])
        WKp = V(WA + 2048, BF16, [NCH, 128])
        WVp = V(WA + 4096, BF16, [NCH, 256])
        WGp = V(WA + 8192, BF16, [NCH, 256])
        WLF = V(WA + 12288, BF16, [NCH, 16])
        WLB = V(WA + 12544, BF16, [NCH, 16])
        WGF = V(WA + 12800, BF16, [128])
        WGB = V(WA + 13056, BF16, [128])
        WOp = V(WA + 13312, BF16, [2, D])
        src = d["l0_w_in"].rearrange("(kc q) n -> q kc n", q=128)
        self.dma(WLF, src[:, :, 1536:1552], "g_wlf", eng="pool")
        self.dma(WKp, src[:, :, 256 + p * 128:256 + (p + 1) * 128], "g_wk", eng="pool")
        self.dma(WGF[0:16, :], d["l0_w_gf"][:, p * 128:(p + 1) * 128], "g_wgf", eng="pool")
        self.dma(WVp, src[:, :, 512 + p * 256:512 + (p + 1) * 256], "g_wv", eng="pool")
        self.dma(WQp, src[:, :, p * 128:(p + 1) * 128], "g_wq", eng="pool")
        self.dma(WLB, src[:, :, 1552:1568], "g_wlb", eng="pool")
        self.dma(WGB[0:16, :], d["l0_w_gb"][:, p * 128:(p + 1) * 128], "g_wgb", eng="pool")
        self.dma(WGp, src[:, :, 1024 + p * 256:1024 + (p + 1) * 256], "g_wg", eng="pool")
        self.dma(WOp, d["l0_w_out"][p * 256:(p + 1) * 256, :].rearrange("(h q) n -> q h n", q=128), "g_wo", eng="pool")
        o = P1
        SP = V(o, F32, [TT]); o += 2048
        PSP = V(o, F32, [TT]); o += 2048
        T3 = V(o, F32, [TT]); o += 2048
        E1 = V(o, F32, [TT]); o += 2048
        E2 = V(o, F32, [TT]); o += 2048
        QDF = V(o, BF16, [TT]); o += 1024
        KIF = V(o, BF16, [TT]); o += 1024
        QDB = V(o, BF16, [TT]); o += 1024
        KIB = V(o, BF16, [TT]); o += 1024
        KE = V(o, BF16, [TT]); o += 1024
        KETOK = V(o, BF16, [4, 128]); o += 1024
        VTOK = V(o, BF16, [4, 256]); o += 2048
        LFT = V(o, BF16, [TT]); o += 1024
        LBT = LFT
        ATF = V(o, BF16, [2, 128]); o += 512
        ATB = V(o, BF16, [2, 128]); o += 512
        SALL = V(o, F32, [8, 128]); o += 4096
        INIT = V(o, F32, [128]); o += 512
        ZERO = V(o, F32, [128]); o += 512
        SB16 = V(o, BF16, [8, 128]); o += 2048
        GL = V(o, BF16, [2, TT]); o += 2048
        SQ16 = V(o, BF16, [TT]); o += 1024
        assert o <= self.arena_bytes, o
        SF16 = V(scr, BF16, [33, 128])
        self.memset(ZERO, 0.0)
        ps = self.ps
        c3 = lambda a: a.rearrange("q (c t) -> q c t", c=8)
        nb_f = self.NBG[:, 0, p:p + 1]
        nb_b = self.NBG[:, 1, p:p + 1]
        ktb = ps[5].bitcast(BF16)
        ktb3 = ps[3].bitcast(BF16)

        def proj_small(dst, W, tile):
            lp = ps[2][0:16, :]
            for kc in range(NCH):
                self.mm(lp, W[:, kc, :], self.HT[:, tile, kc, :], start=(kc == 0), stop=(kc == NCH - 1))
            self.cp(dst[0:16, :], lp, eng="act")

        def proj(bank, W, tile):
            for kc in range(NCH):
                self.mm(bank, W[:, kc, :], self.HT[:, tile, kc, :], start=(kc == 0), stop=(kc == NCH - 1))

        def softplus_neg(glog, nb):
            self.act(SP, glog, AF.Exp, bias=nb, scale=-1.0)
            self.act(SP, SP, AF.Ln, bias=1.0, scale=1.0)

        def ke_tok(tb):
            for blk in range(4):
                self.tr(tb[:, blk * 128:(blk + 1) * 128], KE[:, blk * 128:(blk + 1) * 128], self.IDB)
            self.cp(KETOK, tb[:, 0:512].rearrange("q (b c) -> q b c", b=4))

        def v_tok(tile):
            for blk in range(4):
                vp = ps[6 + blk // 2][:, (blk % 2) * 256:(blk % 2 + 1) * 256]
                for kc in range(NCH):
                    self.mm(vp, self.HT[:, tile, kc, blk * 128:(blk + 1) * 128], WVp[:, kc, :],
                            start=(kc == 0), stop=(kc == NCH - 1))
            self.cp(VTOK[:, 0:2, :], ps[6].rearrange("q (b c) -> q b c", b=2), eng="dve")
            self.cp(VTOK[:, 2:4, :], ps[7].rearrange("q (b c) -> q b c", b=2), eng="act")

        def delta(bank, blk, cc):
            dl = bank[:, 0:128]
            for h2 in range(2):
                self.mm(dl[h2 * 64:(h2 + 1) * 64, :], KETOK[cc * 64:(cc + 1) * 64, blk, h2 * 64:(h2 + 1) * 64],
                        VTOK[cc * 64:(cc + 1) * 64, blk, h2 * 128:(h2 + 1) * 128], start=True, stop=True)
            return dl

        def st_out(name, S32, seq):
            self.dma(d[name][seq, 2 * p:2 * p + 2].rearrange("h k v -> (h k) v"), S32, "so_%s%d" % (name, seq % 2),
                     final=True)

        def deltas(banks):
            dls = []
            for c in range(8):
                dl = banks[c % 2][:, (c // 2) * 128:(c // 2 + 1) * 128]
                blk, cc = c // 2, c % 2
                for h2 in range(2):
                    self.mm(dl[h2 * 64:(h2 + 1) * 64, :], KETOK[cc * 64:(cc + 1) * 64, blk, h2 * 64:(h2 + 1) * 64],
                            VTOK[cc * 64:(cc + 1) * 64, blk, h2 * 128:(h2 + 1) * 128], start=True, stop=True)
                dls.append(dl)
            return dls

        for ti, tile in enumerate(tiles):
            proj_small(LFT, WLF, tile)
            proj(ps[1], WKp, tile)
            self.mm(ps[3], WGF[0:16, :], LFT[0:16, :])
            v_tok(tile)
            softplus_neg(ps[3], nb_f)
            self.scan(PSP, self.SCANM, SP)
            self.tt(c3(T3), c3(PSP), c3(PSP)[:, :, 63:64].broadcast_to([128, 8, 64]), ALU.subtract)
            self.act(E2, T3, AF.Exp, scale=1.0 / 16.0)
            self.tt(KE, ps[1], E2, ALU.mult)
            self.act(E1, PSP, AF.Exp, scale=-1.0 / 16.0)
            ke_tok(ktb)
            dls = deltas([ps[0], ps[4]])
            n0 = ti * 8
            if kind == "s" and ti == 0:
                self.dma(INIT, d["st_f"][2 * p:2 * p + 2].rearrange("h k v -> (h k) v"), "st_in")
                self.cp(SF16[:, 0, :], INIT, eng="act")
            for c in range(8):
                if kind == "s":
                    s_in = INIT if (ti == 0 and c == 0) else SALL[:, (c - 1) % 8, :]
                else:
                    s_in = ZERO if c % 4 == 0 else SALL[:, c - 1, :]
                self.stt(SALL[:, c, :], s_in, E1[:, c * 64 + 63:c * 64 + 64], dls[c], ALU.mult, ALU.add)
            if kind == "s":
                self.cp(SF16[:, n0 + 1:n0 + 9, :], SALL, eng="act")
            else:
                self.cp(SF16[:, 1:8, :], SALL[:, 0:7, :], eng="act")
                self.memset(SF16[:, 0, :], 0.0)
                self.memset(SF16[:, 4, :], 0.0)
                st_out("o_sf", SALL[:, 3, :], (tile - 4) * 2)
                st_out("o_sf", SALL[:, 7, :], (tile - 4) * 2 + 1)
        for ti in range(len(tiles) - 1, -1, -1):
            tile = tiles[ti]
            w = self.who(tile)
            proj(ps[0], WQp, tile)
            proj(ps[1], WKp, tile)
            proj_small(LFT, WLF, tile)
            self.mm(ps[3], WGF[0:16, :], LFT[0:16, :])
            proj_small(LBT, WLB, tile)
            v_tok(tile)
            for h2 in range(2):
                for kc in range(NCH):
                    self.mm(ps[6 + h2], WGp[:, kc, h2 * 128:(h2 + 1) * 128], self.HT[:, tile, kc, :],
                            start=(kc == 0), stop=(kc == NCH - 1))
            softplus_neg(ps[3], nb_f)
            self.scan(PSP, self.SCANM, SP)
            self.act(E1, PSP, AF.Exp, scale=-1.0 / 16.0)
            self.act(E2, PSP, AF.Exp, scale=1.0 / 16.0)
            self.stt(QDF, ps[0], 0.125, E1, ALU.mult, ALU.mult)
            self.tt(KIF, ps[1], E2, ALU.mult)
            self.mm(ps[3], WGB[0:16, :], LBT[0:16, :])
            softplus_neg(ps[3], nb_b)
            self.scan(PSP, self.SCANM, SP)
            self.tt(T3, SP, PSP, ALU.subtract)
            self.tt(c3(SP), c3(T3), c3(PSP)[:, :, 63:64].broadcast_to([128, 8, 64]), ALU.add)
            self.act(E1, SP, AF.Exp, scale=-1.0 / 16.0)
            self.act(E2, SP, AF.Exp, scale=1.0 / 16.0)
            self.stt(QDB, ps[0], 0.125, E1, ALU.mult, ALU.mult)
            self.tt(KIB, ps[1], E2, ALU.mult)
            self.act(T3, T3, AF.Exp, scale=1.0 / 16.0)
            self.tt(KE, ps[1], T3, ALU.mult)
            ke_tok(ktb3)
            dls = deltas([ps[0], ps[1]])
            last_t = (ti == len(tiles) - 1)
            if kind == "s":
                if last_t:
                    self.dma(INIT, d["st_b"][2 * p:2 * p + 2].rearrange("h k v -> (h k) v"), "st_in")
                    s7 = INIT
                else:
                    s7 = SALL[:, 0, :]
                self.cp(SB16[:, 7, :], s7, eng="act")
            for c in range(7, -1, -1):
                if kind == "s":
                    s_in = s7 if c == 7 else SALL[:, c + 1, :]
                else:
                    s_in = ZERO if c % 4 == 3 else SALL[:, c + 1, :]
                self.stt(SALL[:, c, :], s_in, E1[:, c * 64:c * 64 + 1], dls[c], ALU.mult, ALU.add)
            self.cp(SB16[:, 0:7, :], SALL[:, 1:8, :], eng="act")
            if kind == "p":
                self.memset(SB16[:, 3, :], 0.0)
                self.memset(SB16[:, 7, :], 0.0)
                st_out("o_sb", SALL[:, 0, :], (tile - 4) * 2)
                st_out("o_sb", SALL[:, 4, :], (tile - 4) * 2 + 1)
            for blk in range(3, -1, -1):
                bs = slice(blk * 128, (blk + 1) * 128)
                ats, sls = [], []
                for h2 in range(2):
                    hs = slice(h2 * 64, (h2 + 1) * 64)
                    at = ps[2 + self.rot("atb", 2)]
                    self.mm(at[:, 0:128], KIF[hs, bs], QDF[hs, bs])
                    self.mm(at[:, 128:256], KIB[hs, bs], QDB[hs, bs])
                    ats.append(at)
                for h2 in range(2):
                    sl = self.rot("afs", 2)
                    self.tt(ATF[:, sl, :], ats[h2][:, 0:128], self.MF, ALU.mult)
                    self.tt(ATB[:, sl, :], ats[h2][:, 128:256], self.MB, ALU.mult)
                    sls.append(sl)
                for h2 in range(2):
                    hs = slice(h2 * 64, (h2 + 1) * 64)
                    sl = sls[h2]
                    ot = ps[4 + h2]
                    vl = VTOK[:, blk, h2 * 128:(h2 + 1) * 128]
                    self.mm(ot[:, bs], vl, ATF[:, sl, :], start=True, stop=False)
                    self.mm(ot[:, bs], vl, ATB[:, sl, :], start=False, stop=False)
                    for cc in range(2):
                        c = blk * 2 + cc
                        n = ti * 8 + c
                        cs = slice(c * 64, (c + 1) * 64)
                        self.mm(ot[:, cs], SF16[hs, n, :], QDF[hs, cs], start=False, stop=False)
                        self.mm(ot[:, cs], SB16[hs, c, :], QDB[hs, cs], start=False, stop=(cc == 1))
            for h2 in range(2):
                og = ps[6 + h2]
                ot = ps[4 + h2]
                self.act(SQ16, ot, AF.Square)
                self.mm(ps[2], self.ONES_DV, SQ16)
                self.rsqrt(SP, ps[2])
                self.stt(T3, ot, self.GHEAD[:, 0:1], SP, ALU.mult, ALU.mult)
                self.sigmoid(E1, og)
                self.tt(E2, og, E1, ALU.mult)
                self.tt(GL[:, h2, :], T3, E2, ALU.mult)
            for dc in range(NCH):
                ob = ps[self.rot("opb1", 2)]
                for h2 in range(2):
                    self.mm(ob, WOp[:, h2, dc * 128:(dc + 1) * 128], GL[:, h2, :], start=(h2 == 0), stop=(h2 == 1))
                x = self.XT[:, tile, dc, :]
                self.stt(x, ob, self.GH[:, 0, 1, w, dc:dc + 1], x, ALU.mult, ALU.add)
            if p == 1 and kind == "p" and self.cfg.get("early_ada", True):
                self.adanorm(tile, 0, 2)
                self.prenormed.add(tile)

    def mixer_l1(self):
        d = self.dram
        l, s = 1, 1
        ntiles = self.cfg.get("ntiles", NT)
        for tile in range(ntiles):
            self.adanorm(tile, l, s)
        WA = self.WA_off
        WQ = self.V(WA, BF16, [NCH, 512])
        WK = self.V(WA + 8192, BF16, [NCH, 128])
        WV = self.V(WA + 10240, BF16, [NCH, 128])
        WO = self.V(WA + 12288, BF16, [4, D])
        P = self.P1_off
        KT = self.V(P, BF16, [2048]); P += 4096
        VT = self.V(P, BF16, [16, 128]); P += 4096
        QT = self.V(P, BF16, [4, TT]); P += 4096
        ROPE = self.V(P, F32, [2, TT]); P += 4096
        CKT = self.V(WA + 20480, BF16, [2, 256])
        CV = self.V(WA + 21504, BF16, [2, 256])
        PT = [self.V(P + i * 1024, BF16, [TT]) for i in range(4)]; P += 4096
        OB = self.V(P, BF16, [4, TT]); P += 4096
        DENR = self.V(P, F32, [TT]); P += 2048
        KVO = self.V(P, F32, [4, 2, 128]); P += 4096
        QB = self.V(WA + 22528, BF16, [TT])
        assert P <= self.arena_bytes, P
        for p in range(2):
            for e in range(2):
                self.dma(self.SINKE[e * 64:(e + 1) * 64, p, :],
                         d["sink"][2 * p + e:2 * p + e + 1, :].partition_broadcast(64), "sink%d%d" % (p, e))
        self.act(self.SINKE, self.SINKE, AF.Exp)
        has_sample = ntiles >= 4
        groups = []
        if has_sample:
            groups.append(("s", [0, 1, 2, 3]))
        for t in range(4, ntiles):
            groups.append(("p", [t]))
        for p in range(2):
            wsrc = d["l1_w_in"].rearrange("(kc q) n -> q kc n", q=128)
            self.dma(WK, wsrc[:, :, 1024 + p * 128:1024 + (p + 1) * 128], "l1wk", eng="pool")
            self.dma(WV, wsrc[:, :, 1280 + p * 128:1280 + (p + 1) * 128], "l1wv", eng="pool")
            if has_sample:
                self.dma(CV, d["cv"].rearrange("(b t) c -> t b c", t=128), "cvf", eng="pool")
            wq_src = wsrc[:, :, p * 512:(p + 1) * 512].rearrange("q k (e g dd) -> q k e g dd", e=2, g=4)
            wq_dst = WQ.rearrange("q k (g e dd) -> q k g e dd", g=4, e=2)
            for g in range(4):
                for e in range(2):
                    self.dma(wq_dst[:, :, g, e, :], wq_src[:, :, e, g, :], "l1wq%d%d" % (g, e), eng="pool")
            wo = d["l1_w_out"].rearrange("(pp e g dd) n -> pp e dd g n", pp=2, e=2, g=4)
            for e in range(2):
                self.dma(WO[e * 64:(e + 1) * 64, :, :], wo[p, e], "l1wo%d" % e, eng="pool")
            if has_sample:
                ckf = self.V(self.P1_off + 12288, F32, [2, 128])
                self.dma(ckf, d["ck"].rearrange("(b t) c -> t b c", t=128)[:, :, p * 128:(p + 1) * 128], "ckf")
                for b in range(2):
                    ps = self.ps[6]
                    self.tr(ps[:, b * 128:(b + 1) * 128], ckf[:, b, :], self.IDF)
                self.cp(CKT[:, p, :], self.ps[6][:, 0:256])
            L1S = self.cfg.get("l1stop", 99)
            if L1S <= 1:
                return
            for kind, tiles in groups:
                w = 0 if kind == "s" else 1
                for ti, tile in enumerate(tiles):
                    if kind == "s":
                        self.dma(ROPE, d["rope"][:, :, tile * TT:(tile + 1) * TT].rearrange("c q t -> q c t"),
                                 "rope")
                    kps = self.ps[6]
                    for kc in range(NCH):
                        self.mm(kps, WK[:, kc, :], self.HT[:, tile, kc, :], start=(kc == 0), stop=(kc == NCH - 1))
                    kdst = KT[:, ti * TT:(ti + 1) * TT]
                    P1S = self.cfg.get("p1", "")
                    if P1S == "k":
                        self.cp(kdst, kps, eng="act")
                        return
                    if kind == "s":
                        self.cp(QB, kps, eng="act")
                        rps = self.ps[7]
                        self.mm(rps, self.ROT, QB)
                        self.tt(DENR, kps, ROPE[:, 0, :], ALU.mult)
                        self.tt(KVO.rearrange("q a b c -> q (a b c)")[:, 0:TT], rps, ROPE[:, 1, :], ALU.mult)
                        self.tt(kdst, DENR, KVO.rearrange("q a b c -> q (a b c)")[:, 0:TT], ALU.add)
                    else:
                        self.cp(kdst, kps, eng="act")
                    if P1S == "rope":
                        return
                    vps = self.ps[0]
                    for blk in range(4):
                        for kc in range(NCH):
                            self.mm(vps[:, blk * 128:(blk + 1) * 128], self.HT[:, tile, kc, blk * 128:(blk + 1) * 128],
                                    WV[:, kc, :], start=(kc == 0), stop=(kc == NCH - 1))
                    if P1S == "v1":
                        return
                    self.cp(VT[:, ti * 4:(ti + 1) * 4, :], vps.rearrange("q (b c) -> q b c", b=4))
                    if P1S == "v2" and ti + 1 >= self.cfg.get("p1n", 1):
                        return
                    if kind == "p":
                        self.cp(KVO[:, :, 1, :], vps.rearrange("q (b c) -> q b c", b=4), eng="act")
                        k2 = self.ps[1]
                        for blk in range(4):
                            for kc in range(NCH):
                                self.mm(k2[:, blk * 128:(blk + 1) * 128],
                                        self.HT[:, tile, kc, blk * 128:(blk + 1) * 128],
                                        WK[:, kc, :], start=(kc == 0), stop=(kc == NCH - 1))
                        self.cp(KVO[:, :, 0, :], k2.rearrange("q (b c) -> q b c", b=4))
                        t0 = (tile - 4) * TT
                        self.dma(d["o_kc"][t0:t0 + TT, p * 128:(p + 1) * 128].rearrange("(b t) c -> t b c", t=128),
                                 KVO[:, :, 0, :], "okc", final=True)
                        self.dma(d["o_vc"][t0:t0 + TT, p * 128:(p + 1) * 128].rearrange("(b t) c -> t b c", t=128),
                                 KVO[:, :, 1, :], "ovc", final=True)
                if L1S <= 2:
                    return
                for ti, tile in enumerate(tiles):
                    if kind == "s":
                        self.dma(ROPE, d["rope"][:, :, tile * TT:(tile + 1) * TT].rearrange("c q t -> q c t"),
                                 "rope")
                    if self.cfg.get("p2", "") == "r":
                        return
                    for g in range(4):
                        qps = self.ps[(g % 2) * 2]
                        for kc in range(NCH):
                            self.mm(qps, WQ[:, kc, g * 128:(g + 1) * 128], self.HT[:, tile, kc, :],
                                    start=(kc == 0), stop=(kc == NCH - 1))
                        P2S = self.cfg.get("p2", "")
                        if P2S == "q0":
                            return
                        if P2S == "q1":
                            self.act(QT[:, g, :], qps, AF.Identity, scale=0.125)
                            return
                        if kind == "s":
                            self.cp(QB, qps, eng="act")
                            rps = self.ps[(g % 2) * 2 + 1]
                            self.mm(rps, self.ROT, QB)
                            t2 = KVO.rearrange("q a b c -> q (a b c)")[:, 0:TT]
                            self.tt(DENR, qps, ROPE[:, 0, :], ALU.mult)
                            self.tt(t2, rps, ROPE[:, 1, :], ALU.mult)
                            self.tt(DENR, DENR, t2, ALU.add)
                            self.act(QT[:, g, :], DENR, AF.Identity, scale=0.125)
                            if P2S == "q2":
                                return
                        else:
                            self.act(QT[:, g, :], qps, AF.Identity, scale=0.125)
                    if L1S <= 3:
                        return
                    steps = []
                    for qb in range(4):
                        if kind == "s":
                            i = ti * 4 + qb
                            kbs = [("c", 0, None), ("c", 1, None)]
                            if i - 1 >= 0:
                                kbs.append(("l", i - 1, self.MNLO))
                            kbs.append(("l", i, None))
                            if i + 1 <= 15:
                                kbs.append(("l", i + 1, self.MNHI))
                        else:
                            sq = qb // 2
                            kbs = [("l", sq * 2, None), ("l", sq * 2 + 1, None)]
                        for ki, kb_ in enumerate(kbs):
                            steps.append((qb, ki, len(kbs), kb_))

                    def scores(stp):
                        qb, ki, nk, (src, kb, mask) = stp
                        qsl = slice(qb * 128, (qb + 1) * 128)
                        sts, pts = [], []
                        for e in range(2):
                            hs = slice(e * 64, (e + 1) * 64)
                            st = self.ps[self.rot("stbank", 4)]
                            kl = CKT[hs, p, kb * 128:(kb + 1) * 128] if src == "c" else KT[hs, kb * 128:(kb + 1) * 128]
                            self.mm(st, kl, QT[hs, :, qsl], start=True, stop=(mask is None))
                            sts.append(st)
                        if mask is not None:
                            for e in range(2):
                                self.mm(sts[e], self.IDB, mask, start=False, stop=True)
                        for e in range(2):
                            pt = PT[self.rot("pt", 4)]
                            self.act(pt, sts[e], AF.Exp)
                            pts.append(pt)
                        return pts

                    def pv(stp, pts):
                        qb, ki, nk, (src, kb, mask) = stp
                        qsl = slice(qb * 128, (qb + 1) * 128)
                        par = qb % 2
                        OT = self.ps[4 + 2 * par]
                        DEN = self.ps[5 + 2 * par]
                        first, last = (ki == 0), (ki == nk - 1)
                        for e in range(2):
                            hs = slice(e * 64, (e + 1) * 64)
                            if src == "c":
                                vl = CV[:, kb, (2 * p + e) * 64:(2 * p + e + 1) * 64]
                            else:
                                vl = VT[:, kb, e * 64:(e + 1) * 64]
                            self.mm(OT[hs, :], vl, pts[e], start=first, stop=last)
                        for e in range(2):
                            hs = slice(e * 64, (e + 1) * 64)
                            self.mm(DEN[hs, :], self.ONE1[:, 0:64], pts[e], start=first, stop=last)
                        if last:
                            self.tt(DENR.rearrange("q (g t) -> q g t", g=4), DEN.rearrange("q (g t) -> q g t", g=4),
                                    self.SINKE[:, p, :].unsqueeze(2).broadcast_to([128, 4, 128]), ALU.add)
                            self.S.add("dve", lambda e_, o=DENR: e_.reciprocal(o, o), outs=[DENR], ins=[DENR])
                            self.tt(OB[:, :, qsl], OT.rearrange("q (g t) -> q g t", g=4),
                                    DENR.rearrange("q (g t) -> q g t", g=4), ALU.mult)

                    prev = None
                    for stp in steps:
                        pts = scores(stp)
                        if prev is not None:
                            pv(*prev)
                        prev = (stp, pts)
                    pv(*prev)
                    if L1S <= 4:
                        return
                    for dc in range(NCH):
                        o = self.ps[self.rot("opbank", 4)]
                        for g in range(4):
                            self.mm(o, WO[:, g, dc * 128:(dc + 1) * 128], OB[:, g, :], start=(g == 0), stop=(g == 3))
                        x = self.XT[:, tile, dc, :]
                        self.stt(x, o, self.GH[:, l, s, w, dc:dc + 1], x, ALU.mult, ALU.add)
                    if p == 1 and self.cfg.get("early_ada", True):
                        self.adanorm(tile, 1, 2)
                        self.prenormed.add(tile)


_CACHE = {}


def _prep_inputs(inp, consts):
    f = lambda a: np.ascontiguousarray(np.asarray(a, dtype=np.float32))
    xs = f(inp["x_sample"])
    xp = f(inp["x_prompt"])
    shared = {
        "mod_w": f(inp["mod_w"]), "ffn_w1": f(inp["ffn_w1"]), "ffn_w3": f(inp["ffn_w3"]),
        "ffn_w2": f(inp["ffn_w2"]), "l0_w_in": f(inp["l0_w_in"]), "l0_w_gf": f(inp["l0_w_gf"]),
        "l0_w_gb": f(inp["l0_w_gb"]), "l0_w_out": f(inp["l0_w_out"]), "l1_w_in": f(inp["l1_w_in"]),
        "l1_w_out": f(inp["l1_w_out"]), "sink": f(inp["l1_sink"]),
        "modb": f(inp["mod_b"]).reshape(144, 128),
        "cb": consts["cb"], "cf": consts["cf"], "rope": consts["rope"], "dftc": consts["dftc"],
        "dft2k": consts["dft2k"], "dft256": consts["dft256"],
    }
    c = f(inp["c"])
    cctx = f(inp["c_ctx"])
    tail = np.concatenate([
        f(inp["norm_g"]).reshape(48, 128), f(inp["final_g"]).reshape(8, 128),
        f(inp["l0_g_head"]).reshape(1, 128), f(inp["l0_b_gf"]).reshape(2, 128),
        f(inp["l0_b_gb"]).reshape(2, 128)], axis=0)
    maps = []
    for b in range(8):
        m = dict(shared)
        m["xin"] = np.ascontiguousarray(np.concatenate([xs[b], xp[4 * b:4 * b + 4].reshape(1024, D)], axis=0))
        m["small1"] = np.ascontiguousarray(np.concatenate([c[b].reshape(8, 128), cctx.reshape(8, 128), tail], 0))
        m["st_f"] = f(inp["state_l0_gla_fwd"])[b]
        m["st_b"] = f(inp["state_l0_gla_bwd"])[b]
        m["ck"] = f(inp["cache_l1_k"])[b].reshape(256, 256)
        m["cv"] = f(inp["cache_l1_v"])[b].reshape(256, 256)
        maps.append(m)
    return maps


def run(inp, cfg=None, trace=False):
    key = repr(sorted((cfg or {}).items()))
    if "consts" not in _CACHE:
        _CACHE["consts"] = _consts()
    consts = _CACHE["consts"]
    B = Builder(cfg)
    nc = B.build()
    maps = _prep_inputs(inp, consts)
    maps = [{k: m[k] for k in B.in_names} for m in maps]
    ncores = (cfg or {}).get("cores", 8)
    c0 = (cfg or {}).get("core0", 0)
    res = run_bass_kernel_spmd(nc, maps[:ncores], core_ids=list(range(c0, c0 + ncores)), trace=trace)
    return res


def kernel(**inp):
    res = run(inp)
    r = res.results
    y = np.stack([r[b]["y"] for b in range(8)], 0)
    y_sample = np.ascontiguousarray(y[:, :2048, :])
    y_prompt = np.ascontiguousarray(y[:, 2048:, :].reshape(32, 256, D))
    sf = np.concatenate([r[b]["o_sf"] for b in range(8)], 0).astype(np.float32)
    sb = np.concatenate([r[b]["o_sb"] for b in range(8)], 0).astype(np.float32)
    kc = np.concatenate([r[b]["o_kc"] for b in range(8)], 0).reshape(32, 256, 4, 64).astype(np.float32)
    vc = np.concatenate([r[b]["o_vc"] for b in range(8)], 0).reshape(32, 256, 4, 64).astype(np.float32)
    return (y_prompt.astype(np.float32), y_sample.astype(np.float32), sf, sb, kc, vc)
```
